# Optimizing a Trainium2 kernel written in Bass

```python
import jax, jax.numpy as jnp
from jax import lax
import numpy as np

D_MODEL = 1024
BATCH = 4
SEQ = 8192
DEPTH = 1
DEC_BATCH = 32
DEC_SEQ = 64
PAST_LEN = 1024

CHUNK = 64
HEAD_DIM = 64
D_RWKV = D_MODEL // 2
D_SB = D_MODEL // 2
H_RWKV = D_RWKV // HEAD_DIM
H_SB = D_SB // HEAD_DIM
D_DECAY_LORA = 64
D_AAA_LORA = 64
D_GATE_LORA = 160
D_RWKV_IN = 3 * D_RWKV + D_DECAY_LORA + D_AAA_LORA + D_GATE_LORA
D_IN = D_RWKV_IN + 3 * D_SB + 2 * D_MODEL
SPLIT_RWKV = [D_RWKV, 2 * D_RWKV, 3 * D_RWKV, 3 * D_RWKV + D_DECAY_LORA,
              3 * D_RWKV + D_DECAY_LORA + D_AAA_LORA]
D_FF = 4 * D_MODEL
Q_BLOCK = 128
SB_SCALE = HEAD_DIM ** -0.5
RMS_EPS = 1e-6
GN_EPS = 64e-5
L2_EPS = 1e-24

kernel_name = 'rwkv7_stickbreak_gated_streaming_encoder'


def rms_norm(x, g):
    xf = x.astype(jnp.float32)
    y = xf * lax.rsqrt(jnp.mean(xf * xf, axis=-1, keepdims=True) + RMS_EPS)
    return (y * g.astype(jnp.float32)).astype(x.dtype)


def stick_breaking(q, k, v, q_pos, k_pos):
    f32 = jnp.float32
    z = jnp.einsum('bhqd,bhkd->bhqk', q.astype(f32), k.astype(f32)) * SB_SCALE
    mask = k_pos[None, :] < q_pos[:, None]
    sp = jnp.where(mask, jax.nn.softplus(z), 0.0)
    tail = lax.cumsum(sp, axis=3, reverse=True)
    att = jnp.exp(jnp.where(mask, z - tail, -jnp.inf))
    return jnp.einsum('bhqk,bhkd->bhqd', att, v.astype(f32))


def sb_prompt(q, k, v):
    b, h, t, dh = q.shape
    nb = t // Q_BLOCK
    q_blocks = jnp.moveaxis(q.reshape(b, h, nb, Q_BLOCK, dh), 2, 0)
    k_pos = jnp.arange(t)

    def block(args):
        q_blk, i = args
        q_pos = i * Q_BLOCK + jnp.arange(Q_BLOCK)
        return stick_breaking(q_blk, k, v, q_pos, k_pos)

    out = lax.map(block, (q_blocks, jnp.arange(nb)))
    return jnp.moveaxis(out, 0, 2).reshape(b, h, t, dh)


def wkv_scan(s0, r, w, k, v, a, b):
    def step(s, inp):
        r_t, w_t, k_t, v_t, a_t, b_t = inp
        sa = jnp.einsum('bhvk,bhk->bhv', s, a_t)
        s = (s * w_t[:, :, None, :] + sa[..., None] * b_t[:, :, None, :]
             + v_t[..., None] * k_t[:, :, None, :])
        return s, jnp.einsum('bhvk,bhk->bhv', s, r_t)

    xs = tuple(jnp.moveaxis(u, 1, 0) for u in (r, w, k, v, a, b))
    s, o = lax.scan(step, s0.astype(jnp.float32), xs)
    return s, jnp.moveaxis(o, 0, 1)


def rwkv7_mix(p, shift_prev, s0, mu, w0, w2, a0, a2, g2, k_k, k_a, r_k, lnx_w, lnx_b):
    bsz, t, _ = p.shape
    f32 = jnp.float32
    p_prev = jnp.concatenate([shift_prev.astype(p.dtype), p[:, :-1]], axis=1)
    pm = p + (p_prev - p) * mu
    r, k, v, wl, al, gl = jnp.split(pm, SPLIT_RWKV, axis=-1)
    r, k, v = r.astype(f32), k.astype(f32), v.astype(f32)
    w = -jax.nn.softplus(-(w0 + jnp.tanh(wl) @ w2).astype(f32)) - 0.5
    decay = jnp.exp(-jnp.exp(w))
    a = jax.nn.sigmoid((a0 + al @ a2).astype(f32))
    g = jax.nn.sigmoid(gl) @ g2
    heads = lambda u: u.reshape(bsz, t, H_RWKV, HEAD_DIM)
    kk = heads(k * k_k)
    kk = kk * lax.rsqrt(jnp.maximum(jnp.sum(kk * kk, -1, keepdims=True), L2_EPS))
    k = k * (1.0 + (a - 1.0) * k_a)
    r_h, k_h, v_h, a_h = heads(r), heads(k), heads(v), heads(a)
    s, o = wkv_scan(s0, r_h, heads(decay), k_h, v_h, -kk, kk * a_h)
    o_mean = jnp.mean(o, -1, keepdims=True)
    o_var = jnp.mean(jnp.square(o - o_mean), -1, keepdims=True)
    o = (o - o_mean) * lax.rsqrt(o_var + GN_EPS)
    bonus = jnp.sum(r_h * k_h * r_k, -1, keepdims=True) * v_h
    o = (o.reshape(bsz, t, D_RWKV) * lnx_w + lnx_b + bonus.reshape(bsz, t, D_RWKV)) * g
    return o.astype(p.dtype), s, p[:, -1:]


def hybrid_layer(x, shift_prev, s0, k_past, v_past, g_norm1, w_in, rwkv_mu, rwkv_w0,
                 rwkv_w2, rwkv_a0, rwkv_a2, rwkv_g2, rwkv_k_k, rwkv_k_a, rwkv_r_k,
                 rwkv_lnx_w, rwkv_lnx_b, sb_q_norm_g, sb_k_norm_g, w_up_a, w_up_b,
                 w_out, g_norm2, w_ff1, w_ff2):
    bsz, t, _ = x.shape
    xn = rms_norm(x, g_norm1)
    proj = xn @ w_in
    p_rwkv = proj[..., :D_RWKV_IN]
    p_sb = proj[..., D_RWKV_IN:D_RWKV_IN + 3 * D_SB]
    p_gate = proj[..., D_RWKV_IN + 3 * D_SB:]
    y_a, s_new, shift_new = rwkv7_mix(p_rwkv, shift_prev, s0, rwkv_mu, rwkv_w0, rwkv_w2,
                                      rwkv_a0, rwkv_a2, rwkv_g2, rwkv_k_k, rwkv_k_a,
                                      rwkv_r_k, rwkv_lnx_w, rwkv_lnx_b)
    heads = lambda u: u.reshape(bsz, t, H_SB, HEAD_DIM).transpose(0, 2, 1, 3)
    q, k, v = (heads(u) for u in jnp.split(p_sb, 3, axis=-1))
    q = rms_norm(q, sb_q_norm_g)
    k = rms_norm(k, sb_k_norm_g)
    if k_past is None:
        y_b = sb_prompt(q, k, v)
    else:
        past = k_past.shape[2]
        k_all = jnp.concatenate([k_past.astype(k.dtype), k], axis=2)
        v_all = jnp.concatenate([v_past.astype(v.dtype), v], axis=2)
        y_b = stick_breaking(q, k_all, v_all, past + jnp.arange(t), jnp.arange(past + t))
    y_b = y_b.transpose(0, 2, 1, 3).reshape(bsz, t, D_SB).astype(x.dtype)
    gate = jax.nn.sigmoid(p_gate.astype(jnp.float32)).astype(x.dtype)
    merged = gate[..., :D_MODEL] * (y_a @ w_up_a) + gate[..., D_MODEL:] * (y_b @ w_up_b)
    x = x + merged @ w_out
    h = jax.nn.relu(rms_norm(x, g_norm2) @ w_ff1)
    x = x + jnp.square(h) @ w_ff2
    return x, shift_new, s_new.astype(x.dtype), k, v


def setup_inputs(seed: int = 0) -> dict:
    key = jax.random.key(seed)
    ks = jax.random.split(key, 32)
    f32 = jnp.float32
    n = lambda i, shape, scale: jax.random.normal(ks[i], shape, f32) * scale
    L = DEPTH
    return {
        'x_prompt': n(0, (BATCH, SEQ, D_MODEL), 1.0),
        'x_sample': n(1, (DEC_BATCH, DEC_SEQ, D_MODEL), 1.0),
        'cache_sb_k': n(2, (L, DEC_BATCH, H_SB, PAST_LEN, HEAD_DIM), 1.0),
        'cache_sb_v': n(3, (L, DEC_BATCH, H_SB, PAST_LEN, HEAD_DIM), 1.0),
        'state_rwkv_wkv': n(4, (L, DEC_BATCH, H_RWKV, HEAD_DIM, HEAD_DIM), 0.3),
        'state_rwkv_shift': n(5, (L, DEC_BATCH, 1, D_RWKV_IN), 1.0),
        'g_norm1': 1.0 + n(6, (L, D_MODEL), 0.02),
        'w_in': n(7, (L, D_MODEL, D_IN), D_MODEL ** -0.5),
        'rwkv_mu': jax.random.uniform(ks[8], (L, D_RWKV_IN), f32),
        'rwkv_w0': jax.random.uniform(ks[9], (L, D_RWKV), f32, -6.0, -1.0),
        'rwkv_w2': n(10, (L, D_DECAY_LORA, D_RWKV), 0.1),
        'rwkv_a0': n(11, (L, D_RWKV), 0.1),
        'rwkv_a2': n(12, (L, D_AAA_LORA, D_RWKV), 0.1),
        'rwkv_g2': n(13, (L, D_GATE_LORA, D_RWKV), D_GATE_LORA ** -0.5),
        'rwkv_k_k': 0.85 + n(14, (L, D_RWKV), 0.02),
        'rwkv_k_a': 1.0 + n(15, (L, D_RWKV), 0.02),
        'rwkv_r_k': n(16, (L, H_RWKV, HEAD_DIM), 0.1),
        'rwkv_lnx_w': 1.0 + n(17, (L, D_RWKV), 0.02),
        'rwkv_lnx_b': n(18, (L, D_RWKV), 0.02),
        'sb_q_norm_g': 1.0 + n(19, (L, HEAD_DIM), 0.02),
        'sb_k_norm_g': 1.0 + n(20, (L, HEAD_DIM), 0.02),
        'w_up_a': n(21, (L, D_RWKV, D_MODEL), D_RWKV ** -0.5),
        'w_up_b': n(22, (L, D_SB, D_MODEL), D_SB ** -0.5),
        'w_out': n(23, (L, D_MODEL, D_MODEL), D_MODEL ** -0.5),
        'g_norm2': 1.0 + n(24, (L, D_MODEL), 0.02),
        'w_ff1': n(25, (L, D_MODEL, D_FF), D_MODEL ** -0.5),
        'w_ff2': n(26, (L, D_FF, D_MODEL), D_FF ** -0.5),
    }


def reference(x_prompt, x_sample, cache_sb_k, cache_sb_v, state_rwkv_wkv, state_rwkv_shift,
              g_norm1, w_in, rwkv_mu, rwkv_w0, rwkv_w2, rwkv_a0, rwkv_a2, rwkv_g2,
              rwkv_k_k, rwkv_k_a, rwkv_r_k, rwkv_lnx_w, rwkv_lnx_b, sb_q_norm_g,
              sb_k_norm_g, w_up_a, w_up_b, w_out, g_norm2, w_ff1, w_ff2):
    x_p, x_s = x_prompt, x_sample
    kp_l, vp_l, sp_l, shp_l = [], [], [], []
    ks_l, vs_l, ss_l, shs_l = [], [], [], []
    for l in range(DEPTH):
        lw = (g_norm1[l], w_in[l], rwkv_mu[l], rwkv_w0[l], rwkv_w2[l], rwkv_a0[l],
              rwkv_a2[l], rwkv_g2[l], rwkv_k_k[l], rwkv_k_a[l], rwkv_r_k[l],
              rwkv_lnx_w[l], rwkv_lnx_b[l], sb_q_norm_g[l], sb_k_norm_g[l],
              w_up_a[l], w_up_b[l], w_out[l], g_norm2[l], w_ff1[l], w_ff2[l])
        s0_p = jnp.zeros((x_p.shape[0], H_RWKV, HEAD_DIM, HEAD_DIM), jnp.float32)
        shift0_p = jnp.zeros((x_p.shape[0], 1, D_RWKV_IN), x_p.dtype)
        x_p, sh_p, s_p, k_p, v_p = hybrid_layer(x_p, shift0_p, s0_p, None, None, *lw)
        x_s, sh_s, s_s, k_s, v_s = hybrid_layer(x_s, state_rwkv_shift[l], state_rwkv_wkv[l],
                                                cache_sb_k[l], cache_sb_v[l], *lw)
        kp_l.append(k_p); vp_l.append(v_p); sp_l.append(s_p); shp_l.append(sh_p)
        ks_l.append(k_s); vs_l.append(v_s); ss_l.append(s_s); shs_l.append(sh_s)
    return (x_p, x_s, jnp.stack(kp_l), jnp.stack(vp_l), jnp.stack(sp_l), jnp.stack(shp_l),
            jnp.stack(ks_l), jnp.stack(vs_l), jnp.stack(ss_l), jnp.stack(shs_l))
```

```python
import contextlib
import numpy as np
import concourse.bass as bass
import concourse.mybir as mybir
from concourse.bass_utils import run_bass_kernel_spmd

F32 = mybir.dt.float32
BF16 = mybir.dt.bfloat16
AF = mybir.ActivationFunctionType
ALU = mybir.AluOpType
AX = mybir.AxisListType

D = 1024
DIN = 5408
NRW = 1824
NPRE = 4096
NMAIN = 4096
NT = 256
NSEQ = 4
PAST = 1024
DFF = 4096
C_SBQ, C_SBK, C_SBV, C_GATE = 1824, 2336, 2848, 3360
NCONST = 128 + 512 * 4 + 256 + 128 + 2 + 128 + 128 + 512 * 4 + 64
import ml_dtypes


def make_consts():
    p = np.arange(128)[:, None]
    cols = []
    cols.append(np.eye(128, dtype=np.float32))
    f = np.arange(512)[None, :]
    cols.append(((p % 64) < (f % 64)).astype(np.float32))
    cols.append(((p % 64) <= (f % 64)).astype(np.float32))
    cols.append(((f % 64) < (p % 64)).astype(np.float32))
    cols.append(((p % 64) == (f % 64)).astype(np.float32))
    t = np.arange(256)[None, :]
    cols.append(np.broadcast_to((t % 64 != 0), (128, 256)).astype(np.float32))
    q = np.arange(128)[None, :]
    cols.append(((p // 64) == (q // 64)).astype(np.float32))
    cols.append(((p // 64) == np.arange(2)[None, :]).astype(np.float32))
    cols.append((p >= q).astype(np.float32))
    cols.append((p < q).astype(np.float32))
    tq = np.arange(512)[None, :]
    for i in range(4):
        cols.append(((i * 128 + p) < tq).astype(np.float32))
    cols.append((p < np.arange(64)[None, :]).astype(np.float32))
    c = np.concatenate(cols, axis=1)
    assert c.shape[1] == NCONST, c.shape
    return np.ascontiguousarray(c)


class Sched:
    def __init__(self, nc, es):
        self.nc = nc
        self.eng = {'pe': nc.tensor, 'act': nc.scalar, 'dve': nc.vector, 'pool': nc.gpsimd, 'sp': nc.sync}
        self.sems = {}
        for e in ['pe', 'act', 'dve', 'pool']:
            self.sems[e] = es.enter_context(nc.semaphore('c_' + e))
        self.nd = {'sp': 8, 'pool': 4, 'act': 4}
        for q, n in self.nd.items():
            for i in range(n):
                self.sems[('d', q, i)] = es.enter_context(nc.semaphore('d_%s%d' % (q, i)))
        self.cnt = {k: 0 for k in self.sems}
        self.dn = {q: 0 for q in self.nd}
        self.waited = {}
        self.lastw = {}
        self.readers = {}
        self.ninst = 0
        self.rec = None

    def _deps(self, e, reads, writes):
        toks = {}

        def add(t):
            if t is None:
                return
            k, v = t
            if toks.get(k, 0) < v:
                toks[k] = v
        for r in reads:
            add(self.lastw.get(r))
            if isinstance(r, str) and r[:2] in ('ps', 'pb'):
                for k, v in self.readers.get(r, {}).items():
                    if k != e:
                        add((k, v))
        for w in writes:
            add(self.lastw.get(w))
            for k, v in self.readers.get(w, {}).items():
                add((k, v))
        for k, v in toks.items():
            if k == e and e == 'pe':
                continue
            if self.waited.get((e, k), 0) >= v:
                continue
            self.eng[e].wait_ge(self.sems[k], v)
            self.waited[(e, k)] = v
            self.ninst += 1

    def _record(self, tok, reads, writes):
        k, v = tok
        for r in reads:
            d = self.readers.setdefault(r, {})
            if d.get(k, 0) < v:
                d[k] = v
        for w in writes:
            self.lastw[w] = tok
            self.readers[w] = {}

    def record(self, f, *a):
        old = self.rec
        self.rec = []
        r = f(*a)
        if r is not None and hasattr(r, '__next__'):
            for _ in r:
                pass
        lst = self.rec
        self.rec = old
        return lst

    def emit(self, lists):
        lists = [l for l in lists if l]
        idx = [0] * len(lists)
        while True:
            best, bf = None, 2.0
            for i, l in enumerate(lists):
                if idx[i] < len(l):
                    fr = idx[i] / len(l)
                    if fr < bf:
                        best, bf = i, fr
            if best is None:
                break
            kind, e, fn, reads, writes, inc = lists[best][idx[best]]
            idx[best] += 1
            if kind == 'op':
                self.op(e, fn, reads, writes, inc)
            else:
                self.dma(e, fn, reads, writes)

    def op(self, e, fn, reads=(), writes=(), inc=True):
        if self.rec is not None:
            self.rec.append(('op', e, fn, tuple(reads), tuple(writes), inc))
            return None
        self._deps(e, reads, writes)
        ins = fn()
        self.ninst += 1
        if inc:
            self.cnt[e] += 1
            ins.then_inc(self.sems[e], 1)
            tok = (e, self.cnt[e])
        else:
            tok = (e, self.cnt[e] + 1)
        self._record(tok, reads, writes)
        return tok

    def dma(self, q, fn, reads=(), writes=()):
        if self.rec is not None:
            self.rec.append(('dma', q, fn, tuple(reads), tuple(writes), True))
            return None
        self._deps(q, reads, writes)
        i = self.dn[q] % self.nd[q]
        self.dn[q] += 1
        k = ('d', q, i)
        ins = fn()
        self.cnt[k] += 16
        ins.then_inc(self.sems[k], 16)
        self.ninst += 1
        tok = (k, self.cnt[k])
        self._record(tok, reads, writes)
        return tok

    def barrier(self):
        for e in ['pe', 'act', 'dve', 'pool', 'sp']:
            for k, v in self.cnt.items():
                if v > 0 and self.waited.get((e, k), 0) < v and not (k == e):
                    self.eng[e].wait_ge(self.sems[k], v)
                    self.waited[(e, k)] = v

    def finish(self):
        for k, v in self.cnt.items():
            if v > 0 and self.waited.get(('sp', k), 0) < v:
                self.nc.sync.wait_ge(self.sems[k], v)


def build():
    nc = bass.Bass("TRN2", target_bir_lowering=False)

    def din(name, shape):
        return nc.dram_tensor(name, list(shape), F32, kind="ExternalInput").ap()

    def dout(name, shape):
        return nc.dram_tensor(name, list(shape), F32, kind="ExternalOutput").ap()

    def dscr(name, shape, dt=BF16):
        return nc.dram_tensor(name, list(shape), dt, kind="Internal").ap()

    xp = din("xp", [NPRE + NMAIN, D])
    flag = din("flag", [1, 1])
    xs = din("xs", [NSEQ * 64, D])
    ck = din("ck", [NSEQ, 8, PAST, 64])
    cv = din("cv", [NSEQ, 8, PAST, 64])
    st_in = din("st", [NSEQ, 8, 64, 64])
    sh_in = din("sh", [NSEQ, NRW])
    consts = nc.dram_tensor("consts", [128, NCONST], BF16, kind="ExternalInput").ap()
    constsf = din("constsf", [128, 256])
    g1 = din("g1", [D]); w_in = din("w_in", [D, DIN]); mu = din("mu", [NRW])
    w0 = din("w0", [512]); w2 = din("w2", [64, 512]); a0 = din("a0", [512]); a2 = din("a2", [64, 512])
    g2 = din("g2", [160, 512]); k_k = din("k_k", [512]); k_a = din("k_a", [512]); r_k = din("r_k", [512])
    lnw = din("lnw", [512]); lnb = din("lnb", [512]); qg = din("qg", [64]); kg = din("kg", [64])
    wua = din("wua", [512, D]); wub = din("wub", [512, D]); wo = din("wo", [D, D]); gn2 = din("gn2", [D])
    wf1 = din("wf1", [D, DFF]); wf2 = din("wf2", [DFF, D])

    yp = dout("yp", [NMAIN, D]); kp = dout("kp", [8, NMAIN, 64]); vp = dout("vp", [8, NMAIN, 64])
    wkvp = dout("wkvp", [8, 64, 64]); shp = dout("shp", [NRW])
    ys = dout("ys", [NSEQ * 64, D]); ksn = dout("ksn", [NSEQ, 8, 64, 64]); vsn = dout("vsn", [NSEQ, 8, 64, 64])
    wkvs = dout("wkvs", [NSEQ, 8, 64, 64]); shs = dout("shs", [NSEQ, NRW])

    w_in_b = dscr("w_in_b", [D, DIN]); wua_b = dscr("wua_b", [512, D]); wub_b = dscr("wub_b", [512, D])
    wo_b = dscr("wo_b", [D, D]); wf1_b = dscr("wf1_b", [D, DFF]); wf2_b = dscr("wf2_b", [DFF, D])
    NK = NPRE + NMAIN
    kT_scr = dscr("kT_scr", [4, 128, NK]); v_scr = dscr("v_scr", [NK // 128, 128, 512])
    qT_scr = dscr("qT_scr", [4, 128, NMAIN])
    yaT_scr = dscr("yaT_scr", [4, 128, NMAIN + 256]); ybT_scr = dscr("ybT_scr", [4, 128, NMAIN + 256])
    kTs_scr = dscr("kTs_scr", [4, 128, 256]); vs_scr = dscr("vs_scr", [2, 128, 512]); qTs_scr = dscr("qTs_scr", [4, 128, 256])

    es = contextlib.ExitStack()
    with es:
        S = Sched(nc, es)
        V, A, P, T = nc.vector, nc.scalar, nc.gpsimd, nc.tensor

        def sb(name, shape, dt=F32):
            return es.enter_context(nc.sbuf_tensor(name, list(shape), dt))

        def pst(name, shape, dt=F32):
            return es.enter_context(nc.psum_tensor(name, list(shape), dt))

        pall = pst("pall", [128, 8, 512])
        ps = [pall[:, i, :] for i in range(8)]
        pb = [ps[6][:, :].bitcast(BF16), ps[7][:, :].bitcast(BF16)]
        pb5 = ps[5][:, :].bitcast(BF16)

        deferred = []

        def cast_rows(dst, src, rows, step, name):
            for r0 in range(0, rows, step):
                deferred.append(lambda r0=r0: S.dma('pool', lambda: P.dma_start(out=dst[r0:r0 + step, :], in_=src[r0:r0 + step, :]), writes=[name]))
        for (c0, c1) in [(0, NRW), (NRW, C_GATE), (C_GATE, DIN)]:
            for r0 in range(0, D, 256):
                S.dma('pool', lambda r0=r0, c0=c0, c1=c1: P.dma_start(out=w_in_b[r0:r0 + 256, c0:c1], in_=w_in[r0:r0 + 256, c0:c1]), writes=[('w_in_b', c0)])
        cast_rows(wua_b, wua, 512, 128, 'wua_b')
        cast_rows(wub_b, wub, 512, 128, 'wub_b')
        cast_rows(wo_b, wo, D, 128, 'wo_b')
        cast_rows(wf1_b, wf1, D, 128, 'wf1_b')
        cast_rows(wf2_b, wf2, DFF, 256, 'wf2_b')

        NC_A = 2818
        cst = sb("cst", [128, NC_A], BF16)
        cstf = sb("cstf", [128, 256])
        S.dma('sp', lambda: nc.sync.dma_start(out=cst[:], in_=consts[:, 0:NC_A]), writes=['cst'])
        S.dma('sp', lambda: nc.sync.dma_start(out=cstf[:], in_=constsf[:, :]), writes=['cst'])
        o = [0]

        def cslice(n):
            a = cst[:, o[0]:o[0] + n]
            o[0] += n
            return a
        ident_b = cslice(128); m_strict = cslice(512); m_incl = cslice(512); m_low = cslice(512); eyeT = cslice(512)
        scanmask = cslice(256); blockones_b = cslice(128); headsel_b = cslice(2)
        tri_b = cslice(128); upp_b = cslice(128)
        ident_f = cstf[:, 0:128]; blockones_f = cstf[:, 128:256]
        cb = sb("cb", [128, 256], BF16)
        triP_b = cb[:, 0:128]; uppP_b = cb[:, 128:256]
        tri_f = tri_b; upp_f = upp_b
        flag_t = sb("flag_t", [128, 1])
        S.dma('sp', lambda: nc.sync.dma_start(out=flag_t[:], in_=flag.partition_broadcast(128)[:, 0, :]), writes=['flag_t'])
        S.op('dve', lambda: V.tensor_scalar(out=triP_b, in0=tri_f, scalar1=flag_t[:, 0:1], scalar2=None, op0=ALU.mult),
             reads=['cst', 'flag_t'], writes=['cb'])
        S.op('dve', lambda: V.tensor_scalar(out=uppP_b, in0=upp_f, scalar1=flag_t[:, 0:1], scalar2=None, op0=ALU.mult),
             reads=['cst', 'flag_t'], writes=['cb'])

        g1_b = sb("g1_b", [128, D])
        S.dma('sp', lambda: nc.sync.dma_start(out=g1_b[:], in_=g1.partition_broadcast(128)), writes=['g1_b'])
        lnw_b = sb("lnw_b", [128, 512]); lnb_b = sb("lnb_b", [128, 512])
        S.dma('sp', lambda: nc.sync.dma_start(out=lnw_b[:], in_=lnw.partition_broadcast(128)), writes=['lnw_b'])
        S.dma('sp', lambda: nc.sync.dma_start(out=lnb_b[:], in_=lnb.partition_broadcast(128)), writes=['lnb_b'])
        qg_b = sb("qg_b", [128, 64]); kg_b = sb("kg_b", [128, 64])
        S.dma('sp', lambda: nc.sync.dma_start(out=qg_b[:], in_=qg.partition_broadcast(128)), writes=['qg_b'])
        S.dma('sp', lambda: nc.sync.dma_start(out=kg_b[:], in_=kg.partition_broadcast(128)), writes=['kg_b'])
        S.op('dve', lambda: V.tensor_scalar(out=qg_b[:], in0=qg_b[:], scalar1=0.125, scalar2=None, op0=ALU.mult),
             reads=['qg_b'], writes=['qg_b'])
        mu_t = sb("mu_t", [128, 15])
        S.op('pool', lambda: P.memset(mu_t[:], 0.0), writes=['mu_t'])
        for cc in range(15):
            n = 128 if cc < 14 else 32
            S.dma('act', lambda cc=cc, n=n: A.dma_start(out=mu_t[0:n, cc:cc + 1],
                  in_=mu[cc * 128:cc * 128 + n].rearrange("(p o) -> p o", o=1)), writes=['mu_t'])
        vec = sb("vec", [128, 8, 4])
        for i, src in enumerate([w0, a0, k_k, k_a, r_k]):
            for j in range(4):
                S.dma('act', lambda i=i, j=j, src=src: A.dma_start(out=vec[:, i, j:j + 1],
                      in_=src[j * 128:(j + 1) * 128].rearrange("(p o) -> p o", o=1)), writes=['vec'])
        S.op('dve', lambda: V.tensor_scalar(out=vec[:, 5, :], in0=vec[:, 0, :], scalar1=-1.0, scalar2=None, op0=ALU.mult),
             reads=['vec'], writes=['vec'])
        S.op('dve', lambda: V.tensor_scalar(out=vec[:, 6, :], in0=vec[:, 3, :], scalar1=-1.0, scalar2=1.0, op0=ALU.mult, op1=ALU.add),
             reads=['vec'], writes=['vec'])
        S.op('dve', lambda: V.tensor_scalar(out=vec[:, 7, :], in0=vec[:, 1, :], scalar1=-1.0, scalar2=None, op0=ALU.mult),
             reads=['vec'], writes=['vec'])
        xt = [sb("xt%d" % i, [128, D]) for i in range(2)]
        wtmp = xt[0][:, :].rearrange("p (a c) -> p a c", a=2)
        w2a2 = sb("w2a2", [128, 512], BF16); g2_t = sb("g2_t", [128, 2, 512], BF16)
        S.dma('sp', lambda: nc.sync.dma_start(out=wtmp[0:64, 0, :], in_=w2[:, :]), writes=[('xt', 0)])
        S.dma('sp', lambda: nc.sync.dma_start(out=wtmp[64:128, 0, :], in_=a2[:, :]), writes=[('xt', 0)])
        S.op('dve', lambda: V.tensor_copy(out=w2a2[:], in_=wtmp[:, 0, :]), reads=[('xt', 0)], writes=['w2a2'])
        S.dma('sp', lambda: nc.sync.dma_start(out=wtmp[:, 0, :], in_=g2[0:128, :]), writes=[('xt', 0)])
        S.dma('sp', lambda: nc.sync.dma_start(out=wtmp[0:32, 1, :], in_=g2[128:160, :]), writes=[('xt', 0)])
        S.op('pool', lambda: P.memset(g2_t[:], 0.0), writes=['g2_t'])
        S.op('dve', lambda: V.tensor_copy(out=g2_t[:, 0, :], in_=wtmp[:, 0, :]), reads=[('xt', 0)], writes=['g2_t'])
        S.op('dve', lambda: V.tensor_copy(out=g2_t[0:32, 1, :], in_=wtmp[0:32, 1, :]), reads=[('xt', 0)], writes=['g2_t'])

        def rstd_from_ss(out_ap, ss_ap, scale, eps, eng_res_r, eng_res_w):
            S.op('act', lambda: A.activation(out=out_ap, in_=ss_ap, func=AF.Ln, bias=eps_t[:, 0:1] if eps == 'rms' else eps_g[:, 0:1], scale=scale),
                 reads=eng_res_r, writes=eng_res_w)
            S.op('act', lambda: A.activation(out=out_ap, in_=out_ap, func=AF.Exp, scale=-0.5), reads=eng_res_w, writes=eng_res_w)

        eps_t = sb("eps_t", [128, 1]); eps_g = sb("eps_g", [128, 1]); one_t = sb("one_t", [128, 1]); mhalf_t = sb("mhalf_t", [128, 1])
        S.op('pool', lambda: P.memset(eps_t[:], 1e-6), writes=['eps_t'])
        S.op('pool', lambda: P.memset(eps_g[:], 64e-5), writes=['eps_g'])
        S.op('pool', lambda: P.memset(one_t[:], 1.0), writes=['one_t'])
        S.op('pool', lambda: P.memset(mhalf_t[:], -0.5), writes=['mhalf_t'])

        xsq = sb("xsq", [128, D], BF16)
        xn_b = sb("xn_b", [128, D], BF16)
        ssq = sb("ssq", [128, 4])
        xcount = [0]

        def load_norm_transpose(x_rows, xnT_dst, g_b, keep_x=False):
            bi = xcount[0] % 2
            xcount[0] += 1
            xb = xt[bi]
            S.dma('sp', lambda: nc.sync.dma_start(out=xb[:], in_=x_rows), writes=[('xt', bi)])
            S.op('act', lambda: A.activation(out=xsq[:], in_=xb[:], func=AF.Square, accum_out=ssq[:, 0:1]),
                 reads=[('xt', bi)], writes=['xsq', 'ssq'])
            S.op('act', lambda: A.activation(out=ssq[:, 1:2], in_=ssq[:, 0:1], func=AF.Ln, bias=eps_t[:, 0:1], scale=1.0 / D),
                 reads=['ssq', 'eps_t'], writes=['ssq'])
            S.op('act', lambda: A.activation(out=ssq[:, 2:3], in_=ssq[:, 1:2], func=AF.Exp, scale=-0.5), reads=['ssq'], writes=['ssq'])
            S.op('dve', lambda: V.scalar_tensor_tensor(out=xn_b[:], in0=xb[:], scalar=ssq[:, 2:3], in1=g_b[:], op0=ALU.mult, op1=ALU.mult),
                 reads=[('xt', bi), 'ssq', 'g1_b', 'gn2_b'], writes=['xn_b'])
            for kc in range(8):
                S.op('pe', lambda kc=kc: T.transpose(out=pb[0][:, kc * 128:(kc + 1) * 128], in_=xn_b[:, kc * 128:(kc + 1) * 128], identity=ident_b),
                     reads=['xn_b', 'cb'], writes=['ps6'])
            S.op('act', lambda: A.copy(out=xnT_dst, in_=pb[0][:, :].rearrange("p (k t) -> p k t", k=8)),
                 reads=['ps6'], writes=['xnT'])
            return bi

        ph1 = contextlib.ExitStack()
        with ph1:
            def sb1(name, shape, dt=F32):
                return ph1.enter_context(nc.sbuf_tensor(name, list(shape), dt))
            win_sb = sb1("win_sb", [128, 8, C_GATE], BF16)
            for kc in range(8):
                S.dma('sp', lambda kc=kc: nc.sync.dma_start(out=win_sb[:, kc, :], in_=w_in_b[kc * 128:(kc + 1) * 128, 0:C_GATE]),
                      reads=[('w_in_b', 0), ('w_in_b', NRW)], writes=['win_sb'])
            xnT = sb1("xnT", [128, 8, NT], BF16)
            pm = sb1("pm", [128, 15, NT])
            ptmp = sb1("ptmp", [128, NT]); dtmp = sb1("dtmp", [128, NT])
            pprev0 = sb1("pprev0", [128, 15, 4])
            carry = sb1("carry", [128, 15]); plast = sb1("plast", [128, 15, 4])
            S.op('pool', lambda: P.memset(pm[:], 0.0), writes=['pm'])
            S.op('pool', lambda: P.memset(carry[:], 0.0), writes=['carry'])
            S.op('pool', lambda: P.memset(plast[:], 0.0), writes=['plast'])
            sq_t = sb1("sq_t", [128, 512]); t_t = sb1("t_t", [128, 512]); kn_f = sb1("kn_f", [128, 512]); v_f = sb1("v_f", [128, 512])
            kn_bt = sb1("kn_bt", [128, 512], BF16); qn_bt = sb1("qn_bt", [128, 512], BF16); v_bt = sb1("v_bt", [128, 512], BF16)
            ss8 = sb1("ss8", [128, 4, 8])
            kTst = sb1("kTst", [128, 4, 128], BF16); qTst = sb1("qTst", [128, 4, 128], BF16)
            f_e = sb1("f_e", [128, NT]); f_cl = sb1("f_cl", [128, NT]); f_a = sb1("f_a", [128, NT])
            f_kk = sb1("f_kk", [128, NT]); f_k2 = sb1("f_k2", [128, NT]); f_t1 = sb1("f_t1", [128, NT]); f_t2 = sb1("f_t2", [128, NT])
            f_gh = sb1("f_gh", [128, NT]); f_gi = sb1("f_gi", [128, NT])
            FT = [[f_e, f_cl, f_a, f_kk, f_k2, f_t1, f_t2, f_gh, f_gi], [sb1("ft1_%d" % i, [128, NT]) for i in range(9)]]
            tw_b = sb1("tw_b", [128, NT], BF16)
            sgl_b = sb1("sgl_b", [128, 2, NT], BF16)
            aT = sb1("aT", [128, 4, NT], BF16); bT = sb1("bT", [128, 4, NT], BF16); kT_ = sb1("kT_", [128, 4, NT], BF16); rT = sb1("rT", [128, 4, NT], BF16)
            bh_f = sb1("bh_f", [128, 4, NT], BF16); kh_f = sb1("kh_f", [128, 4, NT], BF16); v_fb = sb1("v_fb", [128, 4, NT], BF16)
            rk_b = sb1("rk_b", [128, 4, NT], BF16)
            gC = sb1("gC", [128, 4, 4])
            Atok = sb1("Atok", [128, 2, 512], BF16); Bhat = sb1("Bhat", [128, 2, 512], BF16); Khat = sb1("Khat", [128, 2, 512], BF16); Vtok = sb1("Vtok", [128, 2, 512], BF16)
            CT = []
            for ci_ in range(2):
                X = {}
                for nm in ['AakT_s', 'ArbT_s', 'ArkT_s', 'TT_b', 'Ah_s', 'AhT_s', 'W1_s', 'Uv_b', 'GT_s']:
                    X[nm] = sb1("%s%d" % (nm, ci_), [128, 512], BF16)
                X['Pm'] = [sb1("Pm%d_%d" % (i, ci_), [128, 512], BF16) for i in range(2)]
                X['Qm'] = [sb1("Qm%d_%d" % (i, ci_), [128, 512], BF16) for i in range(2)]
                for nm in ['TT_f', 'Uv_f', 'H_s']:
                    X[nm] = sb1("%s%d" % (nm, ci_), [128, 512])
                CT.append(X)
            U_b = sb1("U_b", [128, 512], BF16)
            S_f = sb1("S_f", [128, 4, 64]); S_b = sb1("S_b", [128, 4, 64], BF16); S_t = sb1("S_t", [128, 4, 64])
            st_ld = sq_t[0:64, :].rearrange("p (h k) -> p h k", h=8); st_o = t_t[0:64, :].rearrange("p (h k) -> p h k", h=8)
            S.op('pool', lambda: P.memset(S_f[:], 0.0), writes=['S_f'])
            S.op('pool', lambda: P.memset(S_b[:], 0.0), writes=['S_b'])
            S.op('pool', lambda: P.memset(sgl_b[:], 0.0), writes=['sgl_b'])
            o_sq = sq_t; o_n = sb1("o_n", [128, 512]); o_t = t_t; OSB = [v_f, kn_f]; OSBN = ['v_f', 'kn_f']
            st8 = sb1("st8", [128, 6, 8]); ya_b = sb1("ya_b", [128, 512], BF16); yaT_st = sb1("yaT_st", [128, 4, 128], BF16)
            shst = sb1("shst", [128, 15]); shst4 = sb1("shst4", [128, 15, 4])

            def sb_part(tt, tok0, main, kout, vout, kT_dst, v_dst, qT_dst):
                xl = xnT[:, :, tt * 128:(tt + 1) * 128]
                banks = {'k': 6, 'v': 7, 'q': 6}
                colb = {'q': C_SBQ, 'k': C_SBK, 'v': C_SBV}

                def proj(nm):
                    bk = banks[nm]
                    for kc in range(8):
                        S.op('pe', lambda kc=kc, bk=bk, nm=nm: T.matmul(ps[bk][:, :], lhsT=xnT[:, kc, tt * 128:(tt + 1) * 128],
                             rhs=win_sb[:, kc, colb[nm]:colb[nm] + 512], start=(kc == 0), stop=(kc == 7)),
                             reads=['xnT', 'win_sb'], writes=['ps%d' % bk])
                proj('v')
                S.op('act', lambda: A.copy(out=v_f[:], in_=ps[7][:, :]), reads=['ps7'], writes=['v_f'])
                S.op('pool', lambda: P.tensor_copy(out=v_bt[:], in_=v_f[:]), reads=['v_f'], writes=['v_bt'])
                for (p0, p1, dst_) in (vout or []):
                    S.dma('sp', lambda p0=p0, p1=p1, dst_=dst_: nc.sync.dma_start(out=dst_, in_=v_f[p0:p1, :].rearrange("p (h d) -> p h d", h=8)), reads=['v_f'], writes=['out_v'])
                S.dma('sp', lambda: nc.sync.dma_start(out=v_dst, in_=v_bt[:]), reads=['v_bt'], writes=['v_scr'])
                for nm in (['k', 'q'] if main else ['k']):
                    bk = banks[nm]
                    proj(nm)
                    gb = kg_b if nm == 'k' else qg_b
                    dstb = kn_bt if nm == 'k' else qn_bt
                    si = 0 if nm == 'k' else 2
                    S.op('act', lambda bk=bk: A.activation(out=sq_t[:], in_=ps[bk][:, :], func=AF.Square), reads=['ps%d' % bk], writes=['sq_t'])
                    S.op('dve', lambda si=si: V.tensor_reduce(out=ss8[:, si, :], in_=sq_t[:].rearrange("p (h d) -> p h d", h=8), axis=AX.X, op=ALU.add),
                         reads=['sq_t'], writes=['ss8'])
                    S.op('act', lambda si=si: A.activation(out=ss8[:, si + 1, :], in_=ss8[:, si, :], func=AF.Ln, bias=eps_t[:, 0:1], scale=1.0 / 64),
                         reads=['ss8', 'eps_t'], writes=['ss8'])
                    S.op('act', lambda si=si: A.activation(out=ss8[:, si + 1, :], in_=ss8[:, si + 1, :], func=AF.Exp, scale=-0.5), reads=['ss8'], writes=['ss8'])
                    S.op('dve', lambda bk=bk, gb=gb: V.tensor_tensor(out=t_t[:].rearrange("p (h d) -> p h d", h=8), in0=ps[bk][:, :].rearrange("p (h d) -> p h d", h=8), in1=gb[:].unsqueeze(1).broadcast_to([128, 8, 64]), op=ALU.mult),
                         reads=['ps%d' % bk, 'kg_b', 'qg_b'], writes=['t_t'])
                    if nm == 'k':
                        S.op('dve', lambda si=si: V.tensor_tensor(out=kn_f[:].rearrange("p (h d) -> p h d", h=8), in0=t_t[:].rearrange("p (h d) -> p h d", h=8),
                             in1=ss8[:, si + 1, :].unsqueeze(2).broadcast_to([128, 8, 64]), op=ALU.mult), reads=['t_t', 'ss8'], writes=['kn_f'])
                        S.op('pool', lambda: P.tensor_copy(out=kn_bt[:], in_=kn_f[:]), reads=['kn_f'], writes=['kn_bt'])
                        for (p0, p1, dst_) in (kout or []):
                            S.dma('sp', lambda p0=p0, p1=p1, dst_=dst_: nc.sync.dma_start(out=dst_, in_=kn_f[p0:p1, :].rearrange("p (h d) -> p h d", h=8)), reads=['kn_f'], writes=['out_k'])
                    else:
                        S.op('dve', lambda si=si: V.tensor_tensor(out=qn_bt[:].rearrange("p (h d) -> p h d", h=8), in0=t_t[:].rearrange("p (h d) -> p h d", h=8),
                             in1=ss8[:, si + 1, :].unsqueeze(2).broadcast_to([128, 8, 64]), op=ALU.mult), reads=['t_t', 'ss8'], writes=['qn_bt'])
                    stg = kTst if nm == 'k' else qTst
                    stn = 'kTst' if nm == 'k' else 'qTst'
                    srcn = 'kn_bt' if nm == 'k' else 'qn_bt'
                    for pr in range(4):
                        S.op('pe', lambda pr=pr, dstb=dstb: T.transpose(out=pb[1][:, pr * 128:(pr + 1) * 128], in_=dstb[:, pr * 128:(pr + 1) * 128], identity=ident_b),
                             reads=[srcn, 'cb'], writes=['ps7'])
                    S.op('act', lambda stg=stg: A.copy(out=stg[:], in_=pb[1][:, 0:512].rearrange("p (k t) -> p k t", k=4)), reads=['ps7'], writes=[stn])
                    dst = kT_dst if nm == 'k' else qT_dst
                    S.dma('sp', lambda stg=stg, dst=dst: nc.sync.dma_start(out=dst, in_=stg[:]), reads=[stn], writes=['kq_scr'])

            def rwkv_proj(main, seq_starts, shift_srcs, allcols=False):
                ccs = list(range(15)) if (main or allcols) else [4, 5, 6, 7, 8, 9, 10, 11, 12]
                for ci in range(4):
                    if seq_starts[ci]:
                        if shift_srcs[ci] is None:
                            S.op('pool', lambda ci=ci: P.memset(pprev0[:, :, ci:ci + 1], 0.0), writes=['pprev0'])
                        else:
                            for cc in range(15):
                                n = 128 if cc < 14 else 32
                                S.dma('sp', lambda cc=cc, n=n, ci=ci: nc.sync.dma_start(out=pprev0[0:n, cc, ci:ci + 1],
                                      in_=shift_srcs[ci][cc * 128:cc * 128 + n].rearrange("(p o) -> p o", o=1)), writes=['pprev0'])
                    elif ci == 0:
                        S.op('pool', lambda: P.tensor_copy(out=pprev0[:, :, 0], in_=carry[:]), reads=['carry'], writes=['pprev0'])
                for cc in ccs:
                    n = 128 if cc < 14 else 32
                    bk = 6 + (cc % 2)
                    for kc in range(8):
                        S.op('pe', lambda kc=kc, cc=cc, n=n, bk=bk: T.matmul(ps[bk][0:n, 0:NT], lhsT=win_sb[:, kc, cc * 128:cc * 128 + n],
                             rhs=xnT[:, kc, :], start=(kc == 0), stop=(kc == 7)), reads=['xnT', 'win_sb'], writes=['ps%d' % bk])
                    S.op('act', lambda n=n, bk=bk: A.copy(out=ptmp[0:n, :], in_=ps[bk][0:n, 0:NT]), reads=['ps%d' % bk], writes=['ptmp'])
                    p3 = ptmp[0:n, :].rearrange("p (c t) -> p c t", c=4)
                    d3 = dtmp[0:n, :].rearrange("p (c t) -> p c t", c=4)
                    for ci in range(1, 4):
                        if not seq_starts[ci]:
                            S.op('dve', lambda ci=ci, cc=cc, n=n: V.tensor_copy(out=pprev0[0:n, cc, ci:ci + 1], in_=ptmp[0:n, ci * 64 - 1:ci * 64]),
                                 reads=['ptmp'], writes=['pprev0'])
                    S.op('dve', lambda n=n, cc=cc: V.tensor_copy(out=carry[0:n, cc:cc + 1], in_=ptmp[0:n, NT - 1:NT]), reads=['ptmp'], writes=['carry'])
                    S.op('dve', lambda n=n, cc=cc, p3=p3: V.tensor_copy(out=plast[0:n, cc, :], in_=p3[:, :, 63]), reads=['ptmp'], writes=['plast'])
                    S.op('dve', lambda p3=p3, d3=d3: V.tensor_tensor(out=d3[:, :, 1:64], in0=p3[:, :, 0:63], in1=p3[:, :, 1:64], op=ALU.subtract),
                         reads=['ptmp'], writes=['dtmp'])
                    S.op('dve', lambda p3=p3, d3=d3, n=n, cc=cc: V.tensor_tensor(out=d3[:, :, 0], in0=pprev0[0:n, cc, :], in1=p3[:, :, 0], op=ALU.subtract),
                         reads=['ptmp', 'pprev0'], writes=['dtmp'])
                    S.op('dve', lambda n=n, cc=cc: V.scalar_tensor_tensor(out=pm[0:n, cc, :], in0=dtmp[0:n, :], scalar=mu_t[0:n, cc:cc + 1], in1=ptmp[0:n, :],
                         op0=ALU.mult, op1=ALU.add), reads=['dtmp', 'ptmp', 'mu_t'], writes=['pm'])

            def prep_pre(main):
                S.op('act', lambda: A.activation(out=f_e[0:64, :], in_=pm[0:64, 12, :], func=AF.Exp, scale=-2.0), reads=['pm'], writes=['f_e#0'])
                S.op('dve', lambda: V.tensor_scalar(out=f_e[0:64, :], in0=f_e[0:64, :], scalar1=1.0, scalar2=None, op0=ALU.add), reads=['f_e#0'], writes=['f_e#0'])
                S.op('dve', lambda: V.reciprocal(out=f_e[0:64, :], in_=f_e[0:64, :]), reads=['f_e#0'], writes=['f_e#0'])
                S.op('dve', lambda: V.tensor_scalar(out=tw_b[0:64, :], in0=f_e[0:64, :], scalar1=2.0, scalar2=-1.0, op0=ALU.mult, op1=ALU.add), reads=['f_e#0'], writes=['tw_b'])
                S.op('pool', lambda: P.tensor_copy(out=tw_b[64:128, :], in_=pm[64:128, 12, :]), reads=['pm'], writes=['tw_b'])
                if main:
                    for (np_, ci_, tmp_, tn_) in ((128, 13, f_t1, 'f_t1#0'), (32, 14, f_t2, 'f_t2#0')):
                        S.op('act', lambda np_=np_, ci_=ci_, tmp_=tmp_: A.activation(out=tmp_[0:np_, :], in_=pm[0:np_, ci_, :], func=AF.Exp, scale=-1.0), reads=['pm'], writes=[tn_])
                        S.op('act', lambda np_=np_, tmp_=tmp_: A.activation(out=tmp_[0:np_, :], in_=tmp_[0:np_, :], func=AF.Ln, bias=one_t[0:np_, 0:1]), reads=[tn_, 'one_t'], writes=[tn_])
                        S.op('act', lambda np_=np_, ci_=ci_, tmp_=tmp_: A.activation(out=sgl_b[0:np_, ci_ - 13, :], in_=tmp_[0:np_, :], func=AF.Exp, scale=-1.0), reads=[tn_], writes=['sgl_b'])

            def prep_pair(main, j, ti, BK):
                f_e, f_cl, f_a, f_kk, f_k2, f_t1, f_t2, f_gh, f_gi = FT[ti]
                f_lw = f_e
                tx = '#%d' % ti
                rj, kj, vj = pm[:, j, :], pm[:, 4 + j, :], pm[:, 8 + j, :]
                cs = slice(j * 128, (j + 1) * 128)
                S.op('pe', lambda cs=cs: T.matmul(ps[BK[0]][:, 0:NT], lhsT=w2a2[0:64, cs], rhs=tw_b[0:64, :], start=True, stop=True),
                     reads=['w2a2', 'tw_b'], writes=['ps%d' % BK[0]])
                S.op('pe', lambda cs=cs: T.matmul(ps[BK[1]][:, 0:NT], lhsT=w2a2[64:128, cs], rhs=tw_b[64:128, :], start=True, stop=True),
                     reads=['w2a2', 'tw_b'], writes=['ps%d' % BK[1]])
                S.op('act', lambda j=j: A.activation(out=f_e[:], in_=ps[BK[0]][:, 0:NT], func=AF.Exp, bias=vec[:, 5, j:j + 1], scale=-1.0),
                     reads=['ps%d' % BK[0], 'vec'], writes=['f_e' + tx])
                S.op('act', lambda: A.activation(out=f_e[:], in_=f_e[:], func=AF.Ln, bias=one_t[:, 0:1]), reads=['f_e' + tx, 'one_t'], writes=['f_e' + tx])
                S.op('act', lambda: A.activation(out=f_e[:], in_=f_e[:], func=AF.Exp, bias=mhalf_t[:, 0:1], scale=-1.0), reads=['f_e' + tx, 'mhalf_t'], writes=['f_e' + tx])
                S.op('dve', lambda: V.tensor_scalar(out=f_lw[:], in0=f_e[:], scalar1=-1.0, scalar2=None, op0=ALU.mult), reads=['f_e' + tx], writes=['f_e' + tx])
                S.op('dve', lambda: V.tensor_tensor_scan(out=f_cl[:], data0=scanmask, data1=f_lw[:], initial=0.0, op0=ALU.mult, op1=ALU.add),
                     reads=['f_e' + tx, 'cst'], writes=['f_cl' + tx])
                S.op('act', lambda j=j: A.activation(out=f_a[:], in_=ps[BK[1]][:, 0:NT], func=AF.Exp, bias=vec[:, 7, j:j + 1], scale=-1.0),
                     reads=['ps%d' % BK[1], 'vec'], writes=['f_a' + tx])
                S.op('act', lambda: A.activation(out=f_a[:], in_=f_a[:], func=AF.Ln, bias=one_t[:, 0:1]), reads=['f_a' + tx, 'one_t'], writes=['f_a' + tx])
                S.op('act', lambda: A.activation(out=f_a[:], in_=f_a[:], func=AF.Exp, scale=-1.0), reads=['f_a' + tx], writes=['f_a' + tx])
                S.op('dve', lambda j=j, kj=kj: V.tensor_scalar(out=f_kk[:], in0=kj, scalar1=vec[:, 2, j:j + 1], scalar2=None, op0=ALU.mult),
                     reads=['pm', 'vec'], writes=['f_kk' + tx])
                S.op('pool', lambda: P.tensor_tensor(out=f_t1[:], in0=f_kk[:], in1=f_kk[:], op=ALU.mult), reads=['f_kk' + tx], writes=['f_t1' + tx])
                S.op('pe', lambda: T.matmul(ps[BK[2]][:, 0:NT], lhsT=blockones_f, rhs=f_t1[:], start=True, stop=True), reads=['cst', 'f_t1' + tx], writes=['ps%d' % BK[2]])
                S.op('dve', lambda: V.tensor_scalar(out=f_t2[:], in0=ps[BK[2]][:, 0:NT], scalar1=1e-18, scalar2=None, op0=ALU.max), reads=['ps%d' % BK[2]], writes=['f_t2' + tx])
                S.op('act', lambda: A.activation(out=f_t2[:], in_=f_t2[:], func=AF.Ln), reads=['f_t2' + tx], writes=['f_t2' + tx])
                S.op('act', lambda: A.activation(out=f_t2[:], in_=f_t2[:], func=AF.Exp, scale=-0.5), reads=['f_t2' + tx], writes=['f_t2' + tx])
                S.op('dve', lambda: V.tensor_tensor(out=f_kk[:], in0=f_kk[:], in1=f_t2[:], op=ALU.mult), reads=['f_kk' + tx, 'f_t2' + tx], writes=['f_kk' + tx])
                S.op('dve', lambda j=j: V.tensor_scalar(out=f_t1[:], in0=f_a[:], scalar1=vec[:, 3, j:j + 1], scalar2=vec[:, 6, j:j + 1], op0=ALU.mult, op1=ALU.add),
                     reads=['f_a' + tx, 'vec'], writes=['f_t1' + tx])
                S.op('dve', lambda kj=kj: V.tensor_tensor(out=f_k2[:], in0=kj, in1=f_t1[:], op=ALU.mult), reads=['pm', 'f_t1' + tx], writes=['f_k2' + tx])
                cl3 = f_cl[:].rearrange("p (c t) -> p c t", c=4)
                S.op('act', lambda j=j, cl3=cl3: A.activation(out=gC[:, j, :], in_=cl3[:, :, 63], func=AF.Exp), reads=['f_cl' + tx], writes=['gC'])
                S.op('dve', lambda cl3=cl3: V.tensor_tensor(out=f_gh[:].rearrange("p (c t) -> p c t", c=4), in0=cl3[:, :, 63:64].broadcast_to([128, 4, 64]), in1=cl3,
                     op=ALU.subtract), reads=['f_cl' + tx], writes=['f_gh' + tx])
                S.op('act', lambda: A.activation(out=f_gh[:], in_=f_gh[:], func=AF.Exp), reads=['f_gh' + tx], writes=['f_gh' + tx])
                S.op('act', lambda: A.activation(out=f_gi[:], in_=f_cl[:], func=AF.Exp, scale=-1.0), reads=['f_cl' + tx], writes=['f_gi' + tx])
                S.op('pool', lambda: P.tensor_tensor(out=f_t1[:], in0=f_kk[:], in1=f_a[:], op=ALU.mult), reads=['f_kk' + tx, 'f_a' + tx], writes=['f_t1' + tx])
                S.op('dve', lambda j=j: V.tensor_tensor(out=bT[:, j, :], in0=f_t1[:], in1=f_gi[:], op=ALU.mult), reads=['f_t1' + tx, 'f_gi' + tx], writes=['bT'])
                S.op('pool', lambda j=j: P.tensor_tensor(out=bh_f[:, j, :], in0=f_t1[:], in1=f_gh[:], op=ALU.mult), reads=['f_t1' + tx, 'f_gh' + tx], writes=['bh_f'])
                S.op('dve', lambda j=j: V.tensor_tensor(out=kT_[:, j, :], in0=f_k2[:], in1=f_gi[:], op=ALU.mult), reads=['f_k2' + tx, 'f_gi' + tx], writes=['kT_'])
                S.op('pool', lambda j=j: P.tensor_tensor(out=kh_f[:, j, :], in0=f_k2[:], in1=f_gh[:], op=ALU.mult), reads=['f_k2' + tx, 'f_gh' + tx], writes=['kh_f'])
                S.op('pool', lambda j=j, vj=vj: P.tensor_copy(out=v_fb[:, j, :], in_=vj), reads=['pm'], writes=['v_fb'])
                S.op('dve', lambda: V.tensor_tensor(out=f_t2[:], in0=f_cl[:], in1=f_lw[:], op=ALU.subtract), reads=['f_cl' + tx, 'f_e' + tx], writes=['f_t2' + tx])
                S.op('act', lambda: A.activation(out=f_t2[:], in_=f_t2[:], func=AF.Exp), reads=['f_t2' + tx], writes=['f_t2' + tx])
                S.op('dve', lambda j=j: V.scalar_tensor_tensor(out=aT[:, j, :], in0=f_kk[:], scalar=-1.0, in1=f_t2[:], op0=ALU.mult, op1=ALU.mult),
                     reads=['f_kk' + tx, 'f_t2' + tx], writes=['aT'])
                if main:
                    S.op('act', lambda: A.activation(out=f_gi[:], in_=f_cl[:], func=AF.Exp), reads=['f_cl' + tx], writes=['f_gi' + tx])
                    S.op('dve', lambda j=j, rj=rj: V.tensor_tensor(out=rT[:, j, :], in0=rj, in1=f_gi[:], op=ALU.mult), reads=['pm', 'f_gi' + tx], writes=['rT'])
                    S.op('dve', lambda j=j, rj=rj: V.scalar_tensor_tensor(out=rk_b[:, j, :], in0=rj, scalar=vec[:, 4, j:j + 1], in1=f_k2[:], op0=ALU.mult, op1=ALU.mult),
                         reads=['pm', 'vec', 'f_k2' + tx], writes=['rk_b'])

            def prep_T():
                for tt in range(2):
                    for (src, srcn, dst, dstn) in [(aT, 'aT', Atok, 'Atok'), (bh_f, 'bh_f', Bhat, 'Bhat'), (kh_f, 'kh_f', Khat, 'Khat'), (v_fb, 'v_fb', Vtok, 'Vtok')]:
                        for j in range(4):
                            S.op('pe', lambda j=j, src=src, tt=tt: T.transpose(out=pb5[:, j * 128:(j + 1) * 128], in_=src[:, j, tt * 128:(tt + 1) * 128], identity=ident_b),
                                 reads=[srcn, 'cb'], writes=['ps5'])
                        S.op('act', lambda dst=dst, tt=tt: A.copy(out=dst[:, tt, :], in_=pb5[:, 0:512]), reads=['ps5'], writes=[dstn])


            def prep_pairs(main, js, ti, BK):
                for j in js:
                    prep_pair(main, j, ti, BK)

            def blk_tok(t_, c, h):
                return t_[c * 64:(c + 1) * 64, h * 64:(h + 1) * 64]

            def blk_feat(t_, c, h):
                hb = (h % 2) * 64
                col = ((h // 2) * 2 + c) * 64
                return t_[hb:hb + 64, col:col + 64]

            def fm(t_, h, tt, c):
                hb = (h % 2) * 64
                t0 = tt * 128 + c * 64
                return t_[hb:hb + 64, h // 2, t0:t0 + 64]

            def tm(t_, tt, c, h):
                return t_[c * 64:(c + 1) * 64, tt, h * 64:(h + 1) * 64]

            def sv(ap, kind, i):
                if kind in ('tc', 'fh'):
                    return ap[i * 64:(i + 1) * 64, :]
                return ap.rearrange("p (a two t) -> p a two t", two=2, t=64)[:, :, i, :]

            def step16(banks, fnl, reads):
                for c in range(2):
                    for h in range(8):
                        lst = fnl(c, h)
                        n = len(lst)
                        for idx, (of_, l_, r_, rb) in enumerate(lst):
                            bk = banks[rb // 64]
                            S.op('pe', lambda of_=of_, l_=l_, r_=r_, idx=idx, n=n, bk=bk: T.matmul(of_(ps[bk]), lhsT=l_, rhs=r_, start=(idx == 0), stop=(idx == n - 1)),
                                 reads=reads, writes=['ps%d' % bk], inc=(h >= 6 and idx == n - 1))

            def ev(eng, kind, banks, fn, reads, writes):
                for i in range(2):
                    bk = banks[i]
                    S.op(eng, lambda i=i, bk=bk: fn(lambda ap: sv(ap, kind, i), ps[bk]), reads=reads + ['ps%d' % bk], writes=writes)

            def hb_(h):
                return (h % 2) * 64

            def chunk_math(tt, main):
                X = CT[tt]
                Pm_, Qm_, TT_f, TT_b, AakT_s, ArbT_s, ArkT_s = X['Pm'], X['Qm'], X['TT_f'], X['TT_b'], X['AakT_s'], X['ArbT_s'], X['ArkT_s']
                Ah_s, AhT_s, W1_s, Uv_f, Uv_b, GT_s, H_s = X['Ah_s'], X['AhT_s'], X['W1_s'], X['Uv_f'], X['Uv_b'], X['GT_s'], X['H_s']
                P0, P1 = ((0, 1), (2, 3)) if tt == 0 else ((4, 5), (6, 7))
                P2 = P0
                sfx = '_%d' % tt
                yield
                step16(P0, lambda c, h: [(lambda b, c=c, h=h: blk_tok(b, c, h), fm(bT, h, tt, c), fm(aT, h, tt, c), hb_(h))], ['bT', 'aT'])
                yield
                ev('dve', 'th', P0, lambda v, b: V.tensor_tensor(out=v(Pm_[0][:]), in0=v(b[:, :]), in1=v(m_strict), op=ALU.mult), ['cst'], ['Pm0' + sfx])
                yield
                ev('dve', 'th', P0, lambda v, b: V.tensor_tensor(out=v(TT_f[:]), in0=v(b[:, :]), in1=v(m_strict), op=ALU.mult), ['cst'], ['TT_f' + sfx])
                yield
                step16(P1, lambda c, h: [(lambda b, c=c, h=h: blk_tok(b, c, h), fm(kT_, h, tt, c), fm(aT, h, tt, c), hb_(h))], ['kT_', 'aT'])
                yield
                ev('dve', 'th', P1, lambda v, b: V.tensor_tensor(out=v(AakT_s[:]), in0=v(b[:, :]), in1=v(m_strict), op=ALU.mult), ['cst'], ['AakT_s' + sfx])
                yield
                step16(P2, lambda c, h: [(lambda b, c=c, h=h: blk_tok(b, c, h), fm(aT, h, tt, c), fm(bT, h, tt, c), hb_(h))], ['bT', 'aT'])
                yield
                ev('dve', 'th', P2, lambda v, b: V.tensor_tensor(out=v(Qm_[0][:]), in0=v(b[:, :]), in1=v(m_low), op=ALU.mult), ['cst'], ['Qm0' + sfx])
                yield
                S.op('pool', lambda: P.tensor_tensor(out=TT_f[:], in0=TT_f[:], in1=eyeT, op=ALU.add), reads=['TT_f' + sfx, 'cst'], writes=['TT_f' + sfx])
                yield
                S.op('pool', lambda: P.tensor_copy(out=TT_b[:], in_=TT_f[:]), reads=['TT_f' + sfx], writes=['TT_b' + sfx])
                if main:
                    yield
                    step16(P0, lambda c, h: [(lambda b, c=c, h=h: blk_tok(b, c, h), fm(bT, h, tt, c), fm(rT, h, tt, c), hb_(h))], ['bT', 'rT'])
                    yield
                    ev('dve', 'th', P0, lambda v, b: V.tensor_tensor(out=v(ArbT_s[:]), in0=v(b[:, :]), in1=v(m_incl), op=ALU.mult), ['cst'], ['ArbT_s' + sfx])
                    yield
                    step16(P1, lambda c, h: [(lambda b, c=c, h=h: blk_tok(b, c, h), fm(kT_, h, tt, c), fm(rT, h, tt, c), hb_(h))], ['kT_', 'rT'])
                    yield
                    ev('dve', 'th', P1, lambda v, b: V.tensor_tensor(out=v(ArkT_s[:]), in0=v(b[:, :]), in1=v(m_incl), op=ALU.mult), ['cst'], ['ArkT_s' + sfx])
                cur = 0
                for j in range(1, 6):
                    nxt = 1 - cur
                    Pc, Qc, Pn, Qn = Pm_[cur], Qm_[cur], Pm_[nxt], Qm_[nxt]
                    yield
                    step16(P0, lambda c, h: [(lambda b, c=c, h=h: blk_tok(b, c, h), blk_tok(Pc, c, h), blk_tok(Qc, c, h), c * 64)], ['Pm%d' % cur + sfx, 'Qm%d' % cur + sfx])
                    yield
                    ev('act', 'tc', P0, lambda v, b, Qn=Qn: A.copy(out=v(Qn[:]), in_=v(b[:, :])), [], ['Qm%d' % nxt + sfx])
                    if j < 5:
                        yield
                        step16(P1, lambda c, h: [(lambda b, c=c, h=h: blk_tok(b, c, h), blk_tok(Qc, c, h), blk_tok(Pc, c, h), c * 64)], ['Pm%d' % cur + sfx, 'Qm%d' % cur + sfx])
                        yield
                        ev('dve', 'tc', P1, lambda v, b, Pn=Pn: V.tensor_copy(out=v(Pn[:]), in_=v(b[:, :])), [], ['Pm%d' % nxt + sfx])
                    yield
                    step16(P2, lambda c, h: [(lambda b, c=c, h=h: blk_tok(b, c, h), blk_tok(Qn, c, h), blk_tok(TT_b, c, h), c * 64)], ['Qm%d' % nxt + sfx, 'TT_b' + sfx])
                    yield
                    ev('dve', 'tc', P2, lambda v, b: V.tensor_tensor(out=v(TT_f[:]), in0=v(b[:, :]), in1=v(TT_f[:]), op=ALU.add), ['TT_f' + sfx], ['TT_f' + sfx])
                    yield
                    S.op('pool', lambda: P.tensor_copy(out=TT_b[:], in_=TT_f[:]), reads=['TT_f' + sfx], writes=['TT_b' + sfx])
                    cur = nxt
                yield
                step16(P0, lambda c, h: [(lambda b, c=c, h=h: blk_tok(b, c, h), blk_tok(TT_b, c, h), tm(Atok, tt, c, h), c * 64)], ['TT_b' + sfx, 'Atok'])
                yield
                ev('act', 'tc', P0, lambda v, b: A.copy(out=v(Ah_s[:]), in_=v(b[:, :])), [], ['Ah_s' + sfx])
                yield
                step16(P1, lambda c, h: [(lambda b, c=c, h=h: blk_tok(b, c, h), blk_tok(AakT_s, c, h), tm(Vtok, tt, c, h), c * 64)], ['AakT_s' + sfx, 'Vtok'])
                yield
                ev('dve', 'tc', P1, lambda v, b: V.tensor_copy(out=v(W1_s[:]), in_=v(b[:, :])), [], ['W1_s' + sfx])
                if main:
                    yield
                    step16(P2, lambda c, h: [(lambda b, c=c, h=h: blk_feat(b, c, h), tm(Atok, tt, c, h), blk_tok(TT_b, c, h), c * 64)], ['TT_b' + sfx, 'Atok'])
                    yield
                    ev('act', 'fc', P2, lambda v, b: A.copy(out=v(AhT_s[:]), in_=v(b[:, :])), [], ['AhT_s' + sfx])
                yield
                step16(P0, lambda c, h: [(lambda b, c=c, h=h: blk_tok(b, c, h), blk_tok(TT_b, c, h), blk_tok(W1_s, c, h), c * 64)], ['TT_b' + sfx, 'W1_s' + sfx])
                yield
                ev('act', 'tc', P0, lambda v, b: A.copy(out=v(Uv_f[:]), in_=v(b[:, :])), [], ['Uv_f' + sfx])
                yield
                ev('dve', 'tc', P0, lambda v, b: V.tensor_copy(out=v(Uv_b[:]), in_=v(b[:, :])), [], ['Uv_b' + sfx])
                yield
                step16(P1, lambda c, h: [(lambda b, c=c, h=h: blk_feat(b, c, h), blk_tok(Ah_s, c, h), tm(Bhat, tt, c, h), c * 64)], ['Ah_s' + sfx, 'Bhat'])
                yield
                ev('act', 'fc', P1, lambda v, b: A.copy(out=v(GT_s[:]), in_=v(b[:, :])), [], ['GT_s' + sfx])
                yield
                step16(P2, lambda c, h: [(lambda b, c=c, h=h: blk_feat(b, c, h), tm(Bhat, tt, c, h), blk_tok(Uv_b, c, h), c * 64),
                                             (lambda b, c=c, h=h: blk_feat(b, c, h), tm(Khat, tt, c, h), tm(Vtok, tt, c, h), c * 64)], ['Bhat', 'Uv_b' + sfx, 'Khat', 'Vtok'])
                yield
                ev('dve', 'fc', P2, lambda v, b: V.tensor_copy(out=v(H_s[:]), in_=v(b[:, :])), [], ['H_s' + sfx])

            def load_state(seq):
                S.dma('sp', lambda: nc.sync.dma_start(out=st_ld, in_=st_in[seq].rearrange("h v k -> v h k")), writes=['sq_t'])
                for pr in range(4):
                    S.op('pe', lambda pr=pr: T.transpose(out=ps[1][:, pr * 64:(pr + 1) * 64], in_=st_ld[:, 2 * pr:2 * pr + 2, :].rearrange("v h k -> v (h k)"),
                         identity=ident_f[0:64, 0:64]), reads=['sq_t', 'cst'], writes=['ps1'])
                S.op('dve', lambda: V.tensor_copy(out=S_f[:].rearrange("p a v -> p (a v)"), in_=ps[1][:, 0:256]), reads=['ps1'], writes=['S_f'])
                S.op('act', lambda: A.copy(out=S_b[:].rearrange("p a v -> p (a v)"), in_=ps[1][:, 0:256]), reads=['ps1'], writes=['S_b'])

            def store_state(dst):
                for pr in range(4):
                    S.op('pe', lambda pr=pr: T.transpose(out=ps[1][0:64, pr * 128:(pr + 1) * 128], in_=S_f[:, pr, :], identity=ident_f),
                         reads=['S_f', 'cst'], writes=['ps1'])
                S.op('dve', lambda: V.tensor_copy(out=t_t[0:64, :], in_=ps[1][0:64, :]), reads=['ps1'], writes=['t_t'])
                S.dma('sp', lambda: nc.sync.dma_start(out=dst.rearrange("h v k -> v h k"), in_=st_o), reads=['t_t'], writes=['out_st'])

            def seq_pass(tt, main, seq_starts, seq_ids, seq_ends, state_dsts):
                X = CT[tt]
                ArbT_s, ArkT_s, AhT_s, Uv_f, GT_s, H_s = X['ArbT_s'], X['ArkT_s'], X['AhT_s'], X['Uv_f'], X['GT_s'], X['H_s']
                O_sb = OSB[tt]
                sfx = '_%d' % tt
                for c in range(2):
                    ci = tt * 2 + c
                    rows = slice(c * 64, (c + 1) * 64)
                    if seq_starts[ci]:
                        if seq_ids[ci] is None:
                            S.op('pool', lambda: P.memset(S_f[:], 0.0), writes=['S_f'])
                            S.op('pool', lambda: P.memset(S_b[:], 0.0), writes=['S_b'])
                        else:
                            load_state(seq_ids[ci])
                    if main:
                        for h in range(8):
                            hb = (h % 2) * 64
                            bk = 0 + (h % 2)
                            S.op('pe', lambda h=h, hb=hb, c=c, bk=bk: T.matmul(blk_tok(ps[bk], c, h), lhsT=blk_feat(AhT_s, c, h), rhs=S_b[hb:hb + 64, h // 2, :], start=True, stop=True),
                                 reads=['AhT_s' + sfx, 'S_b'], writes=['ps%d' % bk])
                        for par in range(2):
                            S.op('dve', lambda par=par, rows=rows: V.tensor_tensor(out=sv(U_b[rows, :], 'th', par), in0=sv(ps[par][rows, :], 'th', par), in1=sv(Uv_f[rows, :], 'th', par), op=ALU.add),
                                 reads=['ps%d' % par, 'Uv_f' + sfx], writes=['U_b'])
                        for h in range(8):
                            hb = (h % 2) * 64
                            bk = 2 + (h % 2)
                            S.op('pe', lambda h=h, hb=hb, c=c, bk=bk: T.matmul(blk_tok(ps[bk], c, h), lhsT=fm(rT, h, tt, c), rhs=S_b[hb:hb + 64, h // 2, :], start=True, stop=True),
                                 reads=['rT', 'S_b'], writes=['ps%d' % bk])
                        for h in range(8):
                            bk = 4 + c
                            S.op('pe', lambda h=h, c=c, bk=bk: T.matmul(blk_tok(ps[bk], c, h), lhsT=blk_tok(ArbT_s, c, h), rhs=blk_tok(U_b, c, h), start=True, stop=False),
                                 reads=['ArbT_s' + sfx, 'U_b'], writes=['ps%d' % bk])
                            S.op('pe', lambda h=h, c=c, bk=bk: T.matmul(blk_tok(ps[bk], c, h), lhsT=blk_tok(ArkT_s, c, h), rhs=tm(Vtok, tt, c, h), start=False, stop=True),
                                 reads=['ArkT_s' + sfx, 'Vtok'], writes=['ps%d' % bk])
                        S.op('act', lambda c=c, rows=rows: A.copy(out=O_sb[rows, :], in_=ps[4 + c][rows, :]), reads=['ps%d' % (4 + c)], writes=[OSBN[tt]])
                        for par in range(2):
                            S.op('dve', lambda par=par, rows=rows: V.tensor_tensor(out=sv(O_sb[rows, :], 'th', par), in0=sv(ps[2 + par][rows, :], 'th', par), in1=sv(O_sb[rows, :], 'th', par), op=ALU.add),
                                 reads=['ps%d' % (2 + par), OSBN[tt]], writes=[OSBN[tt]])
                    for h in range(8):
                        hb = (h % 2) * 64
                        bk = h % 2
                        S.op('pe', lambda h=h, hb=hb, c=c, bk=bk: T.matmul(ps[bk][hb:hb + 64, (h // 2) * 64:(h // 2) * 64 + 64], lhsT=blk_feat(GT_s, c, h), rhs=S_b[hb:hb + 64, h // 2, :],
                             start=True, stop=True), reads=['GT_s' + sfx, 'S_b'], writes=['ps%d' % bk])
                    gcb = gC[:, :, ci:ci + 1].broadcast_to([128, 4, 64])
                    H3 = H_s[:].rearrange("p (a c v) -> p a c v", a=4, c=2)[:, :, c, :]
                    S.op('pool', lambda gcb=gcb: P.tensor_tensor(out=S_t[:], in0=S_f[:], in1=gcb, op=ALU.mult), reads=['S_f', 'gC'], writes=['S_t'])
                    S.op('pool', lambda H3=H3: P.tensor_tensor(out=S_t[:], in0=S_t[:], in1=H3, op=ALU.add), reads=['S_t', 'H_s' + sfx], writes=['S_t'])
                    for par in range(2):
                        pr_ = slice(par * 64, (par + 1) * 64)
                        S.op('dve', lambda par=par, pr_=pr_: V.tensor_tensor(out=S_f[pr_].rearrange("p a v -> p (a v)"), in0=ps[par][pr_, 0:256], in1=S_t[pr_].rearrange("p a v -> p (a v)"), op=ALU.add),
                             reads=['ps%d' % par, 'S_t'], writes=['S_f'])
                    S.op('act', lambda: A.copy(out=S_b[:], in_=S_f[:]), reads=['S_f'], writes=['S_b'])
                    if seq_ends[ci]:
                        store_state(state_dsts[ci])

            def out_stage(tt, ya_dst):
                O_sb = OSB[tt]
                sfx = "_%d" % tt
                O3 = O_sb[:].rearrange("p (h v) -> p h v", h=8)
                S.op('dve', lambda: V.tensor_reduce(out=st8[:, 0, :], in_=O3, axis=AX.X, op=ALU.add), reads=[OSBN[tt]], writes=['st8'])
                S.op('act', lambda: A.activation(out=o_sq[:], in_=O_sb[:], func=AF.Square), reads=[OSBN[tt]], writes=['sq_t'])
                S.op('dve', lambda: V.tensor_reduce(out=st8[:, 1, :], in_=o_sq[:].rearrange("p (h v) -> p h v", h=8), axis=AX.X, op=ALU.add), reads=['sq_t'], writes=['st8'])
                S.op('dve', lambda: V.tensor_scalar(out=st8[:, 2, :], in0=st8[:, 0, :], scalar1=1.0 / 64, scalar2=None, op0=ALU.mult), reads=['st8'], writes=['st8'])
                S.op('dve', lambda: V.tensor_tensor(out=st8[:, 3, :], in0=st8[:, 2, :], in1=st8[:, 2, :], op=ALU.mult), reads=['st8'], writes=['st8'])
                S.op('dve', lambda: V.scalar_tensor_tensor(out=st8[:, 4, :], in0=st8[:, 1, :], scalar=1.0 / 64, in1=st8[:, 3, :], op0=ALU.mult, op1=ALU.subtract),
                     reads=['st8'], writes=['st8'])
                S.op('act', lambda: A.activation(out=st8[:, 5, :], in_=st8[:, 4, :], func=AF.Ln, bias=eps_g[:, 0:1]), reads=['st8', 'eps_g'], writes=['st8'])
                S.op('act', lambda: A.activation(out=st8[:, 5, :], in_=st8[:, 5, :], func=AF.Exp, scale=-0.5), reads=['st8'], writes=['st8'])
                on3 = o_n[:].rearrange("p (h v) -> p h v", h=8)
                S.op('dve', lambda: V.tensor_tensor(out=on3, in0=O3, in1=st8[:, 2, :].unsqueeze(2).broadcast_to([128, 8, 64]), op=ALU.subtract),
                     reads=[OSBN[tt], 'st8'], writes=['o_n'])
                S.op('dve', lambda: V.tensor_tensor(out=on3, in0=on3, in1=st8[:, 5, :].unsqueeze(2).broadcast_to([128, 8, 64]), op=ALU.mult), reads=['o_n', 'st8'], writes=['o_n'])
                S.op('pool', lambda: P.tensor_tensor(out=o_n[:], in0=o_n[:], in1=lnw_b[:], op=ALU.mult), reads=['o_n', 'lnw_b'], writes=['o_n'])
                S.op('pool', lambda: P.tensor_tensor(out=o_n[:], in0=o_n[:], in1=lnb_b[:], op=ALU.add), reads=['o_n', 'lnb_b'], writes=['o_n'])
                for j in range(4):
                    S.op('pe', lambda j=j: T.matmul(ps[0][:, j * 2:j * 2 + 2], lhsT=rk_b[:, j, tt * 128:(tt + 1) * 128], rhs=headsel_b, start=True, stop=True),
                         reads=['rk_b', 'cb'], writes=['ps0'])
                S.op('dve', lambda: V.tensor_copy(out=st8[:, 0, :], in_=ps[0][:, 0:8]), reads=['ps0'], writes=['st8'])
                S.op('dve', lambda: V.tensor_tensor(out=o_t[:].rearrange("p (h v) -> p h v", h=8), in0=Vtok[:, tt, :].rearrange("p (h v) -> p h v", h=8),
                     in1=st8[:, 0, :].unsqueeze(2).broadcast_to([128, 8, 64]), op=ALU.mult), reads=['Vtok', 'st8'], writes=['t_t'])
                S.op('pool', lambda: P.tensor_tensor(out=o_n[:], in0=o_n[:], in1=o_t[:], op=ALU.add), reads=['o_n', 't_t'], writes=['o_n'])
                S.op('pe', lambda: T.matmul(ps[1][:, :], lhsT=sgl_b[:, 0, tt * 128:(tt + 1) * 128], rhs=g2_t[:, 0, :], start=True, stop=False), reads=['sgl_b', 'g2_t'], writes=['ps1'])
                S.op('pe', lambda: T.matmul(ps[1][:, :], lhsT=sgl_b[:, 1, tt * 128:(tt + 1) * 128], rhs=g2_t[:, 1, :], start=False, stop=True), reads=['sgl_b', 'g2_t'], writes=['ps1'])
                S.op('dve', lambda: V.tensor_tensor(out=ya_b[:], in0=ps[1][:, :], in1=o_n[:], op=ALU.mult), reads=['ps1', 'o_n'], writes=['ya_b'])
                for j in range(4):
                    S.op('pe', lambda j=j: T.transpose(out=pb5[:, j * 128:(j + 1) * 128], in_=ya_b[:, j * 128:(j + 1) * 128], identity=ident_b), reads=['ya_b', 'cb'], writes=['ps5'])
                S.op('act', lambda: A.copy(out=yaT_st[:], in_=pb5[:, 0:512].rearrange("p (k t) -> p k t", k=4)), reads=['ps5'], writes=['yaT_st'])
                S.dma('sp', lambda: nc.sync.dma_start(out=ya_dst, in_=yaT_st[:]), reads=['yaT_st'], writes=['yaT_scr'])

            class Cfg:
                pass

            def X1(cfg):
                for tt in range(2):
                    load_norm_transpose(cfg.x_src[tt * 128:(tt + 1) * 128, :], xnT[:, :, tt * 128:(tt + 1) * 128], g1_b)
                rwkv_proj(cfg.main, cfg.seq_starts, cfg.shift_srcs, allcols=cfg.allcols)
                if cfg.post_x1 is not None:
                    cfg.post_x1()

            def X2pre(cfg):
                prep_pre(cfg.main)

            def X2pA(cfg):
                prep_pairs(cfg.main, (0, 2), 0, (3, 4, 5))

            def X2pB(cfg):
                prep_pairs(cfg.main, (1, 3), 1, (0, 1, 2))

            def X2T(cfg):
                prep_T()

            def X2b(cfg):
                for tt in range(2):
                    sb_part(tt, None, cfg.main, cfg.kouts[tt], cfg.vouts[tt], cfg.kT_dsts[tt], cfg.v_dsts[tt], cfg.qT_dsts[tt])

            def Y1(cfg, tt):
                return chunk_math(tt, cfg.main)

            def Y2(cfg):
                for tt in range(2):
                    seq_pass(tt, cfg.main, cfg.seq_starts, cfg.seq_ids, cfg.seq_ends, cfg.state_dsts)
                    if cfg.main:
                        out_stage(tt, cfg.ya_dsts[tt])

            def prompt_shift_out():
                S.op('dve', lambda: V.tensor_copy(out=shst[:], in_=carry[:]), reads=['carry'], writes=['shst'])
                for cc in range(15):
                    n = 128 if cc < 14 else 32
                    S.dma('sp', lambda cc=cc, n=n: nc.sync.dma_start(out=shp[cc * 128:cc * 128 + n].rearrange("(p o) -> p o", o=1), in_=shst[0:n, cc:cc + 1]),
                          reads=['shst'], writes=['out_sh'])

            def sample_shift_out():
                S.op('dve', lambda: V.tensor_copy(out=shst4[:], in_=plast[:]), reads=['plast'], writes=['shst4'])
                for q in range(4):
                    for cc in range(15):
                        n = 128 if cc < 14 else 32
                        S.dma('sp', lambda cc=cc, n=n, q=q: nc.sync.dma_start(out=shs[q, cc * 128:cc * 128 + n].rearrange("(p o) -> p o", o=1), in_=shst4[0:n, cc, q:q + 1]),
                              reads=['shst4'], writes=['out_sh'])

            cfgs = []
            nsup = (NPRE + NMAIN) // NT
            for si in range(nsup):
                cfg = Cfg()
                main = ((si // 2) % 2 == 1)
                t0 = si * NT
                m0 = (si // 4) * 512 + (si % 2) * NT
                cfg.main = main
                cfg.x_src = xp[t0:t0 + NT, :]
                cfg.seq_starts = [si == 0, False, False, False]
                cfg.shift_srcs = [None] * 4
                cfg.seq_ids = [None] * 4
                cfg.seq_ends = [False, False, False, si == nsup - 1]
                cfg.state_dsts = [None, None, None, wkvp]
                cfg.allcols = (not main and si % 2 == 1)
                cfg.post_x1 = prompt_shift_out if si == nsup - 1 else None
                cfg.kouts = [[(0, 128, kp[:, m0 + tt * 128:m0 + (tt + 1) * 128, :].rearrange("h t d -> t h d"))] if main else None for tt in range(2)]
                cfg.vouts = [[(0, 128, vp[:, m0 + tt * 128:m0 + (tt + 1) * 128, :].rearrange("h t d -> t h d"))] if main else None for tt in range(2)]
                cfg.kT_dsts = [kT_scr[:, :, t0 + tt * 128:t0 + (tt + 1) * 128].rearrange("k p t -> p k t") for tt in range(2)]
                cfg.v_dsts = [v_scr[t0 // 128 + tt] for tt in range(2)]
                cfg.qT_dsts = [qT_scr[:, :, m0 + tt * 128:m0 + (tt + 1) * 128].rearrange("k p t -> p k t") if main else None for tt in range(2)]
                cfg.ya_dsts = [yaT_scr[:, :, m0 + tt * 128:m0 + (tt + 1) * 128].rearrange("k p t -> p k t") if main else None for tt in range(2)]
                cfgs.append(cfg)
            cfg = Cfg()
            cfg.main = True
            cfg.x_src = xs[0:NT, :]
            cfg.seq_starts = [True] * 4
            cfg.shift_srcs = [sh_in[q] for q in range(4)]
            cfg.seq_ids = [0, 1, 2, 3]
            cfg.seq_ends = [True] * 4
            cfg.state_dsts = [wkvs[q] for q in range(4)]
            cfg.allcols = True
            cfg.post_x1 = sample_shift_out
            cfg.kouts = [[(c * 64, (c + 1) * 64, ksn[tt * 2 + c].rearrange("h t d -> t h d")) for c in range(2)] for tt in range(2)]
            cfg.vouts = [[(c * 64, (c + 1) * 64, vsn[tt * 2 + c].rearrange("h t d -> t h d")) for c in range(2)] for tt in range(2)]
            cfg.kT_dsts = [kTs_scr[:, :, tt * 128:(tt + 1) * 128].rearrange("k p t -> p k t") for tt in range(2)]
            cfg.v_dsts = [vs_scr[tt] for tt in range(2)]
            cfg.qT_dsts = [qTs_scr[:, :, tt * 128:(tt + 1) * 128].rearrange("k p t -> p k t") for tt in range(2)]
            cfg.ya_dsts = [yaT_scr[:, :, NMAIN + tt * 128:NMAIN + (tt + 1) * 128].rearrange("k p t -> p k t") for tt in range(2)]
            cfgs.append(cfg)

            S.emit([S.record(X1, cfgs[0])])
            for si, cfg in enumerate(cfgs):
                S.emit([S.record(X2pre, cfg)])
                S.emit([S.record(X2pA, cfg), S.record(X2pB, cfg), S.record(X2b, cfg)])
                S.emit([S.record(X2T, cfg)])
                S.emit([S.record(Y1, cfg, 0), S.record(Y1, cfg, 1)])
                if si >= 1:
                    for _ in range(2):
                        if deferred:
                            deferred.pop(0)()
                lists = [S.record(Y2, cfg)]
                if si + 1 < len(cfgs):
                    lists.append(S.record(X1, cfgs[si + 1]))
                S.emit(lists)
        while deferred:
            deferred.pop(0)()
        S.barrier()
        ph2 = contextlib.ExitStack()
        with ph2:
            def sb2(name, shape, dt=F32):
                return ph2.enter_context(nc.sbuf_tensor(name, list(shape), dt))
            kTp = [sb2("kTp%d" % i, [128, NK], BF16) for i in range(2)]
            qTp = [sb2("qTp%d" % i, [128, NMAIN], BF16) for i in range(2)]
            Vp = [sb2("Vp%d" % i, [128, NK // 128, 128], BF16) for i in range(2)]
            e1_t = [sb2("e1_%d" % i, [128, 2, 512], BF16) for i in range(3)]
            sp_t = [sb2("sp_%d" % i, [128, 2, 512], BF16) for i in range(4)]
            et_t = [sb2("et_%d" % i, [128, 2, 512], BF16) for i in range(2)]
            att_t = [sb2("att_%d" % i, [128, 2, 512], BF16) for i in range(2)]
            yb_st = sb2("yb_st", [128, 512], BF16)
            cmk = sb2("cmk", [128, NCONST - NC_A], BF16)
            S.dma('sp', lambda: nc.sync.dma_start(out=cmk[:], in_=consts[:, NC_A:NCONST]), writes=['cst'])
            cmask_b = [cmk[:, i * 512:(i + 1) * 512] for i in range(4)]
            cmask64_b = cmk[:, 2048:2112]

            def run_tiles(tiles, nq, out_bank):
                npair = len(tiles) // 2

                def stA(j):
                    t0_, t1_ = tiles[2 * j], tiles[2 * j + 1]
                    nk = t0_['nk']
                    for hh, t_ in ((0, t0_), (1, t1_)):
                        S.op('pe', lambda t_=t_, hh=hh: T.matmul(ps[hh][0:nk, 0:nq], lhsT=t_['kT'], rhs=t_['qT'], start=True, stop=True), reads=t_['rk'], writes=['ps%d' % hh])
                    e1 = e1_t[j % 3]
                    S.op('act', lambda: A.activation(out=e1[0:nk, :, 0:nq], in_=pall[0:nk, 0:2, 0:nq], func=AF.Exp), reads=['ps0', 'ps1'], writes=[('e1', j % 3)])
                    if t0_['mask'] is not None:
                        S.op('dve', lambda: V.tensor_tensor(out=e1[0:nk, :, 0:nq], in0=e1[0:nk, :, 0:nq], in1=t0_['mask'].unsqueeze(1).broadcast_to([nk, 2, nq]), op=ALU.mult),
                             reads=[('e1', j % 3), 'cst'], writes=[('e1', j % 3)])
                    S.op('act', lambda: A.activation(out=sp_t[j % 4][0:nk, :, 0:nq], in_=e1[0:nk, :, 0:nq], func=AF.Ln, bias=one_t[0:nk, 0:1]), reads=[('e1', j % 3), 'one_t'], writes=[('sp', j % 4)])

                def stB1(j):
                    t0_, t1_ = tiles[2 * j], tiles[2 * j + 1]
                    nk = t0_['nk']
                    for hh, t_ in ((0, t0_), (1, t1_)):
                        S.op('pe', lambda t_=t_, hh=hh: T.matmul(ps[2 + hh][0:128, 0:nq], lhsT=t_['tri'], rhs=sp_t[j % 4][0:nk, hh, 0:nq], start=t_['first'], stop=False),
                             reads=[('sp', j % 4), 'cst', 'cb'], writes=['ps%d' % (2 + hh)])
                    S.op('act', lambda: A.activation(out=et_t[j % 2][0:nk, :, 0:nq], in_=pall[0:nk, 2:4, 0:nq], func=AF.Exp, scale=-1.0), reads=['ps2', 'ps3'], writes=[('et', j % 2)])
                    S.op('dve', lambda: V.tensor_tensor(out=att_t[j % 2][0:nk, :, 0:nq], in0=e1_t[j % 3][0:nk, :, 0:nq], in1=et_t[j % 2][0:nk, :, 0:nq], op=ALU.mult),
                         reads=[('e1', j % 3), ('et', j % 2)], writes=[('att', j % 2)])

                def stUpp(j):
                    for hh in range(2):
                        t_ = tiles[2 * j + hh]
                        nk = t_['nk']
                        if not t_['last']:
                            S.op('pe', lambda t_=t_, hh=hh, nk=nk: T.matmul(ps[2 + hh][0:128, 0:nq], lhsT=t_['upp'], rhs=sp_t[j % 4][0:nk, hh, 0:nq], start=False, stop=True),
                                 reads=[('sp', j % 4), 'cst', 'cb'], writes=['ps%d' % (2 + hh)])

                def stAV(j):
                    for hh in range(2):
                        t_ = tiles[2 * j + hh]
                        nk = t_['nk']
                        S.op('pe', lambda t_=t_, hh=hh, nk=nk: T.matmul(ps[out_bank][hh * 64:(hh + 1) * 64, 0:nq], lhsT=t_['V'], rhs=att_t[j % 2][0:nk, hh, 0:nq], start=t_['first'], stop=t_['last']),
                             reads=[('att', j % 2)] + t_['rv'], writes=['ps%d' % out_bank])

                for j in range(npair + 3):
                    if 0 <= j - 2 < npair:
                        stB1(j - 2)
                    if j < npair:
                        stA(j)
                    if 0 <= j - 3 < npair:
                        stAV(j - 3)
                    if 0 <= j - 2 < npair:
                        stUpp(j - 2)

            for pr in range(4):
                bi = pr % 2
                S.dma('sp', lambda: nc.sync.dma_start(out=kTp[bi][:], in_=kT_scr[pr]), reads=['kq_scr'], writes=[('kTp', bi)])
                S.dma('sp', lambda: nc.sync.dma_start(out=qTp[bi][:], in_=qT_scr[pr]), reads=['kq_scr'], writes=[('qTp', bi)])
                for b0 in range(0, NK // 128, 16):
                    S.dma('sp', lambda b0=b0: nc.sync.dma_start(out=Vp[bi][:, b0:b0 + 16, :], in_=v_scr[b0:b0 + 16, :, pr * 128:(pr + 1) * 128].rearrange("b p c -> p b c")),
                          reads=['v_scr'], writes=[('Vp', bi)])
                for G in range(NMAIN // 512):
                    tiles = []
                    kbmax = 4 * (2 * G + 1) + 3
                    for kb in range(kbmax, -1, -1):
                        for hh in range(2):
                            hb = hh * 64
                            di = kb - 4 * (2 * G + 1)
                            pre = kb < 4
                            tiles.append(dict(hh=hh, nk=128, kT=kTp[bi][hb:hb + 64, kb * 128:(kb + 1) * 128], qT=qTp[bi][hb:hb + 64, G * 512:(G + 1) * 512],
                                              V=Vp[bi][:, kb, hb:hb + 64], mask=(cmask_b[di] if di >= 0 else None),
                                              tri=(triP_b if pre else tri_b), upp=(uppP_b if pre else upp_b), first=(kb == kbmax), last=(kb == 0),
                                              rk=[('kTp', bi), ('qTp', bi)], rv=[('Vp', bi)]))
                    run_tiles(tiles, 512, 4)
                    S.op('act', lambda: A.copy(out=yb_st[:], in_=ps[4][:, :]), reads=['ps4'], writes=['yb_st'])
                    S.dma('sp', lambda G=G: nc.sync.dma_start(out=ybT_scr[pr, :, G * 512:(G + 1) * 512], in_=yb_st[:]), reads=['yb_st'], writes=['ybT_scr'])
            ckf = sb2("ckf", [128, 512]); ckb = sb2("ckb", [128, 512], BF16)
            kTc = sb2("kTc", [128, 4, PAST + 64], BF16); Vc = sb2("Vc", [128, 9, 512], BF16); qTs = sb2("qTs", [128, 4, 64], BF16)
            for q in range(4):
                for blk in range(8):
                    S.dma('sp', lambda blk=blk: nc.sync.dma_start(out=ckf[:].rearrange("p (h d) -> p h d", h=8), in_=ck[q, :, blk * 128:(blk + 1) * 128, :].rearrange("h t d -> t h d")), writes=['ckf'])
                    S.op('dve', lambda: V.tensor_copy(out=ckb[:], in_=ckf[:]), reads=['ckf'], writes=['ckb'])
                    for j in range(4):
                        S.op('pe', lambda j=j: T.transpose(out=pb[1][:, j * 128:(j + 1) * 128], in_=ckb[:, j * 128:(j + 1) * 128], identity=ident_b), reads=['ckb', 'cst'], writes=['ps7'])
                    S.op('act', lambda blk=blk: A.copy(out=kTc[:, :, blk * 128:(blk + 1) * 128], in_=pb[1][:, 0:512].rearrange("p (k t) -> p k t", k=4)), reads=['ps7'], writes=['kTc'])
                    S.dma('sp', lambda blk=blk: nc.sync.dma_start(out=ckf[:].rearrange("p (h d) -> p h d", h=8), in_=cv[q, :, blk * 128:(blk + 1) * 128, :].rearrange("h t d -> t h d")), writes=['ckf'])
                    S.op('dve', lambda blk=blk: V.tensor_copy(out=Vc[:, blk, :], in_=ckf[:]), reads=['ckf'], writes=['Vc'])
                S.dma('sp', lambda: nc.sync.dma_start(out=kTc[:, :, PAST:PAST + 64], in_=kTs_scr[:, :, q * 64:(q + 1) * 64].rearrange("k p t -> p k t")), reads=['kq_scr'], writes=['kTc'])
                S.dma('sp', lambda: nc.sync.dma_start(out=qTs[:], in_=qTs_scr[:, :, q * 64:(q + 1) * 64].rearrange("k p t -> p k t")), reads=['kq_scr'], writes=['qTs'])
                S.dma('sp', lambda: nc.sync.dma_start(out=Vc[0:64, 8, :], in_=vs_scr[q // 2, (q % 2) * 64:(q % 2) * 64 + 64, :]), reads=['v_scr'], writes=['Vc'])
                for pr in range(4):
                    tiles = []
                    for kb in range(8, -1, -1):
                        for hh in range(2):
                            hb = hh * 64
                            nk = 64 if kb == 8 else 128
                            tiles.append(dict(hh=hh, nk=nk, kT=kTc[hb:hb + 64, pr, kb * 128:kb * 128 + nk], qT=qTs[hb:hb + 64, pr, :],
                                              V=Vc[0:nk, kb, pr * 128 + hb:pr * 128 + hb + 64], mask=(cmask64_b[0:64, :] if kb == 8 else None),
                                              tri=(tri_b[0:64, :] if kb == 8 else tri_b), upp=(upp_b[0:64, :] if kb == 8 else upp_b), first=(kb == 8), last=(kb == 0),
                                              rk=['kTc', 'qTs'], rv=['Vc']))
                    run_tiles(tiles, 64, 4)
                    S.op('act', lambda: A.copy(out=yb_st[:, 0:64], in_=ps[4][:, 0:64]), reads=['ps4'], writes=['yb_st'])
                    S.dma('sp', lambda pr=pr: nc.sync.dma_start(out=ybT_scr[pr, :, NMAIN + q * 64:NMAIN + (q + 1) * 64], in_=yb_st[:, 0:64]), reads=['yb_st'], writes=['ybT_scr'])
        S.barrier()
        ph3 = contextlib.ExitStack()
        with ph3:
            def sb3(name, shape, dt=F32):
                return ph3.enter_context(nc.sbuf_tensor(name, list(shape), dt))
            gn2_b = sb3("gn2_b", [128, D])
            S.dma('sp', lambda: nc.sync.dma_start(out=gn2_b[:], in_=gn2.partition_broadcast(128)), writes=['gn2_b'])
            wb = [sb3("wb%d" % i, [128, 16384], BF16) for i in range(2)]
            xres2 = [sb3("xres%d" % i, [128, 4, D]) for i in range(2)]; xnT3 = sb3("xnT3", [128, 8, 512], BF16)
            XR = {'t': xres2[0], 'n': ('xres', 0)}
            yaT3 = sb3("yaT3", [128, 4, 512], BF16); ybT3 = sb3("ybT3", [128, 4, 512], BF16)
            mT = sb3("mT", [128, 8, 512], BF16); h2T = sb3("h2T", [128, 32, 512], BF16)
            gT = h2T[:, 0:16, :]
            u1 = sb3("u1", [128, 512], BF16); u2 = sb3("u2", [128, 512], BF16); rl = sb3("rl", [128, 512], BF16)
            yst = [sb3("yst%d" % i, [128, 512]) for i in range(2)]
            wcount = [0]

            def wload(kind):
                i = wcount[0] % 2
                wcount[0] += 1
                buf = wb[i]
                if kind == 'gate':
                    v = buf[:, :].rearrange("p (k c) -> p k c", k=8)
                    for kc in range(8):
                        S.dma('sp', lambda kc=kc: nc.sync.dma_start(out=v[:, kc, :], in_=w_in_b[kc * 128:(kc + 1) * 128, C_GATE:DIN]), reads=[('w_in_b', C_GATE)], writes=[('wb', i)])
                elif kind == 'upo':
                    v = buf[:, :].rearrange("p (k c) -> p k c", k=16)
                    S.dma('sp', lambda: nc.sync.dma_start(out=v[:, 0:4, :], in_=wua_b.rearrange("(k p) c -> p k c", p=128)), reads=['wua_b'], writes=[('wb', i)])
                    S.dma('sp', lambda: nc.sync.dma_start(out=v[:, 4:8, :], in_=wub_b.rearrange("(k p) c -> p k c", p=128)), reads=['wub_b'], writes=[('wb', i)])
                    S.dma('sp', lambda: nc.sync.dma_start(out=v[:, 8:16, :], in_=wo_b.rearrange("(k p) c -> p k c", p=128)), reads=['wo_b'], writes=[('wb', i)])
                elif kind in ('f1a', 'f1b'):
                    c0 = 0 if kind == 'f1a' else 2048
                    v = buf[:, :].rearrange("p (k c) -> p k c", k=8)
                    for kc in range(8):
                        S.dma('sp', lambda kc=kc: nc.sync.dma_start(out=v[:, kc, :], in_=wf1_b[kc * 128:(kc + 1) * 128, c0:c0 + 2048]), reads=['wf1_b'], writes=[('wb', i)])
                else:
                    c0 = 0 if kind == 'f2a' else 512
                    v = buf[:, :].rearrange("p (k c) -> p k c", k=32)
                    for k0 in range(0, 32, 8):
                        S.dma('sp', lambda k0=k0: nc.sync.dma_start(out=v[:, k0:k0 + 8, :], in_=wf2_b[k0 * 128:(k0 + 8) * 128, c0:c0 + 512].rearrange("(k p) c -> p k c", p=128)),
                              reads=['wf2_b'], writes=[('wb', i)])
                return i, v

            xn4 = [xn_b] + [sb3("xn4_%d" % i, [128, D], BF16) for i in range(1, 4)]
            ssq4 = sb3("ssq4", [128, 3, 4])
            pbv = [ps[4 + i][:, :].bitcast(BF16) for i in range(4)]

            def norm_all(ntt, g_b):
                for tt in range(ntt):
                    S.op('act', lambda tt=tt: A.activation(out=xsq[:], in_=XR['t'][:, tt, :], func=AF.Square, accum_out=ssq4[:, 0, tt:tt + 1]), reads=[XR['n']], writes=['xsq', ('ssq4', tt)])
                S.op('act', lambda: A.activation(out=ssq4[:, 1, 0:ntt], in_=ssq4[:, 0, 0:ntt], func=AF.Ln, bias=eps_t[:, 0:1], scale=1.0 / D),
                     reads=[('ssq4', t_) for t_ in range(ntt)] + ['eps_t'], writes=['ssq4b'])
                S.op('act', lambda: A.activation(out=ssq4[:, 2, 0:ntt], in_=ssq4[:, 1, 0:ntt], func=AF.Exp, scale=-0.5), reads=['ssq4b'], writes=['ssq4c'])
                for tt in range(ntt):
                    S.op('dve', lambda tt=tt: V.scalar_tensor_tensor(out=xn4[tt][:], in0=XR['t'][:, tt, :], scalar=ssq4[:, 2, tt:tt + 1], in1=g_b[:], op0=ALU.mult, op1=ALU.mult),
                         reads=[XR['n'], 'ssq4c', 'g1_b', 'gn2_b'], writes=[('xn4', tt) if tt else 'xn_b'])
                for tt in range(ntt):
                    for kc in range(8):
                        S.op('pe', lambda kc=kc, tt=tt: T.transpose(out=pbv[tt][:, kc * 128:(kc + 1) * 128], in_=xn4[tt][:, kc * 128:(kc + 1) * 128], identity=ident_b),
                             reads=[('xn4', tt) if tt else 'xn_b', 'cst'], writes=['ps%d' % (4 + tt)])
                for tt in range(ntt):
                    eng = 'act' if tt % 2 == 0 else 'dve'
                    if eng == 'act':
                        S.op('act', lambda tt=tt: A.copy(out=xnT3[:, :, tt * 128:(tt + 1) * 128], in_=pbv[tt][:, :].rearrange("p (k t) -> p k t", k=8)), reads=['ps%d' % (4 + tt)], writes=['xnT3'])
                    else:
                        S.op('dve', lambda tt=tt: V.tensor_copy(out=xnT3[:, :, tt * 128:(tt + 1) * 128], in_=pbv[tt][:, :].rearrange("p (k t) -> p k t", k=8)), reads=['ps%d' % (4 + tt)], writes=['xnT3'])

            def load_x(idx, x_src, ntb):
                for tt in range(ntb // 128):
                    S.dma('sp', lambda tt=tt: nc.sync.dma_start(out=xres2[idx][:, tt, :], in_=x_src[tt * 128:(tt + 1) * 128, :]), writes=[('xres', idx)])

            GPRE = {'g': None}

            def phaseB(x_src, y_dst, ycol0, ntb, bidx, nxt):
                XR['t'] = xres2[bidx]
                XR['n'] = ('xres', bidx)
                ntt = ntb // 128
                wq = [GPRE['g'] if GPRE['g'] is not None else wload('gate'), wload('upo')]
                GPRE['g'] = None
                if bidx == 0 and ycol0 == 0:
                    load_x(0, x_src, ntb)
                S.dma('sp', lambda: nc.sync.dma_start(out=yaT3[:, :, 0:ntb], in_=yaT_scr[:, :, ycol0:ycol0 + ntb].rearrange("k p t -> p k t")), reads=['yaT_scr'], writes=['yaT3'])
                S.dma('sp', lambda: nc.sync.dma_start(out=ybT3[:, :, 0:ntb], in_=ybT_scr[:, :, ycol0:ycol0 + ntb].rearrange("k p t -> p k t")), reads=['ybT_scr'], writes=['ybT3'])
                norm_all(ntt, g1_b)
                wi, wv = wq[0]
                for gc in range(16):
                    bk = gc % 4
                    for kc in range(8):
                        S.op('pe', lambda kc=kc, gc=gc, bk=bk: T.matmul(ps[bk][:, 0:ntb], lhsT=wv[:, kc, gc * 128:(gc + 1) * 128], rhs=xnT3[:, kc, 0:ntb], start=(kc == 0), stop=(kc == 7)),
                             reads=[('wb', wi), 'xnT3'], writes=['ps%d' % bk])
                    S.op('act', lambda gc=gc, bk=bk: A.activation(out=gT[:, gc, 0:ntb], in_=ps[bk][:, 0:ntb], func=AF.Sigmoid), reads=['ps%d' % bk], writes=['h2T'])
                wq.append(wload('f1a'))
                wi, wv = wq[1]
                for oc in range(8):
                    ba, bb = (oc % 2) * 2, (oc % 2) * 2 + 1
                    for kc in range(4):
                        S.op('pe', lambda kc=kc, oc=oc, ba=ba: T.matmul(ps[ba][:, 0:ntb], lhsT=wv[:, kc, oc * 128:(oc + 1) * 128], rhs=yaT3[:, kc, 0:ntb], start=(kc == 0), stop=(kc == 3)),
                             reads=[('wb', wi), 'yaT3'], writes=['ps%d' % ba])
                    for kc in range(4):
                        S.op('pe', lambda kc=kc, oc=oc, bb=bb: T.matmul(ps[bb][:, 0:ntb], lhsT=wv[:, 4 + kc, oc * 128:(oc + 1) * 128], rhs=ybT3[:, kc, 0:ntb], start=(kc == 0), stop=(kc == 3)),
                             reads=[('wb', wi), 'ybT3'], writes=['ps%d' % bb])
                    S.op('dve', lambda oc=oc, ba=ba: V.tensor_tensor(out=u1[:, 0:ntb], in0=ps[ba][:, 0:ntb], in1=gT[:, oc, 0:ntb], op=ALU.mult), reads=['ps%d' % ba, 'h2T'], writes=['u1'])
                    S.op('dve', lambda oc=oc, bb=bb: V.tensor_tensor(out=u2[:, 0:ntb], in0=ps[bb][:, 0:ntb], in1=gT[:, 8 + oc, 0:ntb], op=ALU.mult), reads=['ps%d' % bb, 'h2T'], writes=['u2'])
                    S.op('pool', lambda oc=oc: P.tensor_tensor(out=mT[:, oc, 0:ntb], in0=u1[:, 0:ntb], in1=u2[:, 0:ntb], op=ALU.add), reads=['u1', 'u2'], writes=['mT'])
                for tt in range(ntt):
                    for hf in range(2):
                        bk = 4 + hf
                        for kc in range(8):
                            S.op('pe', lambda kc=kc, tt=tt, hf=hf, bk=bk: T.matmul(ps[bk][:, :], lhsT=mT[:, kc, tt * 128:(tt + 1) * 128], rhs=wv[:, 8 + kc, hf * 512:(hf + 1) * 512], start=(kc == 0), stop=(kc == 7)),
                                 reads=[('wb', wi), 'mT'], writes=['ps%d' % bk])
                        S.op('dve', lambda tt=tt, hf=hf, bk=bk: V.tensor_tensor(out=XR['t'][:, tt, hf * 512:(hf + 1) * 512], in0=ps[bk][:, :], in1=XR['t'][:, tt, hf * 512:(hf + 1) * 512], op=ALU.add),
                             reads=['ps%d' % bk, XR['n']], writes=[XR['n']])
                wq.append(wload('f1b'))
                norm_all(ntt, gn2_b)
                if nxt is not None:
                    load_x(1 - bidx, nxt[0], nxt[1])
                for part in range(2):
                    wi, wv = wq[2 + part]
                    if part == 1:
                        wq.append(wload('f2a'))
                    for fl in range(16):
                        fc = part * 16 + fl
                        bk = fc % 4
                        for kc in range(8):
                            S.op('pe', lambda kc=kc, fl=fl, bk=bk, wv=wv: T.matmul(ps[bk][:, 0:ntb], lhsT=wv[:, kc, fl * 128:(fl + 1) * 128], rhs=xnT3[:, kc, 0:ntb], start=(kc == 0), stop=(kc == 7)),
                                 reads=[('wb', wi), 'xnT3'], writes=['ps%d' % bk])
                        S.op('act', lambda bk=bk: A.activation(out=rl[:, 0:ntb], in_=ps[bk][:, 0:ntb], func=AF.Relu), reads=['ps%d' % bk], writes=['rl'])
                        S.op('dve', lambda fc=fc, bk=bk: V.tensor_tensor(out=h2T[:, fc, 0:ntb], in0=ps[bk][:, 0:ntb], in1=rl[:, 0:ntb], op=ALU.mult), reads=['ps%d' % bk, 'rl'], writes=['h2T'])
                wq.append(wload('f2b'))
                cnt = 0
                for hf in range(2):
                    wi, wv = wq[4 + hf]
                    for tt in range(ntt):
                        bk = 4 + (cnt % 2)
                        for fc in range(32):
                            S.op('pe', lambda fc=fc, tt=tt, bk=bk, wv=wv: T.matmul(ps[bk][:, :], lhsT=h2T[:, fc, tt * 128:(tt + 1) * 128], rhs=wv[:, fc, :], start=(fc == 0), stop=(fc == 31)),
                                 reads=[('wb', wi), 'h2T'], writes=['ps%d' % bk])
                        yb_ = yst[cnt % 2]
                        S.op('dve', lambda tt=tt, hf=hf, bk=bk, yb_=yb_: V.tensor_tensor(out=yb_[:], in0=ps[bk][:, :], in1=XR['t'][:, tt, hf * 512:(hf + 1) * 512], op=ALU.add),
                             reads=['ps%d' % bk, XR['n']], writes=[('yst', cnt % 2)])
                        S.dma('pool', lambda tt=tt, hf=hf, yb_=yb_: P.dma_start(out=y_dst[tt * 128:(tt + 1) * 128, hf * 512:(hf + 1) * 512], in_=yb_[:]), reads=[('yst', cnt % 2)], writes=['out_y'])
                        cnt += 1
                    if hf == 0 and nxt is not None:
                        GPRE['g'] = wload('gate')

            nG = NMAIN // 512
            for Gs in range(nG):
                nxt = (xp[(2 * Gs + 3) * 512:(2 * Gs + 4) * 512, :], 512) if Gs + 1 < nG else (xs, 256)
                phaseB(xp[(2 * Gs + 1) * 512:(2 * Gs + 2) * 512, :], yp[Gs * 512:(Gs + 1) * 512, :], Gs * 512, 512, Gs % 2, nxt)
            phaseB(xs, ys, NMAIN, 256, nG % 2, None)
        S.finish()
    return nc, S


_CACHE = {}


def kernel(**inp):
    f = lambda a: np.ascontiguousarray(np.asarray(a, dtype=np.float32))
    x_prompt = f(inp['x_prompt']); x_sample = f(inp['x_sample'])
    if 'nc' not in _CACHE:
        _CACHE['nc'] = build()
    nc, S = _CACHE['nc']
    consts = make_consts()
    shared = {
        'consts': consts.astype(ml_dtypes.bfloat16), 'constsf': np.ascontiguousarray(np.concatenate([consts[:, 0:128], consts[:, 128 + 2048 + 256:128 + 2048 + 256 + 128]], axis=1)), 'g1': f(inp['g_norm1'][0]), 'w_in': f(inp['w_in'][0]), 'mu': f(inp['rwkv_mu'][0]),
        'w0': f(inp['rwkv_w0'][0]), 'w2': f(inp['rwkv_w2'][0]), 'a0': f(inp['rwkv_a0'][0]), 'a2': f(inp['rwkv_a2'][0]),
        'g2': f(inp['rwkv_g2'][0]), 'k_k': f(inp['rwkv_k_k'][0]), 'k_a': f(inp['rwkv_k_a'][0]), 'r_k': f(inp['rwkv_r_k'][0]).reshape(512),
        'lnw': f(inp['rwkv_lnx_w'][0]), 'lnb': f(inp['rwkv_lnx_b'][0]), 'qg': f(inp['sb_q_norm_g'][0]), 'kg': f(inp['sb_k_norm_g'][0]),
        'wua': f(inp['w_up_a'][0]), 'wub': f(inp['w_up_b'][0]), 'wo': f(inp['w_out'][0]), 'gn2': f(inp['g_norm2'][0]),
        'wf1': f(inp['w_ff1'][0]), 'wf2': f(inp['w_ff2'][0]),
    }
    in_maps = []
    for c in range(8):
        b, g = c // 2, c % 2
        xpc = np.zeros((NPRE + NMAIN, D), np.float32)
        if g == 1:
            xpc[:] = x_prompt[b]
        else:
            xpc[512:] = x_prompt[b, :NPRE + NMAIN - 512]
        m = dict(shared)
        m['xp'] = xpc
        m['flag'] = np.full((1, 1), float(g), np.float32)
        sl = slice(c * NSEQ, (c + 1) * NSEQ)
        m['xs'] = f(x_sample[sl]).reshape(NSEQ * 64, D)
        m['ck'] = f(inp['cache_sb_k'][0, sl]); m['cv'] = f(inp['cache_sb_v'][0, sl])
        m['st'] = f(inp['state_rwkv_wkv'][0, sl]); m['sh'] = f(inp['state_rwkv_shift'][0, sl, 0])
        in_maps.append(m)
    res = run_bass_kernel_spmd(nc, in_maps, core_ids=list(range(8)))
    R = res.results
    y_p = np.zeros((4, 8192, D), np.float32); k_p = np.zeros((1, 4, 8, 8192, 64), np.float32); v_p = np.zeros_like(k_p)
    wkv_p = np.zeros((1, 4, 8, 64, 64), np.float32); sh_p = np.zeros((1, 4, 1, NRW), np.float32)
    y_s = np.zeros((32, 64, D), np.float32); k_s = np.zeros((1, 32, 8, 64, 64), np.float32); v_s = np.zeros_like(k_s)
    wkv_s = np.zeros((1, 32, 8, 64, 64), np.float32); sh_s = np.zeros((1, 32, 1, NRW), np.float32)
    for c in range(8):
        b, g = c // 2, c % 2
        r = R[c]
        for i in range(NMAIN // 512):
            ts = slice((2 * i + g) * 512, (2 * i + g + 1) * 512)
            ms = slice(i * 512, (i + 1) * 512)
            y_p[b, ts] = r['yp'][ms]; k_p[0, b, :, ts] = r['kp'][:, ms]; v_p[0, b, :, ts] = r['vp'][:, ms]
        if g == 1:
            wkv_p[0, b] = r['wkvp']; sh_p[0, b, 0] = r['shp']
        sl = slice(c * NSEQ, (c + 1) * NSEQ)
        y_s[sl] = r['ys'].reshape(NSEQ, 64, D); k_s[0, sl] = r['ksn']; v_s[0, sl] = r['vsn']
        wkv_s[0, sl] = r['wkvs']; sh_s[0, sl, 0] = r['shs']
    return (y_p, y_s, k_p, v_p, wkv_p, sh_p, k_s, v_s, wkv_s, sh_s)
```

```python
import contextlib
import numpy as np
import concourse.bass as bass
import concourse.mybir as mybir
from concourse.bass_utils import run_bass_kernel_spmd

F32 = mybir.dt.float32
BF16 = mybir.dt.bfloat16
AF = mybir.ActivationFunctionType
ALU = mybir.AluOpType
AX = mybir.AxisListType

D = 1024
DIN = 5408
NRW = 1824
NPRE = 4096
NMAIN = 4096
NT = 256
NSEQ = 4
PAST = 1024
DFF = 4096
C_SBQ, C_SBK, C_SBV, C_GATE = 1824, 2336, 2848, 3360
NCONST = 128 + 512 * 4 + 256 + 128 + 2 + 128 + 128 + 512 * 4 + 64
import ml_dtypes


def make_consts():
    p = np.arange(128)[:, None]
    cols = []
    cols.append(np.eye(128, dtype=np.float32))
    f = np.arange(512)[None, :]
    cols.append(((p % 64) < (f % 64)).astype(np.float32))
    cols.append(((p % 64) <= (f % 64)).astype(np.float32))
    cols.append(((f % 64) < (p % 64)).astype(np.float32))
    cols.append(((p % 64) == (f % 64)).astype(np.float32))
    t = np.arange(256)[None, :]
    cols.append(np.broadcast_to((t % 64 != 0), (128, 256)).astype(np.float32))
    q = np.arange(128)[None, :]
    cols.append(((p // 64) == (q // 64)).astype(np.float32))
    cols.append(((p // 64) == np.arange(2)[None, :]).astype(np.float32))
    cols.append((p >= q).astype(np.float32))
    cols.append((p < q).astype(np.float32))
    tq = np.arange(512)[None, :]
    for i in range(4):
        cols.append(((i * 128 + p) < tq).astype(np.float32))
    cols.append((p < np.arange(64)[None, :]).astype(np.float32))
    c = np.concatenate(cols, axis=1)
    assert c.shape[1] == NCONST, c.shape
    return np.ascontiguousarray(c)


class Sched:
    def __init__(self, nc, es):
        self.nc = nc
        self.eng = {'pe': nc.tensor, 'act': nc.scalar, 'dve': nc.vector, 'pool': nc.gpsimd, 'sp': nc.sync}
        self.sems = {}
        for e in ['pe', 'act', 'dve', 'pool']:
            self.sems[e] = es.enter_context(nc.semaphore('c_' + e))
        self.nd = {'sp': 8, 'pool': 4, 'act': 4}
        for q, n in self.nd.items():
            for i in range(n):
                self.sems[('d', q, i)] = es.enter_context(nc.semaphore('d_%s%d' % (q, i)))
        self.cnt = {k: 0 for k in self.sems}
        self.dn = {q: 0 for q in self.nd}
        self.waited = {}
        self.lastw = {}
        self.readers = {}
        self.ninst = 0
        self.rec = None

    def _deps(self, e, reads, writes):
        toks = {}

        def add(t):
            if t is None:
                return
            k, v = t
            if toks.get(k, 0) < v:
                toks[k] = v
        for r in reads:
            add(self.lastw.get(r))
            if isinstance(r, str) and r[:2] in ('ps', 'pb'):
                for k, v in self.readers.get(r, {}).items():
                    if k != e:
                        add((k, v))
        for w in writes:
            add(self.lastw.get(w))
            for k, v in self.readers.get(w, {}).items():
                add((k, v))
        for k, v in toks.items():
            if k == e and e == 'pe':
                continue
            if self.waited.get((e, k), 0) >= v:
                continue
            self.eng[e].wait_ge(self.sems[k], v)
            self.waited[(e, k)] = v
            self.ninst += 1

    def _record(self, tok, reads, writes):
        k, v = tok
        for r in reads:
            d = self.readers.setdefault(r, {})
            if d.get(k, 0) < v:
                d[k] = v
        for w in writes:
            self.lastw[w] = tok
            self.readers[w] = {}

    def record(self, f, *a):
        old = self.rec
        self.rec = []
        r = f(*a)
        if r is not None and hasattr(r, '__next__'):
            for _ in r:
                pass
        lst = self.rec
        self.rec = old
        return lst

    def emit(self, lists):
        lists = [l for l in lists if l]
        idx = [0] * len(lists)
        while True:
            best, bf = None, 2.0
            for i, l in enumerate(lists):
                if idx[i] < len(l):
                    fr = idx[i] / len(l)
                    if fr < bf:
                        best, bf = i, fr
            if best is None:
                break
            kind, e, fn, reads, writes, inc = lists[best][idx[best]]
            idx[best] += 1
            if kind == 'op':
                self.op(e, fn, reads, writes, inc)
            else:
                self.dma(e, fn, reads, writes)

    def op(self, e, fn, reads=(), writes=(), inc=True):
        if self.rec is not None:
            self.rec.append(('op', e, fn, tuple(reads), tuple(writes), inc))
            return None
        self._deps(e, reads, writes)
        ins = fn()
        self.ninst += 1
        if inc:
            self.cnt[e] += 1
            ins.then_inc(self.sems[e], 1)
            tok = (e, self.cnt[e])
        else:
            tok = (e, self.cnt[e] + 1)
        self._record(tok, reads, writes)
        return tok

    def dma(self, q, fn, reads=(), writes=()):
        if self.rec is not None:
            self.rec.append(('dma', q, fn, tuple(reads), tuple(writes), True))
            return None
        self._deps(q, reads, writes)
        i = self.dn[q] % self.nd[q]
        self.dn[q] += 1
        k = ('d', q, i)
        ins = fn()
        self.cnt[k] += 16
        ins.then_inc(self.sems[k], 16)
        self.ninst += 1
        tok = (k, self.cnt[k])
        self._record(tok, reads, writes)
        return tok

    def barrier(self):
        for e in ['pe', 'act', 'dve', 'pool', 'sp']:
            for k, v in self.cnt.items():
                if v > 0 and self.waited.get((e, k), 0) < v and not (k == e):
                    self.eng[e].wait_ge(self.sems[k], v)
                    self.waited[(e, k)] = v

    def finish(self):
        for k, v in self.cnt.items():
            if v > 0 and self.waited.get(('sp', k), 0) < v:
                self.nc.sync.wait_ge(self.sems[k], v)


def build():
    nc = bass.Bass("TRN2", target_bir_lowering=False)

    def din(name, shape):
        return nc.dram_tensor(name, list(shape), F32, kind="ExternalInput").ap()

    def dout(name, shape):
        return nc.dram_tensor(name, list(shape), F32, kind="ExternalOutput").ap()

    def dscr(name, shape, dt=BF16):
        return nc.dram_tensor(name, list(shape), dt, kind="Internal").ap()

    xp = din("xp", [NPRE + NMAIN, D])
    flag = din("flag", [1, 1])
    xs = din("xs", [NSEQ * 64, D])
    ck = din("ck", [NSEQ, 8, PAST, 64])
    cv = din("cv", [NSEQ, 8, PAST, 64])
    st_in = din("st", [NSEQ, 8, 64, 64])
    sh_in = din("sh", [NSEQ, NRW])
    consts = nc.dram_tensor("consts", [128, NCONST], BF16, kind="ExternalInput").ap()
    constsf = din("constsf", [128, 256])
    g1 = din("g1", [D]); w_in = din("w_in", [D, DIN]); mu = din("mu", [NRW])
    w0 = din("w0", [512]); w2 = din("w2", [64, 512]); a0 = din("a0", [512]); a2 = din("a2", [64, 512])
    g2 = din("g2", [160, 512]); k_k = din("k_k", [512]); k_a = din("k_a", [512]); r_k = din("r_k", [512])
    lnw = din("lnw", [512]); lnb = din("lnb", [512]); qg = din("qg", [64]); kg = din("kg", [64])
    wua = din("wua", [512, D]); wub = din("wub", [512, D]); wo = din("wo", [D, D]); gn2 = din("gn2", [D])
    wf1 = din("wf1", [D, DFF]); wf2 = din("wf2", [DFF, D])

    yp = dout("yp", [NMAIN, D]); kp = dout("kp", [8, NMAIN, 64]); vp = dout("vp", [8, NMAIN, 64])
    wkvp = dout("wkvp", [8, 64, 64]); shp = dout("shp", [NRW])
    ys = dout("ys", [NSEQ * 64, D]); ksn = dout("ksn", [NSEQ, 8, 64, 64]); vsn = dout("vsn", [NSEQ, 8, 64, 64])
    wkvs = dout("wkvs", [NSEQ, 8, 64, 64]); shs = dout("shs", [NSEQ, NRW])

    w_in_b = dscr("w_in_b", [D, DIN]); wua_b = dscr("wua_b", [512, D]); wub_b = dscr("wub_b", [512, D])
    wo_b = dscr("wo_b", [D, D]); wf1_b = dscr("wf1_b", [D, DFF]); wf2_b = dscr("wf2_b", [DFF, D])
    NK = NPRE + NMAIN
    kT_scr = dscr("kT_scr", [4, 128, NK]); v_scr = dscr("v_scr", [NK // 128, 128, 512])
    qT_scr = dscr("qT_scr", [4, 128, NMAIN])
    yaT_scr = dscr("yaT_scr", [4, 128, NMAIN + 256]); ybT_scr = dscr("ybT_scr", [4, 128, NMAIN + 256])
    kTs_scr = dscr("kTs_scr", [4, 128, 256]); vs_scr = dscr("vs_scr", [2, 128, 512]); qTs_scr = dscr("qTs_scr", [4, 128, 256])

    es = contextlib.ExitStack()
    with es:
        S = Sched(nc, es)
        V, A, P, T = nc.vector, nc.scalar, nc.gpsimd, nc.tensor

        def sb(name, shape, dt=F32):
            return es.enter_context(nc.sbuf_tensor(name, list(shape), dt))

        def pst(name, shape, dt=F32):
            return es.enter_context(nc.psum_tensor(name, list(shape), dt))

        pall = pst("pall", [128, 8, 512])
        ps = [pall[:, i, :] for i in range(8)]
        pb = [ps[6][:, :].bitcast(BF16), ps[7][:, :].bitcast(BF16)]
        pb5 = ps[5][:, :].bitcast(BF16)

        deferred = []

        def cast_rows(dst, src, rows, step, name):
            for r0 in range(0, rows, step):
                deferred.append(lambda r0=r0: S.dma('pool', lambda: P.dma_start(out=dst[r0:r0 + step, :], in_=src[r0:r0 + step, :]), writes=[name]))
        for (c0, c1) in [(0, NRW), (NRW, C_GATE), (C_GATE, DIN)]:
            for r0 in range(0, D, 256):
                S.dma('pool', lambda r0=r0, c0=c0, c1=c1: P.dma_start(out=w_in_b[r0:r0 + 256, c0:c1], in_=w_in[r0:r0 + 256, c0:c1]), writes=[('w_in_b', c0)])
        cast_rows(wua_b, wua, 512, 128, 'wua_b')
        cast_rows(wub_b, wub, 512, 128, 'wub_b')
        cast_rows(wo_b, wo, D, 128, 'wo_b')
        cast_rows(wf1_b, wf1, D, 128, 'wf1_b')
        cast_rows(wf2_b, wf2, DFF, 256, 'wf2_b')

        NC_A = 2818
        cst = sb("cst", [128, NC_A], BF16)
        cstf = sb("cstf", [128, 256])
        S.dma('sp', lambda: nc.sync.dma_start(out=cst[:], in_=consts[:, 0:NC_A]), writes=['cst'])
        S.dma('sp', lambda: nc.sync.dma_start(out=cstf[:], in_=constsf[:, :]), writes=['cst'])
        o = [0]

        def cslice(n):
            a = cst[:, o[0]:o[0] + n]
            o[0] += n
            return a
        ident_b = cslice(128); m_strict = cslice(512); m_incl = cslice(512); m_low = cslice(512); eyeT = cslice(512)
        scanmask = cslice(256); blockones_b = cslice(128); headsel_b = cslice(2)
        tri_b = cslice(128); upp_b = cslice(128)
        ident_f = cstf[:, 0:128]; blockones_f = cstf[:, 128:256]
        cb = sb("cb", [128, 256], BF16)
        triP_b = cb[:, 0:128]; uppP_b = cb[:, 128:256]
        tri_f = tri_b; upp_f = upp_b
        flag_t = sb("flag_t", [128, 1])
        S.dma('sp', lambda: nc.sync.dma_start(out=flag_t[:], in_=flag.partition_broadcast(128)[:, 0, :]), writes=['flag_t'])
        S.op('dve', lambda: V.tensor_scalar(out=triP_b, in0=tri_f, scalar1=flag_t[:, 0:1], scalar2=None, op0=ALU.mult),
             reads=['cst', 'flag_t'], writes=['cb'])
        S.op('dve', lambda: V.tensor_scalar(out=uppP_b, in0=upp_f, scalar1=flag_t[:, 0:1], scalar2=None, op0=ALU.mult),
             reads=['cst', 'flag_t'], writes=['cb'])

        g1_b = sb("g1_b", [128, D])
        S.dma('sp', lambda: nc.sync.dma_start(out=g1_b[:], in_=g1.partition_broadcast(128)), writes=['g1_b'])
        lnw_b = sb("lnw_b", [128, 512]); lnb_b = sb("lnb_b", [128, 512])
        S.dma('sp', lambda: nc.sync.dma_start(out=lnw_b[:], in_=lnw.partition_broadcast(128)), writes=['lnw_b'])
        S.dma('sp', lambda: nc.sync.dma_start(out=lnb_b[:], in_=lnb.partition_broadcast(128)), writes=['lnb_b'])
        qg_b = sb("qg_b", [128, 64]); kg_b = sb("kg_b", [128, 64])
        S.dma('sp', lambda: nc.sync.dma_start(out=qg_b[:], in_=qg.partition_broadcast(128)), writes=['qg_b'])
        S.dma('sp', lambda: nc.sync.dma_start(out=kg_b[:], in_=kg.partition_broadcast(128)), writes=['kg_b'])
        S.op('dve', lambda: V.tensor_scalar(out=qg_b[:], in0=qg_b[:], scalar1=0.125, scalar2=None, op0=ALU.mult),
             reads=['qg_b'], writes=['qg_b'])
        mu_t = sb("mu_t", [128, 15])
        S.op('pool', lambda: P.memset(mu_t[:], 0.0), writes=['mu_t'])
        for cc in range(15):
            n = 128 if cc < 14 else 32
            S.dma('sp', lambda cc=cc, n=n: nc.sync.dma_start(out=mu_t[0:n, cc:cc + 1],
                  in_=mu[cc * 128:cc * 128 + n].rearrange("(p o) -> p o", o=1)), writes=['mu_t'])
        vec = sb("vec", [128, 8, 4])
        for i, src in enumerate([w0, a0, k_k, k_a, r_k]):
            for j in range(4):
                S.dma('sp', lambda i=i, j=j, src=src: nc.sync.dma_start(out=vec[:, i, j:j + 1],
                      in_=src[j * 128:(j + 1) * 128].rearrange("(p o) -> p o", o=1)), writes=['vec'])
        S.op('dve', lambda: V.tensor_scalar(out=vec[:, 5, :], in0=vec[:, 0, :], scalar1=-1.0, scalar2=None, op0=ALU.mult),
             reads=['vec'], writes=['vec'])
        S.op('dve', lambda: V.tensor_scalar(out=vec[:, 6, :], in0=vec[:, 3, :], scalar1=-1.0, scalar2=1.0, op0=ALU.mult, op1=ALU.add),
             reads=['vec'], writes=['vec'])
        S.op('dve', lambda: V.tensor_scalar(out=vec[:, 7, :], in0=vec[:, 1, :], scalar1=-1.0, scalar2=None, op0=ALU.mult),
             reads=['vec'], writes=['vec'])
        xt = [sb("xt%d" % i, [128, D]) for i in range(2)]
        wtmp = xt[0][:, :].rearrange("p (a c) -> p a c", a=2)
        w2a2 = sb("w2a2", [128, 512], BF16); g2_t = sb("g2_t", [128, 2, 512], BF16)
        S.dma('sp', lambda: nc.sync.dma_start(out=wtmp[0:64, 0, :], in_=w2[:, :]), writes=[('xt', 0)])
        S.dma('sp', lambda: nc.sync.dma_start(out=wtmp[64:128, 0, :], in_=a2[:, :]), writes=[('xt', 0)])
        S.op('dve', lambda: V.tensor_copy(out=w2a2[:], in_=wtmp[:, 0, :]), reads=[('xt', 0)], writes=['w2a2'])
        S.dma('sp', lambda: nc.sync.dma_start(out=wtmp[:, 0, :], in_=g2[0:128, :]), writes=[('xt', 0)])
        S.dma('sp', lambda: nc.sync.dma_start(out=wtmp[0:32, 1, :], in_=g2[128:160, :]), writes=[('xt', 0)])
        S.op('pool', lambda: P.memset(g2_t[:], 0.0), writes=['g2_t'])
        S.op('dve', lambda: V.tensor_copy(out=g2_t[:, 0, :], in_=wtmp[:, 0, :]), reads=[('xt', 0)], writes=['g2_t'])
        S.op('dve', lambda: V.tensor_copy(out=g2_t[0:32, 1, :], in_=wtmp[0:32, 1, :]), reads=[('xt', 0)], writes=['g2_t'])

        def rstd_from_ss(out_ap, ss_ap, scale, eps, eng_res_r, eng_res_w):
            S.op('act', lambda: A.activation(out=out_ap, in_=ss_ap, func=AF.Ln, bias=eps_t[:, 0:1] if eps == 'rms' else eps_g[:, 0:1], scale=scale),
                 reads=eng_res_r, writes=eng_res_w)
            S.op('act', lambda: A.activation(out=out_ap, in_=out_ap, func=AF.Exp, scale=-0.5), reads=eng_res_w, writes=eng_res_w)

        eps_t = sb("eps_t", [128, 1]); eps_g = sb("eps_g", [128, 1]); one_t = sb("one_t", [128, 1]); mhalf_t = sb("mhalf_t", [128, 1])
        S.op('pool', lambda: P.memset(eps_t[:], 1e-6), writes=['eps_t'])
        S.op('pool', lambda: P.memset(eps_g[:], 64e-5), writes=['eps_g'])
        S.op('pool', lambda: P.memset(one_t[:], 1.0), writes=['one_t'])
        S.op('pool', lambda: P.memset(mhalf_t[:], -0.5), writes=['mhalf_t'])

        xsq = sb("xsq", [128, D], BF16)
        xn_b = sb("xn_b", [128, D], BF16)
        ssq = sb("ssq", [128, 4])
        xcount = [0]

        def load_norm_transpose(x_rows, xnT_dst, g_b, keep_x=False):
            bi = xcount[0] % 2
            xcount[0] += 1
            xb = xt[bi]
            S.dma('sp', lambda: nc.sync.dma_start(out=xb[:], in_=x_rows), writes=[('xt', bi)])
            S.op('act', lambda: A.activation(out=xsq[:], in_=xb[:], func=AF.Square, accum_out=ssq[:, 0:1]),
                 reads=[('xt', bi)], writes=['xsq', 'ssq'])
            S.op('act', lambda: A.activation(out=ssq[:, 1:2], in_=ssq[:, 0:1], func=AF.Ln, bias=eps_t[:, 0:1], scale=1.0 / D),
                 reads=['ssq', 'eps_t'], writes=['ssq'])
            S.op('act', lambda: A.activation(out=ssq[:, 2:3], in_=ssq[:, 1:2], func=AF.Exp, scale=-0.5), reads=['ssq'], writes=['ssq'])
            S.op('dve', lambda: V.scalar_tensor_tensor(out=xn_b[:], in0=xb[:], scalar=ssq[:, 2:3], in1=g_b[:], op0=ALU.mult, op1=ALU.mult),
                 reads=[('xt', bi), 'ssq', 'g1_b', 'gn2_b'], writes=['xn_b'])
            for kc in range(8):
                S.op('pe', lambda kc=kc: T.transpose(out=pb[0][:, kc * 128:(kc + 1) * 128], in_=xn_b[:, kc * 128:(kc + 1) * 128], identity=ident_b),
                     reads=['xn_b', 'cb'], writes=['ps6'])
            S.op('act', lambda: A.copy(out=xnT_dst, in_=pb[0][:, :].rearrange("p (k t) -> p k t", k=8)),
                 reads=['ps6'], writes=['xnT'])
            return bi

        ph1 = contextlib.ExitStack()
        with ph1:
            def sb1(name, shape, dt=F32):
                return ph1.enter_context(nc.sbuf_tensor(name, list(shape), dt))
            win_sb = sb1("win_sb", [128, 8, C_GATE], BF16)
            for kc in range(8):
                S.dma('sp', lambda kc=kc: nc.sync.dma_start(out=win_sb[:, kc, :], in_=w_in_b[kc * 128:(kc + 1) * 128, 0:C_GATE]),
                      reads=[('w_in_b', 0), ('w_in_b', NRW)], writes=['win_sb'])
            xnT = sb1("xnT", [128, 8, NT], BF16)
            pm = sb1("pm", [128, 15, NT])
            ptmp = sb1("ptmp", [128, NT]); dtmp = sb1("dtmp", [128, NT])
            pprev0 = sb1("pprev0", [128, 15, 4])
            carry = sb1("carry", [128, 15]); plast = sb1("plast", [128, 15, 4])
            S.op('pool', lambda: P.memset(pm[:], 0.0), writes=['pm'])
            S.op('pool', lambda: P.memset(carry[:], 0.0), writes=['carry'])
            S.op('pool', lambda: P.memset(plast[:], 0.0), writes=['plast'])
            sq_t = sb1("sq_t", [128, 512]); t_t = sb1("t_t", [128, 512]); kn_f = sb1("kn_f", [128, 512]); v_f = sb1("v_f", [128, 512])
            kn_bt = sb1("kn_bt", [128, 512], BF16); qn_bt = sb1("qn_bt", [128, 512], BF16); v_bt = sb1("v_bt", [128, 512], BF16)
            ss8 = sb1("ss8", [128, 4, 8])
            kTst = sb1("kTst", [128, 4, 128], BF16); qTst = sb1("qTst", [128, 4, 128], BF16)
            f_e = sb1("f_e", [128, NT]); f_cl = sb1("f_cl", [128, NT]); f_a = sb1("f_a", [128, NT])
            f_kk = sb1("f_kk", [128, NT]); f_k2 = sb1("f_k2", [128, NT]); f_t1 = sb1("f_t1", [128, NT]); f_t2 = sb1("f_t2", [128, NT])
            f_gh = sb1("f_gh", [128, NT]); f_gi = sb1("f_gi", [128, NT])
            FT = [[f_e, f_cl, f_a, f_kk, f_k2, f_t1, f_t2, f_gh, f_gi], [sb1("ft1_%d" % i, [128, NT]) for i in range(9)]]
            tw_b = sb1("tw_b", [128, NT], BF16)
            sgl_b = sb1("sgl_b", [128, 2, NT], BF16)
            aT = sb1("aT", [128, 4, NT], BF16); bT = sb1("bT", [128, 4, NT], BF16); kT_ = sb1("kT_", [128, 4, NT], BF16); rT = sb1("rT", [128, 4, NT], BF16)
            bh_f = sb1("bh_f", [128, 4, NT], BF16); kh_f = sb1("kh_f", [128, 4, NT], BF16); v_fb = sb1("v_fb", [128, 4, NT], BF16)
            rk_b = sb1("rk_b", [128, 4, NT], BF16)
            gC = sb1("gC", [128, 4, 4])
            Atok = sb1("Atok", [128, 2, 512], BF16); Bhat = sb1("Bhat", [128, 2, 512], BF16); Khat = sb1("Khat", [128, 2, 512], BF16); Vtok = sb1("Vtok", [128, 2, 512], BF16)
            CT = []
            for ci_ in range(2):
                X = {}
                for nm in ['AakT_s', 'ArbT_s', 'ArkT_s', 'TT_b', 'Ah_s', 'AhT_s', 'W1_s', 'Uv_b', 'GT_s']:
                    X[nm] = sb1("%s%d" % (nm, ci_), [128, 512], BF16)
                X['Pm'] = [sb1("Pm%d_%d" % (i, ci_), [128, 512], BF16) for i in range(2)]
                X['Qm'] = [sb1("Qm%d_%d" % (i, ci_), [128, 512], BF16) for i in range(2)]
                for nm in ['TT_f', 'Uv_f', 'H_s']:
                    X[nm] = sb1("%s%d" % (nm, ci_), [128, 512])
                CT.append(X)
            U_b = sb1("U_b", [128, 512], BF16)
            S_f = sb1("S_f", [128, 4, 64]); S_b = sb1("S_b", [128, 4, 64], BF16); S_t = sb1("S_t", [128, 4, 64])
            st_ld = sq_t[0:64, :].rearrange("p (h k) -> p h k", h=8); st_o = t_t[0:64, :].rearrange("p (h k) -> p h k", h=8)
            S.op('pool', lambda: P.memset(S_f[:], 0.0), writes=['S_f'])
            S.op('pool', lambda: P.memset(S_b[:], 0.0), writes=['S_b'])
            S.op('pool', lambda: P.memset(sgl_b[:], 0.0), writes=['sgl_b'])
            o_sq = sq_t; o_n = sb1("o_n", [128, 512]); o_t = t_t; OSB = [v_f, kn_f]; OSBN = ['v_f', 'kn_f']
            st8 = sb1("st8", [128, 6, 8]); ya_b = sb1("ya_b", [128, 512], BF16); yaT_st = sb1("yaT_st", [128, 4, 128], BF16)
            shst = sb1("shst", [128, 15]); shst4 = sb1("shst4", [128, 15, 4])

            def sb_part(tt, tok0, main, kout, vout, kT_dst, v_dst, qT_dst):
                xl = xnT[:, :, tt * 128:(tt + 1) * 128]
                banks = {'k': 6, 'v': 7, 'q': 6}
                colb = {'q': C_SBQ, 'k': C_SBK, 'v': C_SBV}

                def proj(nm):
                    bk = banks[nm]
                    for kc in range(8):
                        S.op('pe', lambda kc=kc, bk=bk, nm=nm: T.matmul(ps[bk][:, :], lhsT=xnT[:, kc, tt * 128:(tt + 1) * 128],
                             rhs=win_sb[:, kc, colb[nm]:colb[nm] + 512], start=(kc == 0), stop=(kc == 7)),
                             reads=['xnT', 'win_sb'], writes=['ps%d' % bk])
                proj('v')
                S.op('act', lambda: A.copy(out=v_f[:], in_=ps[7][:, :]), reads=['ps7'], writes=['v_f'])
                S.op('pool', lambda: P.tensor_copy(out=v_bt[:], in_=v_f[:]), reads=['v_f'], writes=['v_bt'])
                for (p0, p1, dst_) in (vout or []):
                    S.dma('sp', lambda p0=p0, p1=p1, dst_=dst_: nc.sync.dma_start(out=dst_, in_=v_f[p0:p1, :].rearrange("p (h d) -> p h d", h=8)), reads=['v_f'], writes=['out_v'])
                S.dma('sp', lambda: nc.sync.dma_start(out=v_dst, in_=v_bt[:]), reads=['v_bt'], writes=['v_scr'])
                for nm in (['k', 'q'] if main else ['k']):
                    bk = banks[nm]
                    proj(nm)
                    gb = kg_b if nm == 'k' else qg_b
                    dstb = kn_bt if nm == 'k' else qn_bt
                    si = 0 if nm == 'k' else 2
                    S.op('act', lambda bk=bk: A.activation(out=sq_t[:], in_=ps[bk][:, :], func=AF.Square), reads=['ps%d' % bk], writes=['sq_t'])
                    S.op('dve', lambda si=si: V.tensor_reduce(out=ss8[:, si, :], in_=sq_t[:].rearrange("p (h d) -> p h d", h=8), axis=AX.X, op=ALU.add),
                         reads=['sq_t'], writes=['ss8'])
                    S.op('act', lambda si=si: A.activation(out=ss8[:, si + 1, :], in_=ss8[:, si, :], func=AF.Ln, bias=eps_t[:, 0:1], scale=1.0 / 64),
                         reads=['ss8', 'eps_t'], writes=['ss8'])
                    S.op('act', lambda si=si: A.activation(out=ss8[:, si + 1, :], in_=ss8[:, si + 1, :], func=AF.Exp, scale=-0.5), reads=['ss8'], writes=['ss8'])
                    S.op('dve', lambda bk=bk, gb=gb: V.tensor_tensor(out=t_t[:].rearrange("p (h d) -> p h d", h=8), in0=ps[bk][:, :].rearrange("p (h d) -> p h d", h=8), in1=gb[:].unsqueeze(1).broadcast_to([128, 8, 64]), op=ALU.mult),
                         reads=['ps%d' % bk, 'kg_b', 'qg_b'], writes=['t_t'])
                    if nm == 'k':
                        S.op('dve', lambda si=si: V.tensor_tensor(out=kn_f[:].rearrange("p (h d) -> p h d", h=8), in0=t_t[:].rearrange("p (h d) -> p h d", h=8),
                             in1=ss8[:, si + 1, :].unsqueeze(2).broadcast_to([128, 8, 64]), op=ALU.mult), reads=['t_t', 'ss8'], writes=['kn_f'])
                        S.op('pool', lambda: P.tensor_copy(out=kn_bt[:], in_=kn_f[:]), reads=['kn_f'], writes=['kn_bt'])
                        for (p0, p1, dst_) in (kout or []):
                            S.dma('sp', lambda p0=p0, p1=p1, dst_=dst_: nc.sync.dma_start(out=dst_, in_=kn_f[p0:p1, :].rearrange("p (h d) -> p h d", h=8)), reads=['kn_f'], writes=['out_k'])
                    else:
                        S.op('dve', lambda si=si: V.tensor_tensor(out=qn_bt[:].rearrange("p (h d) -> p h d", h=8), in0=t_t[:].rearrange("p (h d) -> p h d", h=8),
                             in1=ss8[:, si + 1, :].unsqueeze(2).broadcast_to([128, 8, 64]), op=ALU.mult), reads=['t_t', 'ss8'], writes=['qn_bt'])
                    stg = kTst if nm == 'k' else qTst
                    stn = 'kTst' if nm == 'k' else 'qTst'
                    srcn = 'kn_bt' if nm == 'k' else 'qn_bt'
                    for pr in range(4):
                        S.op('pe', lambda pr=pr, dstb=dstb: T.transpose(out=pb[1][:, pr * 128:(pr + 1) * 128], in_=dstb[:, pr * 128:(pr + 1) * 128], identity=ident_b),
                             reads=[srcn, 'cb'], writes=['ps7'])
                    S.op('act', lambda stg=stg: A.copy(out=stg[:], in_=pb[1][:, 0:512].rearrange("p (k t) -> p k t", k=4)), reads=['ps7'], writes=[stn])
                    dst = kT_dst if nm == 'k' else qT_dst
                    S.dma('sp', lambda stg=stg, dst=dst: nc.sync.dma_start(out=dst, in_=stg[:]), reads=[stn], writes=['kq_scr'])

            def rwkv_proj(main, seq_starts, shift_srcs, allcols=False):
                ccs = list(range(15)) if (main or allcols) else [4, 5, 6, 7, 8, 9, 10, 11, 12]
                for ci in range(4):
                    if seq_starts[ci]:
                        if shift_srcs[ci] is None:
                            S.op('pool', lambda ci=ci: P.memset(pprev0[:, :, ci:ci + 1], 0.0), writes=['pprev0'])
                        else:
                            for cc in range(15):
                                n = 128 if cc < 14 else 32
                                S.dma('sp', lambda cc=cc, n=n, ci=ci: nc.sync.dma_start(out=pprev0[0:n, cc, ci:ci + 1],
                                      in_=shift_srcs[ci][cc * 128:cc * 128 + n].rearrange("(p o) -> p o", o=1)), writes=['pprev0'])
                    elif ci == 0:
                        S.op('pool', lambda: P.tensor_copy(out=pprev0[:, :, 0], in_=carry[:]), reads=['carry'], writes=['pprev0'])
                for cc in ccs:
                    n = 128 if cc < 14 else 32
                    bk = 6 + (cc % 2)
                    for kc in range(8):
                        S.op('pe', lambda kc=kc, cc=cc, n=n, bk=bk: T.matmul(ps[bk][0:n, 0:NT], lhsT=win_sb[:, kc, cc * 128:cc * 128 + n],
                             rhs=xnT[:, kc, :], start=(kc == 0), stop=(kc == 7)), reads=['xnT', 'win_sb'], writes=['ps%d' % bk])
                    S.op('act', lambda n=n, bk=bk: A.copy(out=ptmp[0:n, :], in_=ps[bk][0:n, 0:NT]), reads=['ps%d' % bk], writes=['ptmp'])
                    p3 = ptmp[0:n, :].rearrange("p (c t) -> p c t", c=4)
                    d3 = dtmp[0:n, :].rearrange("p (c t) -> p c t", c=4)
                    for ci in range(1, 4):
                        if not seq_starts[ci]:
                            S.op('dve', lambda ci=ci, cc=cc, n=n: V.tensor_copy(out=pprev0[0:n, cc, ci:ci + 1], in_=ptmp[0:n, ci * 64 - 1:ci * 64]),
                                 reads=['ptmp'], writes=['pprev0'])
                    S.op('dve', lambda n=n, cc=cc: V.tensor_copy(out=carry[0:n, cc:cc + 1], in_=ptmp[0:n, NT - 1:NT]), reads=['ptmp'], writes=['carry'])
                    S.op('dve', lambda n=n, cc=cc, p3=p3: V.tensor_copy(out=plast[0:n, cc, :], in_=p3[:, :, 63]), reads=['ptmp'], writes=['plast'])
                    S.op('dve', lambda p3=p3, d3=d3: V.tensor_tensor(out=d3[:, :, 1:64], in0=p3[:, :, 0:63], in1=p3[:, :, 1:64], op=ALU.subtract),
                         reads=['ptmp'], writes=['dtmp'])
                    S.op('dve', lambda p3=p3, d3=d3, n=n, cc=cc: V.tensor_tensor(out=d3[:, :, 0], in0=pprev0[0:n, cc, :], in1=p3[:, :, 0], op=ALU.subtract),
                         reads=['ptmp', 'pprev0'], writes=['dtmp'])
                    S.op('dve', lambda n=n, cc=cc: V.scalar_tensor_tensor(out=pm[0:n, cc, :], in0=dtmp[0:n, :], scalar=mu_t[0:n, cc:cc + 1], in1=ptmp[0:n, :],
                         op0=ALU.mult, op1=ALU.add), reads=['dtmp', 'ptmp', 'mu_t'], writes=['pm'])

            def prep_pre(main):
                S.op('act', lambda: A.activation(out=f_e[0:64, :], in_=pm[0:64, 12, :], func=AF.Exp, scale=-2.0), reads=['pm'], writes=['f_e#0'])
                S.op('dve', lambda: V.tensor_scalar(out=f_e[0:64, :], in0=f_e[0:64, :], scalar1=1.0, scalar2=None, op0=ALU.add), reads=['f_e#0'], writes=['f_e#0'])
                S.op('dve', lambda: V.reciprocal(out=f_e[0:64, :], in_=f_e[0:64, :]), reads=['f_e#0'], writes=['f_e#0'])
                S.op('dve', lambda: V.tensor_scalar(out=tw_b[0:64, :], in0=f_e[0:64, :], scalar1=2.0, scalar2=-1.0, op0=ALU.mult, op1=ALU.add), reads=['f_e#0'], writes=['tw_b'])
                S.op('pool', lambda: P.tensor_copy(out=tw_b[64:128, :], in_=pm[64:128, 12, :]), reads=['pm'], writes=['tw_b'])
                if main:
                    for (np_, ci_, tmp_, tn_) in ((128, 13, f_t1, 'f_t1#0'), (32, 14, f_t2, 'f_t2#0')):
                        S.op('act', lambda np_=np_, ci_=ci_, tmp_=tmp_: A.activation(out=tmp_[0:np_, :], in_=pm[0:np_, ci_, :], func=AF.Exp, scale=-1.0), reads=['pm'], writes=[tn_])
                        S.op('act', lambda np_=np_, tmp_=tmp_: A.activation(out=tmp_[0:np_, :], in_=tmp_[0:np_, :], func=AF.Ln, bias=one_t[0:np_, 0:1]), reads=[tn_, 'one_t'], writes=[tn_])
                        S.op('act', lambda np_=np_, ci_=ci_, tmp_=tmp_: A.activation(out=sgl_b[0:np_, ci_ - 13, :], in_=tmp_[0:np_, :], func=AF.Exp, scale=-1.0), reads=[tn_], writes=['sgl_b'])

            def prep_pair(main, j, ti, BK):
                f_e, f_cl, f_a, f_kk, f_k2, f_t1, f_t2, f_gh, f_gi = FT[ti]
                f_lw = f_e
                tx = '#%d' % ti
                rj, kj, vj = pm[:, j, :], pm[:, 4 + j, :], pm[:, 8 + j, :]
                cs = slice(j * 128, (j + 1) * 128)
                S.op('pe', lambda cs=cs: T.matmul(ps[BK[0]][:, 0:NT], lhsT=w2a2[0:64, cs], rhs=tw_b[0:64, :], start=True, stop=True),
                     reads=['w2a2', 'tw_b'], writes=['ps%d' % BK[0]])
                S.op('pe', lambda cs=cs: T.matmul(ps[BK[1]][:, 0:NT], lhsT=w2a2[64:128, cs], rhs=tw_b[64:128, :], start=True, stop=True),
                     reads=['w2a2', 'tw_b'], writes=['ps%d' % BK[1]])
                S.op('act', lambda j=j: A.activation(out=f_e[:], in_=ps[BK[0]][:, 0:NT], func=AF.Exp, bias=vec[:, 5, j:j + 1], scale=-1.0),
                     reads=['ps%d' % BK[0], 'vec'], writes=['f_e' + tx])
                S.op('act', lambda: A.activation(out=f_e[:], in_=f_e[:], func=AF.Ln, bias=one_t[:, 0:1]), reads=['f_e' + tx, 'one_t'], writes=['f_e' + tx])
                S.op('act', lambda: A.activation(out=f_e[:], in_=f_e[:], func=AF.Exp, bias=mhalf_t[:, 0:1], scale=-1.0), reads=['f_e' + tx, 'mhalf_t'], writes=['f_e' + tx])
                S.op('dve', lambda: V.tensor_scalar(out=f_lw[:], in0=f_e[:], scalar1=-1.0, scalar2=None, op0=ALU.mult), reads=['f_e' + tx], writes=['f_e' + tx])
                S.op('dve', lambda: V.tensor_tensor_scan(out=f_cl[:], data0=scanmask, data1=f_lw[:], initial=0.0, op0=ALU.mult, op1=ALU.add),
                     reads=['f_e' + tx, 'cst'], writes=['f_cl' + tx])
                S.op('act', lambda j=j: A.activation(out=f_a[:], in_=ps[BK[1]][:, 0:NT], func=AF.Exp, bias=vec[:, 7, j:j + 1], scale=-1.0),
                     reads=['ps%d' % BK[1], 'vec'], writes=['f_a' + tx])
                S.op('act', lambda: A.activation(out=f_a[:], in_=f_a[:], func=AF.Ln, bias=one_t[:, 0:1]), reads=['f_a' + tx, 'one_t'], writes=['f_a' + tx])
                S.op('act', lambda: A.activation(out=f_a[:], in_=f_a[:], func=AF.Exp, scale=-1.0), reads=['f_a' + tx], writes=['f_a' + tx])
                S.op('dve', lambda j=j, kj=kj: V.tensor_scalar(out=f_kk[:], in0=kj, scalar1=vec[:, 2, j:j + 1], scalar2=None, op0=ALU.mult),
                     reads=['pm', 'vec'], writes=['f_kk' + tx])
                S.op('pool', lambda: P.tensor_tensor(out=f_t1[:], in0=f_kk[:], in1=f_kk[:], op=ALU.mult), reads=['f_kk' + tx], writes=['f_t1' + tx])
                S.op('pe', lambda: T.matmul(ps[BK[2]][:, 0:NT], lhsT=blockones_f, rhs=f_t1[:], start=True, stop=True), reads=['cst', 'f_t1' + tx], writes=['ps%d' % BK[2]])
                S.op('dve', lambda: V.tensor_scalar(out=f_t2[:], in0=ps[BK[2]][:, 0:NT], scalar1=1e-18, scalar2=None, op0=ALU.max), reads=['ps%d' % BK[2]], writes=['f_t2' + tx])
                S.op('act', lambda: A.activation(out=f_t2[:], in_=f_t2[:], func=AF.Ln), reads=['f_t2' + tx], writes=['f_t2' + tx])
                S.op('act', lambda: A.activation(out=f_t2[:], in_=f_t2[:], func=AF.Exp, scale=-0.5), reads=['f_t2' + tx], writes=['f_t2' + tx])
                S.op('dve', lambda: V.tensor_tensor(out=f_kk[:], in0=f_kk[:], in1=f_t2[:], op=ALU.mult), reads=['f_kk' + tx, 'f_t2' + tx], writes=['f_kk' + tx])
                S.op('dve', lambda j=j: V.tensor_scalar(out=f_t1[:], in0=f_a[:], scalar1=vec[:, 3, j:j + 1], scalar2=vec[:, 6, j:j + 1], op0=ALU.mult, op1=ALU.add),
                     reads=['f_a' + tx, 'vec'], writes=['f_t1' + tx])
                S.op('dve', lambda kj=kj: V.tensor_tensor(out=f_k2[:], in0=kj, in1=f_t1[:], op=ALU.mult), reads=['pm', 'f_t1' + tx], writes=['f_k2' + tx])
                cl3 = f_cl[:].rearrange("p (c t) -> p c t", c=4)
                S.op('act', lambda j=j, cl3=cl3: A.activation(out=gC[:, j, :], in_=cl3[:, :, 63], func=AF.Exp), reads=['f_cl' + tx], writes=['gC'])
                S.op('dve', lambda cl3=cl3: V.tensor_tensor(out=f_gh[:].rearrange("p (c t) -> p c t", c=4), in0=cl3[:, :, 63:64].broadcast_to([128, 4, 64]), in1=cl3,
                     op=ALU.subtract), reads=['f_cl' + tx], writes=['f_gh' + tx])
                S.op('act', lambda: A.activation(out=f_gh[:], in_=f_gh[:], func=AF.Exp), reads=['f_gh' + tx], writes=['f_gh' + tx])
                S.op('act', lambda: A.activation(out=f_gi[:], in_=f_cl[:], func=AF.Exp, scale=-1.0), reads=['f_cl' + tx], writes=['f_gi' + tx])
                S.op('pool', lambda: P.tensor_tensor(out=f_t1[:], in0=f_kk[:], in1=f_a[:], op=ALU.mult), reads=['f_kk' + tx, 'f_a' + tx], writes=['f_t1' + tx])
                S.op('dve', lambda j=j: V.tensor_tensor(out=bT[:, j, :], in0=f_t1[:], in1=f_gi[:], op=ALU.mult), reads=['f_t1' + tx, 'f_gi' + tx], writes=['bT'])
                S.op('pool', lambda j=j: P.tensor_tensor(out=bh_f[:, j, :], in0=f_t1[:], in1=f_gh[:], op=ALU.mult), reads=['f_t1' + tx, 'f_gh' + tx], writes=['bh_f'])
                S.op('dve', lambda j=j: V.tensor_tensor(out=kT_[:, j, :], in0=f_k2[:], in1=f_gi[:], op=ALU.mult), reads=['f_k2' + tx, 'f_gi' + tx], writes=['kT_'])
                S.op('pool', lambda j=j: P.tensor_tensor(out=kh_f[:, j, :], in0=f_k2[:], in1=f_gh[:], op=ALU.mult), reads=['f_k2' + tx, 'f_gh' + tx], writes=['kh_f'])
                S.op('pool', lambda j=j, vj=vj: P.tensor_copy(out=v_fb[:, j, :], in_=vj), reads=['pm'], writes=['v_fb'])
                S.op('dve', lambda: V.tensor_tensor(out=f_t2[:], in0=f_cl[:], in1=f_lw[:], op=ALU.subtract), reads=['f_cl' + tx, 'f_e' + tx], writes=['f_t2' + tx])
                S.op('act', lambda: A.activation(out=f_t2[:], in_=f_t2[:], func=AF.Exp), reads=['f_t2' + tx], writes=['f_t2' + tx])
                S.op('dve', lambda j=j: V.scalar_tensor_tensor(out=aT[:, j, :], in0=f_kk[:], scalar=-1.0, in1=f_t2[:], op0=ALU.mult, op1=ALU.mult),
                     reads=['f_kk' + tx, 'f_t2' + tx], writes=['aT'])
                if main:
                    S.op('act', lambda: A.activation(out=f_gi[:], in_=f_cl[:], func=AF.Exp), reads=['f_cl' + tx], writes=['f_gi' + tx])
                    S.op('dve', lambda j=j, rj=rj: V.tensor_tensor(out=rT[:, j, :], in0=rj, in1=f_gi[:], op=ALU.mult), reads=['pm', 'f_gi' + tx], writes=['rT'])
                    S.op('dve', lambda j=j, rj=rj: V.scalar_tensor_tensor(out=rk_b[:, j, :], in0=rj, scalar=vec[:, 4, j:j + 1], in1=f_k2[:], op0=ALU.mult, op1=ALU.mult),
                         reads=['pm', 'vec', 'f_k2' + tx], writes=['rk_b'])

            def prep_T():
                for tt in range(2):
                    for (src, srcn, dst, dstn) in [(aT, 'aT', Atok, 'Atok'), (bh_f, 'bh_f', Bhat, 'Bhat'), (kh_f, 'kh_f', Khat, 'Khat'), (v_fb, 'v_fb', Vtok, 'Vtok')]:
                        for j in range(4):
                            S.op('pe', lambda j=j, src=src, tt=tt: T.transpose(out=pb5[:, j * 128:(j + 1) * 128], in_=src[:, j, tt * 128:(tt + 1) * 128], identity=ident_b),
                                 reads=[srcn, 'cb'], writes=['ps5'])
                        S.op('act', lambda dst=dst, tt=tt: A.copy(out=dst[:, tt, :], in_=pb5[:, 0:512]), reads=['ps5'], writes=[dstn])


            def prep_pairs(main, js, ti, BK):
                for j in js:
                    prep_pair(main, j, ti, BK)

            def blk_tok(t_, c, h):
                return t_[c * 64:(c + 1) * 64, h * 64:(h + 1) * 64]

            def blk_feat(t_, c, h):
                hb = (h % 2) * 64
                col = ((h // 2) * 2 + c) * 64
                return t_[hb:hb + 64, col:col + 64]

            def fm(t_, h, tt, c):
                hb = (h % 2) * 64
                t0 = tt * 128 + c * 64
                return t_[hb:hb + 64, h // 2, t0:t0 + 64]

            def tm(t_, tt, c, h):
                return t_[c * 64:(c + 1) * 64, tt, h * 64:(h + 1) * 64]

            def sv(ap, kind, i):
                if kind in ('tc', 'fh'):
                    return ap[i * 64:(i + 1) * 64, :]
                return ap.rearrange("p (a two t) -> p a two t", two=2, t=64)[:, :, i, :]

            def step16(banks, fnl, reads):
                for c in range(2):
                    for h in range(8):
                        lst = fnl(c, h)
                        n = len(lst)
                        for idx, (of_, l_, r_, rb) in enumerate(lst):
                            bk = banks[rb // 64]
                            S.op('pe', lambda of_=of_, l_=l_, r_=r_, idx=idx, n=n, bk=bk: T.matmul(of_(ps[bk]), lhsT=l_, rhs=r_, start=(idx == 0), stop=(idx == n - 1)),
                                 reads=reads, writes=['ps%d' % bk], inc=(h >= 6 and idx == n - 1))

            def ev(eng, kind, banks, fn, reads, writes):
                for i in range(2):
                    bk = banks[i]
                    S.op(eng, lambda i=i, bk=bk: fn(lambda ap: sv(ap, kind, i), ps[bk]), reads=reads + ['ps%d' % bk], writes=writes)

            def hb_(h):
                return (h % 2) * 64

            def chunk_math(tt, main):
                X = CT[tt]
                Pm_, Qm_, TT_f, TT_b, AakT_s, ArbT_s, ArkT_s = X['Pm'], X['Qm'], X['TT_f'], X['TT_b'], X['AakT_s'], X['ArbT_s'], X['ArkT_s']
                Ah_s, AhT_s, W1_s, Uv_f, Uv_b, GT_s, H_s = X['Ah_s'], X['AhT_s'], X['W1_s'], X['Uv_f'], X['Uv_b'], X['GT_s'], X['H_s']
                P0, P1 = ((0, 1), (2, 3)) if tt == 0 else ((4, 5), (6, 7))
                P2 = P0
                sfx = '_%d' % tt
                yield
                step16(P0, lambda c, h: [(lambda b, c=c, h=h: blk_tok(b, c, h), fm(bT, h, tt, c), fm(aT, h, tt, c), hb_(h))], ['bT', 'aT'])
                yield
                ev('dve', 'th', P0, lambda v, b: V.tensor_tensor(out=v(Pm_[0][:]), in0=v(b[:, :]), in1=v(m_strict), op=ALU.mult), ['cst'], ['Pm0' + sfx])
                yield
                ev('dve', 'th', P0, lambda v, b: V.tensor_tensor(out=v(TT_f[:]), in0=v(b[:, :]), in1=v(m_strict), op=ALU.mult), ['cst'], ['TT_f' + sfx])
                yield
                step16(P1, lambda c, h: [(lambda b, c=c, h=h: blk_tok(b, c, h), fm(kT_, h, tt, c), fm(aT, h, tt, c), hb_(h))], ['kT_', 'aT'])
                yield
                ev('dve', 'th', P1, lambda v, b: V.tensor_tensor(out=v(AakT_s[:]), in0=v(b[:, :]), in1=v(m_strict), op=ALU.mult), ['cst'], ['AakT_s' + sfx])
                yield
                step16(P2, lambda c, h: [(lambda b, c=c, h=h: blk_tok(b, c, h), fm(aT, h, tt, c), fm(bT, h, tt, c), hb_(h))], ['bT', 'aT'])
                yield
                ev('dve', 'th', P2, lambda v, b: V.tensor_tensor(out=v(Qm_[0][:]), in0=v(b[:, :]), in1=v(m_low), op=ALU.mult), ['cst'], ['Qm0' + sfx])
                yield
                S.op('pool', lambda: P.tensor_tensor(out=TT_f[:], in0=TT_f[:], in1=eyeT, op=ALU.add), reads=['TT_f' + sfx, 'cst'], writes=['TT_f' + sfx])
                yield
                S.op('pool', lambda: P.tensor_copy(out=TT_b[:], in_=TT_f[:]), reads=['TT_f' + sfx], writes=['TT_b' + sfx])
                if main:
                    yield
                    step16(P0, lambda c, h: [(lambda b, c=c, h=h: blk_tok(b, c, h), fm(bT, h, tt, c), fm(rT, h, tt, c), hb_(h))], ['bT', 'rT'])
                    yield
                    ev('dve', 'th', P0, lambda v, b: V.tensor_tensor(out=v(ArbT_s[:]), in0=v(b[:, :]), in1=v(m_incl), op=ALU.mult), ['cst'], ['ArbT_s' + sfx])
                    yield
                    step16(P1, lambda c, h: [(lambda b, c=c, h=h: blk_tok(b, c, h), fm(kT_, h, tt, c), fm(rT, h, tt, c), hb_(h))], ['kT_', 'rT'])
                    yield
                    ev('dve', 'th', P1, lambda v, b: V.tensor_tensor(out=v(ArkT_s[:]), in0=v(b[:, :]), in1=v(m_incl), op=ALU.mult), ['cst'], ['ArkT_s' + sfx])
                cur = 0
                for j in range(1, 6):
                    nxt = 1 - cur
                    Pc, Qc, Pn, Qn = Pm_[cur], Qm_[cur], Pm_[nxt], Qm_[nxt]
                    yield
                    step16(P0, lambda c, h: [(lambda b, c=c, h=h: blk_tok(b, c, h), blk_tok(Pc, c, h), blk_tok(Qc, c, h), c * 64)], ['Pm%d' % cur + sfx, 'Qm%d' % cur + sfx])
                    yield
                    ev('act', 'tc', P0, lambda v, b, Qn=Qn: A.copy(out=v(Qn[:]), in_=v(b[:, :])), [], ['Qm%d' % nxt + sfx])
                    if j < 5:
                        yield
                        step16(P1, lambda c, h: [(lambda b, c=c, h=h: blk_tok(b, c, h), blk_tok(Qc, c, h), blk_tok(Pc, c, h), c * 64)], ['Pm%d' % cur + sfx, 'Qm%d' % cur + sfx])
                        yield
                        ev('dve', 'tc', P1, lambda v, b, Pn=Pn: V.tensor_copy(out=v(Pn[:]), in_=v(b[:, :])), [], ['Pm%d' % nxt + sfx])
                    yield
                    step16(P2, lambda c, h: [(lambda b, c=c, h=h: blk_tok(b, c, h), blk_tok(Qn, c, h), blk_tok(TT_b, c, h), c * 64)], ['Qm%d' % nxt + sfx, 'TT_b' + sfx])
                    yield
                    ev('dve', 'tc', P2, lambda v, b: V.tensor_tensor(out=v(TT_f[:]), in0=v(b[:, :]), in1=v(TT_f[:]), op=ALU.add), ['TT_f' + sfx], ['TT_f' + sfx])
                    yield
                    S.op('pool', lambda: P.tensor_copy(out=TT_b[:], in_=TT_f[:]), reads=['TT_f' + sfx], writes=['TT_b' + sfx])
                    cur = nxt
                yield
                step16(P0, lambda c, h: [(lambda b, c=c, h=h: blk_tok(b, c, h), blk_tok(TT_b, c, h), tm(Atok, tt, c, h), c * 64)], ['TT_b' + sfx, 'Atok'])
                yield
                ev('act', 'tc', P0, lambda v, b: A.copy(out=v(Ah_s[:]), in_=v(b[:, :])), [], ['Ah_s' + sfx])
                yield
                step16(P1, lambda c, h: [(lambda b, c=c, h=h: blk_tok(b, c, h), blk_tok(AakT_s, c, h), tm(Vtok, tt, c, h), c * 64)], ['AakT_s' + sfx, 'Vtok'])
                yield
                ev('dve', 'tc', P1, lambda v, b: V.tensor_copy(out=v(W1_s[:]), in_=v(b[:, :])), [], ['W1_s' + sfx])
                if main:
                    yield
                    step16(P2, lambda c, h: [(lambda b, c=c, h=h: blk_feat(b, c, h), tm(Atok, tt, c, h), blk_tok(TT_b, c, h), c * 64)], ['TT_b' + sfx, 'Atok'])
                    yield
                    ev('act', 'fc', P2, lambda v, b: A.copy(out=v(AhT_s[:]), in_=v(b[:, :])), [], ['AhT_s' + sfx])
                yield
                step16(P0, lambda c, h: [(lambda b, c=c, h=h: blk_tok(b, c, h), blk_tok(TT_b, c, h), blk_tok(W1_s, c, h), c * 64)], ['TT_b' + sfx, 'W1_s' + sfx])
                yield
                ev('act', 'tc', P0, lambda v, b: A.copy(out=v(Uv_f[:]), in_=v(b[:, :])), [], ['Uv_f' + sfx])
                yield
                ev('dve', 'tc', P0, lambda v, b: V.tensor_copy(out=v(Uv_b[:]), in_=v(b[:, :])), [], ['Uv_b' + sfx])
                yield
                step16(P1, lambda c, h: [(lambda b, c=c, h=h: blk_feat(b, c, h), blk_tok(Ah_s, c, h), tm(Bhat, tt, c, h), c * 64)], ['Ah_s' + sfx, 'Bhat'])
                yield
                ev('act', 'fc', P1, lambda v, b: A.copy(out=v(GT_s[:]), in_=v(b[:, :])), [], ['GT_s' + sfx])
                yield
                step16(P2, lambda c, h: [(lambda b, c=c, h=h: blk_feat(b, c, h), tm(Bhat, tt, c, h), blk_tok(Uv_b, c, h), c * 64),
                                             (lambda b, c=c, h=h: blk_feat(b, c, h), tm(Khat, tt, c, h), tm(Vtok, tt, c, h), c * 64)], ['Bhat', 'Uv_b' + sfx, 'Khat', 'Vtok'])
                yield
                ev('dve', 'fc', P2, lambda v, b: V.tensor_copy(out=v(H_s[:]), in_=v(b[:, :])), [], ['H_s' + sfx])

            def load_state(seq):
                S.dma('sp', lambda: nc.sync.dma_start(out=st_ld, in_=st_in[seq].rearrange("h v k -> v h k")), writes=['sq_t'])
                for pr in range(4):
                    S.op('pe', lambda pr=pr: T.transpose(out=ps[1][:, pr * 64:(pr + 1) * 64], in_=st_ld[:, 2 * pr:2 * pr + 2, :].rearrange("v h k -> v (h k)"),
                         identity=ident_f[0:64, 0:64]), reads=['sq_t', 'cst'], writes=['ps1'])
                S.op('dve', lambda: V.tensor_copy(out=S_f[:].rearrange("p a v -> p (a v)"), in_=ps[1][:, 0:256]), reads=['ps1'], writes=['S_f'])
                S.op('act', lambda: A.copy(out=S_b[:].rearrange("p a v -> p (a v)"), in_=ps[1][:, 0:256]), reads=['ps1'], writes=['S_b'])

            def store_state(dst):
                for pr in range(4):
                    S.op('pe', lambda pr=pr: T.transpose(out=ps[1][0:64, pr * 128:(pr + 1) * 128], in_=S_f[:, pr, :], identity=ident_f),
                         reads=['S_f', 'cst'], writes=['ps1'])
                S.op('dve', lambda: V.tensor_copy(out=t_t[0:64, :], in_=ps[1][0:64, :]), reads=['ps1'], writes=['t_t'])
                S.dma('sp', lambda: nc.sync.dma_start(out=dst.rearrange("h v k -> v h k"), in_=st_o), reads=['t_t'], writes=['out_st'])

            def seq_pass(tt, main, seq_starts, seq_ids, seq_ends, state_dsts):
                X = CT[tt]
                ArbT_s, ArkT_s, AhT_s, Uv_f, GT_s, H_s = X['ArbT_s'], X['ArkT_s'], X['AhT_s'], X['Uv_f'], X['GT_s'], X['H_s']
                O_sb = OSB[tt]
                sfx = '_%d' % tt
                for c in range(2):
                    ci = tt * 2 + c
                    rows = slice(c * 64, (c + 1) * 64)
                    if seq_starts[ci]:
                        if seq_ids[ci] is None:
                            S.op('pool', lambda: P.memset(S_f[:], 0.0), writes=['S_f'])
                            S.op('pool', lambda: P.memset(S_b[:], 0.0), writes=['S_b'])
                        else:
                            load_state(seq_ids[ci])
                    if main:
                        for h in range(8):
                            hb = (h % 2) * 64
                            bk = 0 + (h % 2)
                            S.op('pe', lambda h=h, hb=hb, c=c, bk=bk: T.matmul(blk_tok(ps[bk], c, h), lhsT=blk_feat(AhT_s, c, h), rhs=S_b[hb:hb + 64, h // 2, :], start=True, stop=True),
                                 reads=['AhT_s' + sfx, 'S_b'], writes=['ps%d' % bk])
                        for par in range(2):
                            S.op('dve', lambda par=par, rows=rows: V.tensor_tensor(out=sv(U_b[rows, :], 'th', par), in0=sv(ps[par][rows, :], 'th', par), in1=sv(Uv_f[rows, :], 'th', par), op=ALU.add),
                                 reads=['ps%d' % par, 'Uv_f' + sfx], writes=['U_b'])
                        for h in range(8):
                            hb = (h % 2) * 64
                            bk = 2 + (h % 2)
                            S.op('pe', lambda h=h, hb=hb, c=c, bk=bk: T.matmul(blk_tok(ps[bk], c, h), lhsT=fm(rT, h, tt, c), rhs=S_b[hb:hb + 64, h // 2, :], start=True, stop=True),
                                 reads=['rT', 'S_b'], writes=['ps%d' % bk])
                        for h in range(8):
                            bk = 4 + c
                            S.op('pe', lambda h=h, c=c, bk=bk: T.matmul(blk_tok(ps[bk], c, h), lhsT=blk_tok(ArbT_s, c, h), rhs=blk_tok(U_b, c, h), start=True, stop=False),
                                 reads=['ArbT_s' + sfx, 'U_b'], writes=['ps%d' % bk])
                            S.op('pe', lambda h=h, c=c, bk=bk: T.matmul(blk_tok(ps[bk], c, h), lhsT=blk_tok(ArkT_s, c, h), rhs=tm(Vtok, tt, c, h), start=False, stop=True),
                                 reads=['ArkT_s' + sfx, 'Vtok'], writes=['ps%d' % bk])
                        S.op('act', lambda c=c, rows=rows: A.copy(out=O_sb[rows, :], in_=ps[4 + c][rows, :]), reads=['ps%d' % (4 + c)], writes=[OSBN[tt]])
                        for par in range(2):
                            S.op('dve', lambda par=par, rows=rows: V.tensor_tensor(out=sv(O_sb[rows, :], 'th', par), in0=sv(ps[2 + par][rows, :], 'th', par), in1=sv(O_sb[rows, :], 'th', par), op=ALU.add),
                                 reads=['ps%d' % (2 + par), OSBN[tt]], writes=[OSBN[tt]])
                    for h in range(8):
                        hb = (h % 2) * 64
                        bk = h % 2
                        S.op('pe', lambda h=h, hb=hb, c=c, bk=bk: T.matmul(ps[bk][hb:hb + 64, (h // 2) * 64:(h // 2) * 64 + 64], lhsT=blk_feat(GT_s, c, h), rhs=S_b[hb:hb + 64, h // 2, :],
                             start=True, stop=True), reads=['GT_s' + sfx, 'S_b'], writes=['ps%d' % bk])
                    gcb = gC[:, :, ci:ci + 1].broadcast_to([128, 4, 64])
                    H3 = H_s[:].rearrange("p (a c v) -> p a c v", a=4, c=2)[:, :, c, :]
                    S.op('pool', lambda gcb=gcb: P.tensor_tensor(out=S_t[:], in0=S_f[:], in1=gcb, op=ALU.mult), reads=['S_f', 'gC'], writes=['S_t'])
                    S.op('pool', lambda H3=H3: P.tensor_tensor(out=S_t[:], in0=S_t[:], in1=H3, op=ALU.add), reads=['S_t', 'H_s' + sfx], writes=['S_t'])
                    for par in range(2):
                        pr_ = slice(par * 64, (par + 1) * 64)
                        S.op('dve', lambda par=par, pr_=pr_: V.tensor_tensor(out=S_f[pr_].rearrange("p a v -> p (a v)"), in0=ps[par][pr_, 0:256], in1=S_t[pr_].rearrange("p a v -> p (a v)"), op=ALU.add),
                             reads=['ps%d' % par, 'S_t'], writes=['S_f'])
                    S.op('act', lambda: A.copy(out=S_b[:], in_=S_f[:]), reads=['S_f'], writes=['S_b'])
                    if seq_ends[ci]:
                        store_state(state_dsts[ci])

            def out_stage(tt, ya_dst):
                O_sb = OSB[tt]
                sfx = "_%d" % tt
                O3 = O_sb[:].rearrange("p (h v) -> p h v", h=8)
                S.op('dve', lambda: V.tensor_reduce(out=st8[:, 0, :], in_=O3, axis=AX.X, op=ALU.add), reads=[OSBN[tt]], writes=['st8'])
                S.op('act', lambda: A.activation(out=o_sq[:], in_=O_sb[:], func=AF.Square), reads=[OSBN[tt]], writes=['sq_t'])
                S.op('dve', lambda: V.tensor_reduce(out=st8[:, 1, :], in_=o_sq[:].rearrange("p (h v) -> p h v", h=8), axis=AX.X, op=ALU.add), reads=['sq_t'], writes=['st8'])
                S.op('dve', lambda: V.tensor_scalar(out=st8[:, 2, :], in0=st8[:, 0, :], scalar1=1.0 / 64, scalar2=None, op0=ALU.mult), reads=['st8'], writes=['st8'])
                S.op('dve', lambda: V.tensor_tensor(out=st8[:, 3, :], in0=st8[:, 2, :], in1=st8[:, 2, :], op=ALU.mult), reads=['st8'], writes=['st8'])
                S.op('dve', lambda: V.scalar_tensor_tensor(out=st8[:, 4, :], in0=st8[:, 1, :], scalar=1.0 / 64, in1=st8[:, 3, :], op0=ALU.mult, op1=ALU.subtract),
                     reads=['st8'], writes=['st8'])
                S.op('act', lambda: A.activation(out=st8[:, 5, :], in_=st8[:, 4, :], func=AF.Ln, bias=eps_g[:, 0:1]), reads=['st8', 'eps_g'], writes=['st8'])
                S.op('act', lambda: A.activation(out=st8[:, 5, :], in_=st8[:, 5, :], func=AF.Exp, scale=-0.5), reads=['st8'], writes=['st8'])
                on3 = o_n[:].rearrange("p (h v) -> p h v", h=8)
                S.op('dve', lambda: V.tensor_tensor(out=on3, in0=O3, in1=st8[:, 2, :].unsqueeze(2).broadcast_to([128, 8, 64]), op=ALU.subtract),
                     reads=[OSBN[tt], 'st8'], writes=['o_n'])
                S.op('dve', lambda: V.tensor_tensor(out=on3, in0=on3, in1=st8[:, 5, :].unsqueeze(2).broadcast_to([128, 8, 64]), op=ALU.mult), reads=['o_n', 'st8'], writes=['o_n'])
                S.op('pool', lambda: P.tensor_tensor(out=o_n[:], in0=o_n[:], in1=lnw_b[:], op=ALU.mult), reads=['o_n', 'lnw_b'], writes=['o_n'])
                S.op('pool', lambda: P.tensor_tensor(out=o_n[:], in0=o_n[:], in1=lnb_b[:], op=ALU.add), reads=['o_n', 'lnb_b'], writes=['o_n'])
                for j in range(4):
                    S.op('pe', lambda j=j: T.matmul(ps[0][:, j * 2:j * 2 + 2], lhsT=rk_b[:, j, tt * 128:(tt + 1) * 128], rhs=headsel_b, start=True, stop=True),
                         reads=['rk_b', 'cb'], writes=['ps0'])
                S.op('dve', lambda: V.tensor_copy(out=st8[:, 0, :], in_=ps[0][:, 0:8]), reads=['ps0'], writes=['st8'])
                S.op('dve', lambda: V.tensor_tensor(out=o_t[:].rearrange("p (h v) -> p h v", h=8), in0=Vtok[:, tt, :].rearrange("p (h v) -> p h v", h=8),
                     in1=st8[:, 0, :].unsqueeze(2).broadcast_to([128, 8, 64]), op=ALU.mult), reads=['Vtok', 'st8'], writes=['t_t'])
                S.op('pool', lambda: P.tensor_tensor(out=o_n[:], in0=o_n[:], in1=o_t[:], op=ALU.add), reads=['o_n', 't_t'], writes=['o_n'])
                S.op('pe', lambda: T.matmul(ps[1][:, :], lhsT=sgl_b[:, 0, tt * 128:(tt + 1) * 128], rhs=g2_t[:, 0, :], start=True, stop=False), reads=['sgl_b', 'g2_t'], writes=['ps1'])
                S.op('pe', lambda: T.matmul(ps[1][:, :], lhsT=sgl_b[:, 1, tt * 128:(tt + 1) * 128], rhs=g2_t[:, 1, :], start=False, stop=True), reads=['sgl_b', 'g2_t'], writes=['ps1'])
                S.op('dve', lambda: V.tensor_tensor(out=ya_b[:], in0=ps[1][:, :], in1=o_n[:], op=ALU.mult), reads=['ps1', 'o_n'], writes=['ya_b'])
                for j in range(4):
                    S.op('pe', lambda j=j: T.transpose(out=pb5[:, j * 128:(j + 1) * 128], in_=ya_b[:, j * 128:(j + 1) * 128], identity=ident_b), reads=['ya_b', 'cb'], writes=['ps5'])
                S.op('act', lambda: A.copy(out=yaT_st[:], in_=pb5[:, 0:512].rearrange("p (k t) -> p k t", k=4)), reads=['ps5'], writes=['yaT_st'])
                S.dma('sp', lambda: nc.sync.dma_start(out=ya_dst, in_=yaT_st[:]), reads=['yaT_st'], writes=['yaT_scr'])

            class Cfg:
                pass

            def X1(cfg):
                for tt in range(2):
                    load_norm_transpose(cfg.x_src[tt * 128:(tt + 1) * 128, :], xnT[:, :, tt * 128:(tt + 1) * 128], g1_b)
                rwkv_proj(cfg.main, cfg.seq_starts, cfg.shift_srcs, allcols=cfg.allcols)
                if cfg.post_x1 is not None:
                    cfg.post_x1()

            def X2pre(cfg):
                prep_pre(cfg.main)

            def X2pA(cfg):
                prep_pairs(cfg.main, (0, 2), 0, (3, 4, 5))

            def X2pB(cfg):
                prep_pairs(cfg.main, (1, 3), 1, (0, 1, 2))

            def X2T(cfg):
                prep_T()

            def X2b(cfg):
                for tt in range(2):
                    sb_part(tt, None, cfg.main, cfg.kouts[tt], cfg.vouts[tt], cfg.kT_dsts[tt], cfg.v_dsts[tt], cfg.qT_dsts[tt])

            def Y1(cfg, tt):
                return chunk_math(tt, cfg.main)

            def Y2(cfg):
                for tt in range(2):
                    seq_pass(tt, cfg.main, cfg.seq_starts, cfg.seq_ids, cfg.seq_ends, cfg.state_dsts)
                    if cfg.main:
                        out_stage(tt, cfg.ya_dsts[tt])

            def prompt_shift_out():
                S.op('dve', lambda: V.tensor_copy(out=shst[:], in_=carry[:]), reads=['carry'], writes=['shst'])
                for cc in range(15):
                    n = 128 if cc < 14 else 32
                    S.dma('sp', lambda cc=cc, n=n: nc.sync.dma_start(out=shp[cc * 128:cc * 128 + n].rearrange("(p o) -> p o", o=1), in_=shst[0:n, cc:cc + 1]),
                          reads=['shst'], writes=['out_sh'])

            def sample_shift_out():
                S.op('dve', lambda: V.tensor_copy(out=shst4[:], in_=plast[:]), reads=['plast'], writes=['shst4'])
                for q in range(4):
                    for cc in range(15):
                        n = 128 if cc < 14 else 32
                        S.dma('sp', lambda cc=cc, n=n, q=q: nc.sync.dma_start(out=shs[q, cc * 128:cc * 128 + n].rearrange("(p o) -> p o", o=1), in_=shst4[0:n, cc, q:q + 1]),
                              reads=['shst4'], writes=['out_sh'])

            cfgs = []
            nsup = (NPRE + NMAIN) // NT
            for si in range(nsup):
                cfg = Cfg()
                main = ((si // 2) % 2 == 1)
                t0 = si * NT
                m0 = (si // 4) * 512 + (si % 2) * NT
                cfg.main = main
                cfg.x_src = xp[t0:t0 + NT, :]
                cfg.seq_starts = [si == 0, False, False, False]
                cfg.shift_srcs = [None] * 4
                cfg.seq_ids = [None] * 4
                cfg.seq_ends = [False, False, False, si == nsup - 1]
                cfg.state_dsts = [None, None, None, wkvp]
                cfg.allcols = (not main and si % 2 == 1)
                cfg.post_x1 = prompt_shift_out if si == nsup - 1 else None
                cfg.kouts = [[(0, 128, kp[:, m0 + tt * 128:m0 + (tt + 1) * 128, :].rearrange("h t d -> t h d"))] if main else None for tt in range(2)]
                cfg.vouts = [[(0, 128, vp[:, m0 + tt * 128:m0 + (tt + 1) * 128, :].rearrange("h t d -> t h d"))] if main else None for tt in range(2)]
                cfg.kT_dsts = [kT_scr[:, :, t0 + tt * 128:t0 + (tt + 1) * 128].rearrange("k p t -> p k t") for tt in range(2)]
                cfg.v_dsts = [v_scr[t0 // 128 + tt] for tt in range(2)]
                cfg.qT_dsts = [qT_scr[:, :, m0 + tt * 128:m0 + (tt + 1) * 128].rearrange("k p t -> p k t") if main else None for tt in range(2)]
                cfg.ya_dsts = [yaT_scr[:, :, m0 + tt * 128:m0 + (tt + 1) * 128].rearrange("k p t -> p k t") if main else None for tt in range(2)]
                cfgs.append(cfg)
            cfg = Cfg()
            cfg.main = True
            cfg.x_src = xs[0:NT, :]
            cfg.seq_starts = [True] * 4
            cfg.shift_srcs = [sh_in[q] for q in range(4)]
            cfg.seq_ids = [0, 1, 2, 3]
            cfg.seq_ends = [True] * 4
            cfg.state_dsts = [wkvs[q] for q in range(4)]
            cfg.allcols = True
            cfg.post_x1 = sample_shift_out
            cfg.kouts = [[(c * 64, (c + 1) * 64, ksn[tt * 2 + c].rearrange("h t d -> t h d")) for c in range(2)] for tt in range(2)]
            cfg.vouts = [[(c * 64, (c + 1) * 64, vsn[tt * 2 + c].rearrange("h t d -> t h d")) for c in range(2)] for tt in range(2)]
            cfg.kT_dsts = [kTs_scr[:, :, tt * 128:(tt + 1) * 128].rearrange("k p t -> p k t") for tt in range(2)]
            cfg.v_dsts = [vs_scr[tt] for tt in range(2)]
            cfg.qT_dsts = [qTs_scr[:, :, tt * 128:(tt + 1) * 128].rearrange("k p t -> p k t") for tt in range(2)]
            cfg.ya_dsts = [yaT_scr[:, :, NMAIN + tt * 128:NMAIN + (tt + 1) * 128].rearrange("k p t -> p k t") for tt in range(2)]
            cfgs.append(cfg)

            S.emit([S.record(X1, cfgs[0])])
            for si, cfg in enumerate(cfgs):
                S.emit([S.record(X2pre, cfg)])
                S.emit([S.record(X2pA, cfg), S.record(X2pB, cfg), S.record(X2b, cfg)])
                S.emit([S.record(X2T, cfg)])
                S.emit([S.record(Y1, cfg, 0), S.record(Y1, cfg, 1)])
                if si >= 1:
                    for _ in range(2):
                        if deferred:
                            deferred.pop(0)()
                lists = [S.record(Y2, cfg)]
                if si + 1 < len(cfgs):
                    lists.append(S.record(X1, cfgs[si + 1]))
                S.emit(lists)
        while deferred:
            deferred.pop(0)()
        S.barrier()
        ph2 = contextlib.ExitStack()
        with ph2:
            def sb2(name, shape, dt=F32):
                return ph2.enter_context(nc.sbuf_tensor(name, list(shape), dt))
            kTp = [sb2("kTp%d" % i, [128, NK], BF16) for i in range(2)]
            qTp = [sb2("qTp%d" % i, [128, NMAIN], BF16) for i in range(2)]
            Vp = [sb2("Vp%d" % i, [128, NK // 128, 128], BF16) for i in range(2)]
            e1_t = [sb2("e1_%d" % i, [128, 2, 512], BF16) for i in range(3)]
            sp_t = [sb2("sp_%d" % i, [128, 2, 512], BF16) for i in range(4)]
            et_t = [sb2("et_%d" % i, [128, 2, 512], BF16) for i in range(2)]
            att_t = [sb2("att_%d" % i, [128, 2, 512], BF16) for i in range(2)]
            yb_st = sb2("yb_st", [128, 512], BF16)
            cmk = sb2("cmk", [128, NCONST - NC_A], BF16)
            S.dma('sp', lambda: nc.sync.dma_start(out=cmk[:], in_=consts[:, NC_A:NCONST]), writes=['cst'])
            cmask_b = [cmk[:, i * 512:(i + 1) * 512] for i in range(4)]
            cmask64_b = cmk[:, 2048:2112]

            def run_tiles(tiles, nq, out_bank):
                npair = len(tiles) // 2

                def stA(j):
                    t0_, t1_ = tiles[2 * j], tiles[2 * j + 1]
                    nk = t0_['nk']
                    for hh, t_ in ((0, t0_), (1, t1_)):
                        S.op('pe', lambda t_=t_, hh=hh: T.matmul(ps[hh][0:nk, 0:nq], lhsT=t_['kT'], rhs=t_['qT'], start=True, stop=True), reads=t_['rk'], writes=['ps%d' % hh])
                    e1 = e1_t[j % 3]
                    S.op('act', lambda: A.activation(out=e1[0:nk, :, 0:nq], in_=pall[0:nk, 0:2, 0:nq], func=AF.Exp), reads=['ps0', 'ps1'], writes=[('e1', j % 3)])
                    if t0_['mask'] is not None:
                        S.op('dve', lambda: V.tensor_tensor(out=e1[0:nk, :, 0:nq], in0=e1[0:nk, :, 0:nq], in1=t0_['mask'].unsqueeze(1).broadcast_to([nk, 2, nq]), op=ALU.mult),
                             reads=[('e1', j % 3), 'cst'], writes=[('e1', j % 3)])
                    S.op('act', lambda: A.activation(out=sp_t[j % 4][0:nk, :, 0:nq], in_=e1[0:nk, :, 0:nq], func=AF.Ln, bias=one_t[0:nk, 0:1]), reads=[('e1', j % 3), 'one_t'], writes=[('sp', j % 4)])

                def stB1(j):
                    t0_, t1_ = tiles[2 * j], tiles[2 * j + 1]
                    nk = t0_['nk']
                    for hh, t_ in ((0, t0_), (1, t1_)):
                        S.op('pe', lambda t_=t_, hh=hh: T.matmul(ps[2 + hh][0:128, 0:nq], lhsT=t_['tri'], rhs=sp_t[j % 4][0:nk, hh, 0:nq], start=t_['first'], stop=False),
                             reads=[('sp', j % 4), 'cst', 'cb'], writes=['ps%d' % (2 + hh)])
                    S.op('act', lambda: A.activation(out=et_t[j % 2][0:nk, :, 0:nq], in_=pall[0:nk, 2:4, 0:nq], func=AF.Exp, scale=-1.0), reads=['ps2', 'ps3'], writes=[('et', j % 2)])
                    S.op('dve', lambda: V.tensor_tensor(out=att_t[j % 2][0:nk, :, 0:nq], in0=e1_t[j % 3][0:nk, :, 0:nq], in1=et_t[j % 2][0:nk, :, 0:nq], op=ALU.mult),
                         reads=[('e1', j % 3), ('et', j % 2)], writes=[('att', j % 2)])

                def stUpp(j):
                    for hh in range(2):
                        t_ = tiles[2 * j + hh]
                        nk = t_['nk']
                        if not t_['last']:
                            S.op('pe', lambda t_=t_, hh=hh, nk=nk: T.matmul(ps[2 + hh][0:128, 0:nq], lhsT=t_['upp'], rhs=sp_t[j % 4][0:nk, hh, 0:nq], start=False, stop=True),
                                 reads=[('sp', j % 4), 'cst', 'cb'], writes=['ps%d' % (2 + hh)])

                def stAV(j):
                    for hh in range(2):
                        t_ = tiles[2 * j + hh]
                        nk = t_['nk']
                        S.op('pe', lambda t_=t_, hh=hh, nk=nk: T.matmul(ps[out_bank][hh * 64:(hh + 1) * 64, 0:nq], lhsT=t_['V'], rhs=att_t[j % 2][0:nk, hh, 0:nq], start=t_['first'], stop=t_['last']),
                             reads=[('att', j % 2)] + t_['rv'], writes=['ps%d' % out_bank])

                for j in range(npair + 3):
                    if 0 <= j - 2 < npair:
                        stB1(j - 2)
                    if j < npair:
                        stA(j)
                    if 0 <= j - 3 < npair:
                        stAV(j - 3)
                    if 0 <= j - 2 < npair:
                        stUpp(j - 2)

            for pr in range(4):
                bi = pr % 2
                S.dma('sp', lambda: nc.sync.dma_start(out=kTp[bi][:], in_=kT_scr[pr]), reads=['kq_scr'], writes=[('kTp', bi)])
                S.dma('sp', lambda: nc.sync.dma_start(out=qTp[bi][:], in_=qT_scr[pr]), reads=['kq_scr'], writes=[('qTp', bi)])
                for b0 in range(0, NK // 128, 16):
                    S.dma('sp', lambda b0=b0: nc.sync.dma_start(out=Vp[bi][:, b0:b0 + 16, :], in_=v_scr[b0:b0 + 16, :, pr * 128:(pr + 1) * 128].rearrange("b p c -> p b c")),
                          reads=['v_scr'], writes=[('Vp', bi)])
                for G in range(NMAIN // 512):
                    tiles = []
                    kbmax = 4 * (2 * G + 1) + 3
                    for kb in range(kbmax, -1, -1):
                        for hh in range(2):
                            hb = hh * 64
                            di = kb - 4 * (2 * G + 1)
                            pre = kb < 4
                            tiles.append(dict(hh=hh, nk=128, kT=kTp[bi][hb:hb + 64, kb * 128:(kb + 1) * 128], qT=qTp[bi][hb:hb + 64, G * 512:(G + 1) * 512],
                                              V=Vp[bi][:, kb, hb:hb + 64], mask=(cmask_b[di] if di >= 0 else None),
                                              tri=(triP_b if pre else tri_b), upp=(uppP_b if pre else upp_b), first=(kb == kbmax), last=(kb == 0),
                                              rk=[('kTp', bi), ('qTp', bi)], rv=[('Vp', bi)]))
                    run_tiles(tiles, 512, 4)
                    S.op('act', lambda: A.copy(out=yb_st[:], in_=ps[4][:, :]), reads=['ps4'], writes=['yb_st'])
                    S.dma('pool', lambda G=G: P.dma_start(out=ybT_scr[pr, :, G * 512:(G + 1) * 512], in_=yb_st[:]), reads=['yb_st'], writes=['ybT_scr'])
            ckf = sb2("ckf", [128, 512]); ckb = sb2("ckb", [128, 512], BF16)
            kTc = sb2("kTc", [128, 4, PAST + 64], BF16); Vc = sb2("Vc", [128, 9, 512], BF16); qTs = sb2("qTs", [128, 4, 64], BF16)
            for q in range(4):
                for blk in range(8):
                    S.dma('sp', lambda blk=blk: nc.sync.dma_start(out=ckf[:].rearrange("p (h d) -> p h d", h=8), in_=ck[q, :, blk * 128:(blk + 1) * 128, :].rearrange("h t d -> t h d")), writes=['ckf'])
                    S.op('dve', lambda: V.tensor_copy(out=ckb[:], in_=ckf[:]), reads=['ckf'], writes=['ckb'])
                    for j in range(4):
                        S.op('pe', lambda j=j: T.transpose(out=pb[1][:, j * 128:(j + 1) * 128], in_=ckb[:, j * 128:(j + 1) * 128], identity=ident_b), reads=['ckb', 'cst'], writes=['ps7'])
                    S.op('act', lambda blk=blk: A.copy(out=kTc[:, :, blk * 128:(blk + 1) * 128], in_=pb[1][:, 0:512].rearrange("p (k t) -> p k t", k=4)), reads=['ps7'], writes=['kTc'])
                    S.dma('sp', lambda blk=blk: nc.sync.dma_start(out=ckf[:].rearrange("p (h d) -> p h d", h=8), in_=cv[q, :, blk * 128:(blk + 1) * 128, :].rearrange("h t d -> t h d")), writes=['ckf'])
                    S.op('dve', lambda blk=blk: V.tensor_copy(out=Vc[:, blk, :], in_=ckf[:]), reads=['ckf'], writes=['Vc'])
                S.dma('sp', lambda: nc.sync.dma_start(out=kTc[:, :, PAST:PAST + 64], in_=kTs_scr[:, :, q * 64:(q + 1) * 64].rearrange("k p t -> p k t")), reads=['kq_scr'], writes=['kTc'])
                S.dma('sp', lambda: nc.sync.dma_start(out=qTs[:], in_=qTs_scr[:, :, q * 64:(q + 1) * 64].rearrange("k p t -> p k t")), reads=['kq_scr'], writes=['qTs'])
                S.dma('sp', lambda: nc.sync.dma_start(out=Vc[0:64, 8, :], in_=vs_scr[q // 2, (q % 2) * 64:(q % 2) * 64 + 64, :]), reads=['v_scr'], writes=['Vc'])
                for pr in range(4):
                    tiles = []
                    for kb in range(8, -1, -1):
                        for hh in range(2):
                            hb = hh * 64
                            nk = 64 if kb == 8 else 128
                            tiles.append(dict(hh=hh, nk=nk, kT=kTc[hb:hb + 64, pr, kb * 128:kb * 128 + nk], qT=qTs[hb:hb + 64, pr, :],
                                              V=Vc[0:nk, kb, pr * 128 + hb:pr * 128 + hb + 64], mask=(cmask64_b[0:64, :] if kb == 8 else None),
                                              tri=(tri_b[0:64, :] if kb == 8 else tri_b), upp=(upp_b[0:64, :] if kb == 8 else upp_b), first=(kb == 8), last=(kb == 0),
                                              rk=['kTc', 'qTs'], rv=['Vc']))
                    run_tiles(tiles, 64, 4)
                    S.op('act', lambda: A.copy(out=yb_st[:, 0:64], in_=ps[4][:, 0:64]), reads=['ps4'], writes=['yb_st'])
                    S.dma('sp', lambda pr=pr: nc.sync.dma_start(out=ybT_scr[pr, :, NMAIN + q * 64:NMAIN + (q + 1) * 64], in_=yb_st[:, 0:64]), reads=['yb_st'], writes=['ybT_scr'])
        S.barrier()
        ph3 = contextlib.ExitStack()
        with ph3:
            def sb3(name, shape, dt=F32):
                return ph3.enter_context(nc.sbuf_tensor(name, list(shape), dt))
            gn2_b = sb3("gn2_b", [128, D])
            S.dma('sp', lambda: nc.sync.dma_start(out=gn2_b[:], in_=gn2.partition_broadcast(128)), writes=['gn2_b'])
            wb = [sb3("wb%d" % i, [128, 16384], BF16) for i in range(2)]
            xres2 = [sb3("xres%d" % i, [128, 4, D]) for i in range(2)]; xnT3 = sb3("xnT3", [128, 8, 512], BF16)
            XR = {'t': xres2[0], 'n': ('xres', 0)}
            yaT3 = sb3("yaT3", [128, 4, 512], BF16); ybT3 = sb3("ybT3", [128, 4, 512], BF16)
            mT = sb3("mT", [128, 8, 512], BF16); h2T = sb3("h2T", [128, 32, 512], BF16)
            gT = h2T[:, 0:16, :]
            u1 = sb3("u1", [128, 512], BF16); u2 = sb3("u2", [128, 512], BF16); rl = sb3("rl", [128, 512], BF16)
            yst = [sb3("yst%d" % i, [128, 512]) for i in range(2)]
            wcount = [0]

            def wload(kind):
                i = wcount[0] % 2
                wcount[0] += 1
                buf = wb[i]
                if kind == 'gate':
                    v = buf[:, :].rearrange("p (k c) -> p k c", k=8)
                    for kc in range(8):
                        S.dma('sp', lambda kc=kc: nc.sync.dma_start(out=v[:, kc, :], in_=w_in_b[kc * 128:(kc + 1) * 128, C_GATE:DIN]), reads=[('w_in_b', C_GATE)], writes=[('wb', i)])
                elif kind == 'upo':
                    v = buf[:, :].rearrange("p (k c) -> p k c", k=16)
                    S.dma('sp', lambda: nc.sync.dma_start(out=v[:, 0:4, :], in_=wua_b.rearrange("(k p) c -> p k c", p=128)), reads=['wua_b'], writes=[('wb', i)])
                    S.dma('sp', lambda: nc.sync.dma_start(out=v[:, 4:8, :], in_=wub_b.rearrange("(k p) c -> p k c", p=128)), reads=['wub_b'], writes=[('wb', i)])
                    S.dma('sp', lambda: nc.sync.dma_start(out=v[:, 8:16, :], in_=wo_b.rearrange("(k p) c -> p k c", p=128)), reads=['wo_b'], writes=[('wb', i)])
                elif kind in ('f1a', 'f1b'):
                    c0 = 0 if kind == 'f1a' else 2048
                    v = buf[:, :].rearrange("p (k c) -> p k c", k=8)
                    for kc in range(8):
                        S.dma('sp', lambda kc=kc: nc.sync.dma_start(out=v[:, kc, :], in_=wf1_b[kc * 128:(kc + 1) * 128, c0:c0 + 2048]), reads=['wf1_b'], writes=[('wb', i)])
                else:
                    c0 = 0 if kind == 'f2a' else 512
                    v = buf[:, :].rearrange("p (k c) -> p k c", k=32)
                    for k0 in range(0, 32, 8):
                        S.dma('sp', lambda k0=k0: nc.sync.dma_start(out=v[:, k0:k0 + 8, :], in_=wf2_b[k0 * 128:(k0 + 8) * 128, c0:c0 + 512].rearrange("(k p) c -> p k c", p=128)),
                              reads=['wf2_b'], writes=[('wb', i)])
                return i, v

            xn4 = [xn_b] + [sb3("xn4_%d" % i, [128, D], BF16) for i in range(1, 4)]
            ssq4 = sb3("ssq4", [128, 3, 4])
            pbv = [ps[4 + i][:, :].bitcast(BF16) for i in range(4)]

            def norm_all(ntt, g_b):
                for tt in range(ntt):
                    S.op('act', lambda tt=tt: A.activation(out=xsq[:], in_=XR['t'][:, tt, :], func=AF.Square, accum_out=ssq4[:, 0, tt:tt + 1]), reads=[XR['n']], writes=['xsq', ('ssq4', tt)])
                S.op('act', lambda: A.activation(out=ssq4[:, 1, 0:ntt], in_=ssq4[:, 0, 0:ntt], func=AF.Ln, bias=eps_t[:, 0:1], scale=1.0 / D),
                     reads=[('ssq4', t_) for t_ in range(ntt)] + ['eps_t'], writes=['ssq4b'])
                S.op('act', lambda: A.activation(out=ssq4[:, 2, 0:ntt], in_=ssq4[:, 1, 0:ntt], func=AF.Exp, scale=-0.5), reads=['ssq4b'], writes=['ssq4c'])
                for tt in range(ntt):
                    S.op('dve', lambda tt=tt: V.scalar_tensor_tensor(out=xn4[tt][:], in0=XR['t'][:, tt, :], scalar=ssq4[:, 2, tt:tt + 1], in1=g_b[:], op0=ALU.mult, op1=ALU.mult),
                         reads=[XR['n'], 'ssq4c', 'g1_b', 'gn2_b'], writes=[('xn4', tt) if tt else 'xn_b'])
                for tt in range(ntt):
                    for kc in range(8):
                        S.op('pe', lambda kc=kc, tt=tt: T.transpose(out=pbv[tt][:, kc * 128:(kc + 1) * 128], in_=xn4[tt][:, kc * 128:(kc + 1) * 128], identity=ident_b),
                             reads=[('xn4', tt) if tt else 'xn_b', 'cst'], writes=['ps%d' % (4 + tt)])
                for tt in range(ntt):
                    eng = 'act' if tt % 2 == 0 else 'dve'
                    if eng == 'act':
                        S.op('act', lambda tt=tt: A.copy(out=xnT3[:, :, tt * 128:(tt + 1) * 128], in_=pbv[tt][:, :].rearrange("p (k t) -> p k t", k=8)), reads=['ps%d' % (4 + tt)], writes=['xnT3'])
                    else:
                        S.op('dve', lambda tt=tt: V.tensor_copy(out=xnT3[:, :, tt * 128:(tt + 1) * 128], in_=pbv[tt][:, :].rearrange("p (k t) -> p k t", k=8)), reads=['ps%d' % (4 + tt)], writes=['xnT3'])

            def load_x(idx, x_src, ntb):
                for tt in range(ntb // 128):
                    S.dma('sp', lambda tt=tt: nc.sync.dma_start(out=xres2[idx][:, tt, :], in_=x_src[tt * 128:(tt + 1) * 128, :]), writes=[('xres', idx)])

            GPRE = {'g': None}

            def phaseB(x_src, y_dst, ycol0, ntb, bidx, nxt):
                XR['t'] = xres2[bidx]
                XR['n'] = ('xres', bidx)
                ntt = ntb // 128
                wq = [GPRE['g'] if GPRE['g'] is not None else wload('gate'), wload('upo')]
                GPRE['g'] = None
                if bidx == 0 and ycol0 == 0:
                    load_x(0, x_src, ntb)
                S.dma('sp', lambda: nc.sync.dma_start(out=yaT3[:, :, 0:ntb], in_=yaT_scr[:, :, ycol0:ycol0 + ntb].rearrange("k p t -> p k t")), reads=['yaT_scr'], writes=['yaT3'])
                S.dma('sp', lambda: nc.sync.dma_start(out=ybT3[:, :, 0:ntb], in_=ybT_scr[:, :, ycol0:ycol0 + ntb].rearrange("k p t -> p k t")), reads=['ybT_scr'], writes=['ybT3'])
                norm_all(ntt, g1_b)
                wi, wv = wq[0]
                for gc in range(16):
                    bk = gc % 4
                    for kc in range(8):
                        S.op('pe', lambda kc=kc, gc=gc, bk=bk: T.matmul(ps[bk][:, 0:ntb], lhsT=wv[:, kc, gc * 128:(gc + 1) * 128], rhs=xnT3[:, kc, 0:ntb], start=(kc == 0), stop=(kc == 7)),
                             reads=[('wb', wi), 'xnT3'], writes=['ps%d' % bk])
                    S.op('act', lambda gc=gc, bk=bk: A.activation(out=gT[:, gc, 0:ntb], in_=ps[bk][:, 0:ntb], func=AF.Sigmoid), reads=['ps%d' % bk], writes=['h2T'])
                wq.append(wload('f1a'))
                wi, wv = wq[1]
                for oc in range(8):
                    ba, bb = (oc % 2) * 2, (oc % 2) * 2 + 1
                    for kc in range(4):
                        S.op('pe', lambda kc=kc, oc=oc, ba=ba: T.matmul(ps[ba][:, 0:ntb], lhsT=wv[:, kc, oc * 128:(oc + 1) * 128], rhs=yaT3[:, kc, 0:ntb], start=(kc == 0), stop=(kc == 3)),
                             reads=[('wb', wi), 'yaT3'], writes=['ps%d' % ba])
                    for kc in range(4):
                        S.op('pe', lambda kc=kc, oc=oc, bb=bb: T.matmul(ps[bb][:, 0:ntb], lhsT=wv[:, 4 + kc, oc * 128:(oc + 1) * 128], rhs=ybT3[:, kc, 0:ntb], start=(kc == 0), stop=(kc == 3)),
                             reads=[('wb', wi), 'ybT3'], writes=['ps%d' % bb])
                    S.op('dve', lambda oc=oc, ba=ba: V.tensor_tensor(out=u1[:, 0:ntb], in0=ps[ba][:, 0:ntb], in1=gT[:, oc, 0:ntb], op=ALU.mult), reads=['ps%d' % ba, 'h2T'], writes=['u1'])
                    S.op('dve', lambda oc=oc, bb=bb: V.tensor_tensor(out=u2[:, 0:ntb], in0=ps[bb][:, 0:ntb], in1=gT[:, 8 + oc, 0:ntb], op=ALU.mult), reads=['ps%d' % bb, 'h2T'], writes=['u2'])
                    S.op('pool', lambda oc=oc: P.tensor_tensor(out=mT[:, oc, 0:ntb], in0=u1[:, 0:ntb], in1=u2[:, 0:ntb], op=ALU.add), reads=['u1', 'u2'], writes=['mT'])
                for tt in range(ntt):
                    for hf in range(2):
                        bk = 4 + hf
                        for kc in range(8):
                            S.op('pe', lambda kc=kc, tt=tt, hf=hf, bk=bk: T.matmul(ps[bk][:, :], lhsT=mT[:, kc, tt * 128:(tt + 1) * 128], rhs=wv[:, 8 + kc, hf * 512:(hf + 1) * 512], start=(kc == 0), stop=(kc == 7)),
                                 reads=[('wb', wi), 'mT'], writes=['ps%d' % bk])
                        S.op('dve', lambda tt=tt, hf=hf, bk=bk: V.tensor_tensor(out=XR['t'][:, tt, hf * 512:(hf + 1) * 512], in0=ps[bk][:, :], in1=XR['t'][:, tt, hf * 512:(hf + 1) * 512], op=ALU.add),
                             reads=['ps%d' % bk, XR['n']], writes=[XR['n']])
                wq.append(wload('f1b'))
                norm_all(ntt, gn2_b)
                if nxt is not None:
                    load_x(1 - bidx, nxt[0], nxt[1])
                for part in range(2):
                    wi, wv = wq[2 + part]
                    if part == 1:
                        wq.append(wload('f2a'))
                    for fl in range(16):
                        fc = part * 16 + fl
                        bk = fc % 4
                        for kc in range(8):
                            S.op('pe', lambda kc=kc, fl=fl, bk=bk, wv=wv: T.matmul(ps[bk][:, 0:ntb], lhsT=wv[:, kc, fl * 128:(fl + 1) * 128], rhs=xnT3[:, kc, 0:ntb], start=(kc == 0), stop=(kc == 7)),
                                 reads=[('wb', wi), 'xnT3'], writes=['ps%d' % bk])
                        S.op('act', lambda bk=bk: A.activation(out=rl[:, 0:ntb], in_=ps[bk][:, 0:ntb], func=AF.Relu), reads=['ps%d' % bk], writes=['rl'])
                        S.op('dve', lambda fc=fc, bk=bk: V.tensor_tensor(out=h2T[:, fc, 0:ntb], in0=ps[bk][:, 0:ntb], in1=rl[:, 0:ntb], op=ALU.mult), reads=['ps%d' % bk, 'rl'], writes=['h2T'])
                wq.append(wload('f2b'))
                cnt = 0
                for hf in range(2):
                    wi, wv = wq[4 + hf]
                    for tt in range(ntt):
                        bk = 4 + (cnt % 2)
                        for fc in range(32):
                            S.op('pe', lambda fc=fc, tt=tt, bk=bk, wv=wv: T.matmul(ps[bk][:, :], lhsT=h2T[:, fc, tt * 128:(tt + 1) * 128], rhs=wv[:, fc, :], start=(fc == 0), stop=(fc == 31)),
                                 reads=[('wb', wi), 'h2T'], writes=['ps%d' % bk])
                        yb_ = yst[cnt % 2]
                        S.op('dve', lambda tt=tt, hf=hf, bk=bk, yb_=yb_: V.tensor_tensor(out=yb_[:], in0=ps[bk][:, :], in1=XR['t'][:, tt, hf * 512:(hf + 1) * 512], op=ALU.add),
                             reads=['ps%d' % bk, XR['n']], writes=[('yst', cnt % 2)])
                        S.dma('pool', lambda tt=tt, hf=hf, yb_=yb_: P.dma_start(out=y_dst[tt * 128:(tt + 1) * 128, hf * 512:(hf + 1) * 512], in_=yb_[:]), reads=[('yst', cnt % 2)], writes=['out_y'])
                        cnt += 1
                    if hf == 0 and nxt is not None:
                        GPRE['g'] = wload('gate')

            nG = NMAIN // 512
            for Gs in range(nG):
                nxt = (xp[(2 * Gs + 3) * 512:(2 * Gs + 4) * 512, :], 512) if Gs + 1 < nG else (xs, 256)
                phaseB(xp[(2 * Gs + 1) * 512:(2 * Gs + 2) * 512, :], yp[Gs * 512:(Gs + 1) * 512, :], Gs * 512, 512, Gs % 2, nxt)
            phaseB(xs, ys, NMAIN, 256, nG % 2, None)
        S.finish()
    return nc, S


_CACHE = {}


def kernel(**inp):
    f = lambda a: np.ascontiguousarray(np.asarray(a, dtype=np.float32))
    x_prompt = f(inp['x_prompt']); x_sample = f(inp['x_sample'])
    if 'nc' not in _CACHE:
        _CACHE['nc'] = build()
    nc, S = _CACHE['nc']
    consts = make_consts()
    shared = {
        'consts': consts.astype(ml_dtypes.bfloat16), 'constsf': np.ascontiguousarray(np.concatenate([consts[:, 0:128], consts[:, 128 + 2048 + 256:128 + 2048 + 256 + 128]], axis=1)), 'g1': f(inp['g_norm1'][0]), 'w_in': f(inp['w_in'][0]), 'mu': f(inp['rwkv_mu'][0]),
        'w0': f(inp['rwkv_w0'][0]), 'w2': f(inp['rwkv_w2'][0]), 'a0': f(inp['rwkv_a0'][0]), 'a2': f(inp['rwkv_a2'][0]),
        'g2': f(inp['rwkv_g2'][0]), 'k_k': f(inp['rwkv_k_k'][0]), 'k_a': f(inp['rwkv_k_a'][0]), 'r_k': f(inp['rwkv_r_k'][0]).reshape(512),
        'lnw': f(inp['rwkv_lnx_w'][0]), 'lnb': f(inp['rwkv_lnx_b'][0]), 'qg': f(inp['sb_q_norm_g'][0]), 'kg': f(inp['sb_k_norm_g'][0]),
        'wua': f(inp['w_up_a'][0]), 'wub': f(inp['w_up_b'][0]), 'wo': f(inp['w_out'][0]), 'gn2': f(inp['g_norm2'][0]),
        'wf1': f(inp['w_ff1'][0]), 'wf2': f(inp['w_ff2'][0]),
    }
    in_maps = []
    for c in range(8):
        b, g = c // 2, c % 2
        xpc = np.zeros((NPRE + NMAIN, D), np.float32)
        if g == 1:
            xpc[:] = x_prompt[b]
        else:
            xpc[512:] = x_prompt[b, :NPRE + NMAIN - 512]
        m = dict(shared)
        m['xp'] = xpc
        m['flag'] = np.full((1, 1), float(g), np.float32)
        sl = slice(c * NSEQ, (c + 1) * NSEQ)
        m['xs'] = f(x_sample[sl]).reshape(NSEQ * 64, D)
        m['ck'] = f(inp['cache_sb_k'][0, sl]); m['cv'] = f(inp['cache_sb_v'][0, sl])
        m['st'] = f(inp['state_rwkv_wkv'][0, sl]); m['sh'] = f(inp['state_rwkv_shift'][0, sl, 0])
        in_maps.append(m)
    res = run_bass_kernel_spmd(nc, in_maps, core_ids=list(range(8)))
    R = res.results
    y_p = np.zeros((4, 8192, D), np.float32); k_p = np.zeros((1, 4, 8, 8192, 64), np.float32); v_p = np.zeros_like(k_p)
    wkv_p = np.zeros((1, 4, 8, 64, 64), np.float32); sh_p = np.zeros((1, 4, 1, NRW), np.float32)
    y_s = np.zeros((32, 64, D), np.float32); k_s = np.zeros((1, 32, 8, 64, 64), np.float32); v_s = np.zeros_like(k_s)
    wkv_s = np.zeros((1, 32, 8, 64, 64), np.float32); sh_s = np.zeros((1, 32, 1, NRW), np.float32)
    for c in range(8):
        b, g = c // 2, c % 2
        r = R[c]
        for i in range(NMAIN // 512):
            ts = slice((2 * i + g) * 512, (2 * i + g + 1) * 512)
            ms = slice(i * 512, (i + 1) * 512)
            y_p[b, ts] = r['yp'][ms]; k_p[0, b, :, ts] = r['kp'][:, ms]; v_p[0, b, :, ts] = r['vp'][:, ms]
        if g == 1:
            wkv_p[0, b] = r['wkvp']; sh_p[0, b, 0] = r['shp']
        sl = slice(c * NSEQ, (c + 1) * NSEQ)
        y_s[sl] = r['ys'].reshape(NSEQ, 64, D); k_s[0, sl] = r['ksn']; v_s[0, sl] = r['vsn']
        wkv_s[0, sl] = r['wkvs']; sh_s[0, sl, 0] = r['shs']
    return (y_p, y_s, k_p, v_p, wkv_p, sh_p, k_s, v_s, wkv_s, sh_s)
```
